# Optimizing a Trainium2 kernel written in Bass

```python
import math
import jax, jax.numpy as jnp
from jax import lax
import numpy as np

D_MODEL = 4096
BATCH = 4
SEQ = 2048
DEPTH = 2

N_A_LAYERS = DEPTH // 2
N_B_LAYERS = DEPTH - N_A_LAYERS
ROPE_THETA = 10000.0
NORM_EPS = 1e-6
Q_BLOCK = 128
D_FF = -(-8 * D_MODEL // (3 * 256)) * 256

MLA_HEADS = D_MODEL // 128
MLA_Q_LORA = 1536
MLA_KV_LORA = 512
MLA_NOPE = 128
MLA_ROPE = 64
MLA_V = 128

NSA_HEADS = D_MODEL // 128
NSA_KV_HEADS = 4
NSA_HPG = NSA_HEADS // NSA_KV_HEADS
NSA_DK = 192
NSA_DV = 128
CMP_BLOCK = 32
CMP_STRIDE = 16
SEL_BLOCK = 64
SEL_TOPK = 16
WINDOW = 512
SEL_Q_CHUNK = 16
FORCE_BONUS = 1e4

kernel_name = "yoco_mla_nsa_hybrid"


def rms_norm(x, g):
    xf = x.astype(jnp.float32)
    y = xf * lax.rsqrt(jnp.mean(xf * xf, axis=-1, keepdims=True) + NORM_EPS)
    return (y * g.astype(jnp.float32)).astype(x.dtype)


def rope(x, pos):
    d = x.shape[-1]
    inv = ROPE_THETA ** (-jnp.arange(0, d, 2, dtype=jnp.float32) / d)
    ang = pos.astype(jnp.float32)[..., None] * inv
    ang = ang.reshape(ang.shape[:2] + (1,) * (x.ndim - 3) + (d // 2,))
    cos, sin = jnp.cos(ang), jnp.sin(ang)
    xf = x.astype(jnp.float32)
    x1, x2 = xf[..., : d // 2], xf[..., d // 2:]
    return jnp.concatenate([x1 * cos - x2 * sin, x2 * cos + x1 * sin], axis=-1).astype(x.dtype)


def masked_softmax(s, mask):
    s = jnp.where(mask, s.astype(jnp.float32), -jnp.inf)
    m = jnp.max(s, axis=-1, keepdims=True)
    m = jnp.where(jnp.isfinite(m), m, 0.0)
    e = jnp.exp(s - m)
    return e / jnp.maximum(jnp.sum(e, axis=-1, keepdims=True), 1e-30)


def swiglu(h, w_in, w_out):
    u = h @ w_in
    a, b = u[..., :D_FF], u[..., D_FF:]
    return (jax.nn.silu(a) * b) @ w_out


def causal_block_attention(q, k, v, scale):
    B, S, H, Dk = q.shape
    nb = S // Q_BLOCK
    qb = q.reshape(B, nb, Q_BLOCK, H, Dk).swapaxes(0, 1)
    kpos = jnp.arange(S)

    def step(args):
        qi, start = args
        s = jnp.einsum('bqhd,bkhd->bhqk', qi, k).astype(jnp.float32) * scale
        qpos = start + jnp.arange(Q_BLOCK)
        mask = kpos[None, :] <= qpos[:, None]
        p = masked_softmax(s, mask[None, None])
        return jnp.einsum('bhqk,bkhd->bqhd', p.astype(v.dtype), v)

    out = lax.map(step, (qb, jnp.arange(nb) * Q_BLOCK))
    return out.swapaxes(0, 1).reshape(B, S, H, v.shape[-1])


def mla_mixer(h, pos, w_in, g_q, w_uq, g_kv, w_ukv, w_o):
    B, S, _ = h.shape
    u = h @ w_in
    c_q = rms_norm(u[..., :MLA_Q_LORA], g_q)
    c_kv = rms_norm(u[..., MLA_Q_LORA:MLA_Q_LORA + MLA_KV_LORA], g_kv)
    k_rope = rope(u[..., MLA_Q_LORA + MLA_KV_LORA:][:, :, None, :], pos)
    q = (c_q @ w_uq).reshape(B, S, MLA_HEADS, MLA_NOPE + MLA_ROPE)
    q = jnp.concatenate([q[..., :MLA_NOPE], rope(q[..., MLA_NOPE:], pos)], axis=-1)
    kv = (c_kv @ w_ukv).reshape(B, S, MLA_HEADS, MLA_NOPE + MLA_V)
    k = jnp.concatenate([kv[..., :MLA_NOPE],
                         jnp.broadcast_to(k_rope, (B, S, MLA_HEADS, MLA_ROPE))], axis=-1)
    v = kv[..., MLA_NOPE:]
    o = causal_block_attention(q, k, v, (MLA_NOPE + MLA_ROPE) ** -0.5)
    return o.reshape(B, S, MLA_HEADS * MLA_V) @ w_o


def compress_blocks(t, pe, w1, w2):
    B, S, G, d = t.shape
    n_cmp = (S - CMP_BLOCK) // CMP_STRIDE + 1
    idx = (jnp.arange(n_cmp) * CMP_STRIDE)[:, None] + jnp.arange(CMP_BLOCK)[None, :]
    blocks = t[:, idx] + pe[None, None, :, None, :]
    flat = blocks.transpose(0, 1, 3, 2, 4).reshape(B, n_cmp, G, CMP_BLOCK * d)
    return jax.nn.gelu(flat @ w1) @ w2


def nsa_shared_kv(h, pos, g_s, w_kv, pe_k, w1_k, w2_k, pe_v, w1_v, w2_v):
    B, S, _ = h.shape
    u = (rms_norm(h, g_s) @ w_kv).reshape(B, S, 3, NSA_KV_HEADS, NSA_DK + NSA_DV)
    k_c, v_c = rope(u[:, :, 0, :, :NSA_DK], pos), u[:, :, 0, :, NSA_DK:]
    k_s, v_s = rope(u[:, :, 1, :, :NSA_DK], pos), u[:, :, 1, :, NSA_DK:]
    k_w, v_w = rope(u[:, :, 2, :, :NSA_DK], pos), u[:, :, 2, :, NSA_DK:]
    kc = compress_blocks(k_c, pe_k, w1_k, w2_k)
    vc = compress_blocks(v_c, pe_v, w1_v, w2_v)
    n_sel = S // SEL_BLOCK
    ks = k_s.reshape(B, n_sel, SEL_BLOCK, NSA_KV_HEADS, NSA_DK).transpose(0, 3, 1, 2, 4)
    vs = v_s.reshape(B, n_sel, SEL_BLOCK, NSA_KV_HEADS, NSA_DV).transpose(0, 3, 1, 2, 4)
    pad = ((0, 0), (WINDOW, 0), (0, 0), (0, 0))
    kw = jnp.pad(k_w, pad)
    vw = jnp.pad(v_w, pad)
    return kc, vc, ks, vs, kw, vw


def nsa_selected(q, blk_idx, blk_valid, ks, vs, scale):
    B, S, G, HPG, DK = q.shape
    nk = blk_idx.shape[-1]
    n_chunk = S // SEL_Q_CHUNK

    def chunked(a):
        return a.reshape((B, n_chunk, SEL_Q_CHUNK) + a.shape[2:]).swapaxes(0, 1)

    b_ix = jnp.arange(B)[:, None, None, None]
    g_ix = jnp.arange(G)[None, None, :, None]

    def step(args):
        qi, idx, val, start = args
        kb = ks[b_ix, g_ix, idx]
        vb = vs[b_ix, g_ix, idx]
        s = jnp.einsum('bqghd,bqgkld->bqghkl', qi, kb).astype(jnp.float32) * scale
        tok = idx[..., None] * SEL_BLOCK + jnp.arange(SEL_BLOCK)
        qpos = start + jnp.arange(SEL_Q_CHUNK)
        mask = val[..., None] & (tok <= qpos[None, :, None, None, None])
        mask = mask.reshape(B, SEL_Q_CHUNK, G, 1, nk * SEL_BLOCK)
        p = masked_softmax(s.reshape(B, SEL_Q_CHUNK, G, HPG, nk * SEL_BLOCK), mask)
        p = p.reshape(B, SEL_Q_CHUNK, G, HPG, nk, SEL_BLOCK)
        return jnp.einsum('bqghkl,bqgklv->bqghv', p.astype(vb.dtype), vb)

    out = lax.map(step, (chunked(q), chunked(blk_idx), chunked(blk_valid),
                         jnp.arange(n_chunk) * SEL_Q_CHUNK))
    return out.swapaxes(0, 1).reshape(B, S, G, HPG, vs.shape[-1])


def nsa_window(q, kw, vw, scale):
    B, S, G, HPG, DK = q.shape
    nb = S // Q_BLOCK
    span = WINDOW + Q_BLOCK
    qb = q.reshape(B, nb, Q_BLOCK, G, HPG, DK).swapaxes(0, 1)

    def step(args):
        qi, start = args
        kb = lax.dynamic_slice_in_dim(kw, start, span, axis=1)
        vb = lax.dynamic_slice_in_dim(vw, start, span, axis=1)
        s = jnp.einsum('bqghd,bkgd->bqghk', qi, kb).astype(jnp.float32) * scale
        kpos = start - WINDOW + jnp.arange(span)
        qpos = start + jnp.arange(Q_BLOCK)
        mask = ((kpos[None, :] <= qpos[:, None]) & (qpos[:, None] - kpos[None, :] < WINDOW)
                & (kpos[None, :] >= 0))
        p = masked_softmax(s, mask[None, :, None, None, :])
        return jnp.einsum('bqghk,bkgv->bqghv', p.astype(vb.dtype), vb)

    out = lax.map(step, (qb, jnp.arange(nb) * Q_BLOCK))
    return out.swapaxes(0, 1).reshape(B, S, G, HPG, vw.shape[-1])


def nsa_mixer(h, pos, shared, w_in, w_o):
    kc, vc, ks, vs, kw, vw = shared
    B, S, _ = h.shape
    G, HPG = NSA_KV_HEADS, NSA_HPG
    scale = NSA_DK ** -0.5
    u = h @ w_in
    q = rope(u[..., :NSA_HEADS * NSA_DK].reshape(B, S, G, HPG, NSA_DK), pos)
    gates = jax.nn.sigmoid(u[..., NSA_HEADS * NSA_DK:].astype(jnp.float32)).reshape(B, S, G, HPG, 3)
    t = jnp.arange(S)

    n_cmp = kc.shape[1]
    c_start = jnp.arange(n_cmp) * CMP_STRIDE
    mask_c = (c_start[None, :] + CMP_BLOCK - 1) <= t[:, None]
    s_c = jnp.einsum('bsghd,bcgd->bsghc', q, kc).astype(jnp.float32) * scale
    p_c = masked_softmax(s_c, mask_c[None, :, None, None, :])
    o_c = jnp.einsum('bsghc,bcgv->bsghv', p_c.astype(vc.dtype), vc)

    n_sel = S // SEL_BLOCK
    j_start = jnp.arange(n_sel) * SEL_BLOCK
    overlap = ((c_start[:, None] < j_start[None, :] + SEL_BLOCK)
               & (c_start[:, None] + CMP_BLOCK > j_start[None, :])).astype(jnp.float32)
    imp = jnp.einsum('bsghc,cj->bsgj', p_c, overlap)
    j = jnp.arange(n_sel)
    cur = (t // SEL_BLOCK)[:, None]
    valid = j_start[None, :] <= t[:, None]
    forced = valid & ((j[None, :] == 0) | (j[None, :] == cur) | (j[None, :] == cur - 1))
    score = jnp.where(valid[None, :, None, :],
                      imp + jnp.where(forced, FORCE_BONUS, 0.0)[None, :, None, :], -jnp.inf)
    top_vals, blk_idx = lax.top_k(score, min(SEL_TOPK, n_sel))
    blk_valid = jnp.isfinite(top_vals)
    o_s = nsa_selected(q, blk_idx, blk_valid, ks, vs, scale)

    o_w = nsa_window(q, kw, vw, scale)

    o = (gates[..., 0:1] * o_c.astype(jnp.float32) + gates[..., 1:2] * o_s.astype(jnp.float32)
         + gates[..., 2:3] * o_w.astype(jnp.float32)).astype(h.dtype)
    return o.reshape(B, S, NSA_HEADS * NSA_DV) @ w_o


def setup_inputs(seed: int = 0) -> dict:
    key = jax.random.key(seed)
    ks = jax.random.split(key, 32)

    def w(k, shape, fan_in):
        return jax.random.normal(k, shape, jnp.float32) * (fan_in ** -0.5)

    def gain(k, shape):
        return 1.0 + 0.02 * jax.random.normal(k, shape, jnp.float32)

    na, nb = N_A_LAYERS, N_B_LAYERS
    a_in_w = MLA_Q_LORA + MLA_KV_LORA + MLA_ROPE
    x = jax.random.normal(ks[0], (BATCH, SEQ, D_MODEL), jnp.float32)
    positions = (jnp.arange(SEQ, dtype=jnp.int32)[None, :]
                 + jax.random.randint(ks[1], (BATCH, 1), 0, SEQ, dtype=jnp.int32))
    return {
        "x": x,
        "positions": positions,
        "a_norm": gain(ks[2], (na, D_MODEL)),
        "a_w_in": w(ks[3], (na, D_MODEL, a_in_w), D_MODEL),
        "a_q_norm": gain(ks[4], (na, MLA_Q_LORA)),
        "a_w_uq": w(ks[5], (na, MLA_Q_LORA, MLA_HEADS * (MLA_NOPE + MLA_ROPE)), MLA_Q_LORA),
        "a_kv_norm": gain(ks[6], (na, MLA_KV_LORA)),
        "a_w_ukv": w(ks[7], (na, MLA_KV_LORA, MLA_HEADS * (MLA_NOPE + MLA_V)), MLA_KV_LORA),
        "a_w_o": w(ks[8], (na, MLA_HEADS * MLA_V, D_MODEL), MLA_HEADS * MLA_V),
        "s_norm": gain(ks[9], (D_MODEL,)),
        "s_w_kv": w(ks[10], (D_MODEL, 3 * NSA_KV_HEADS * (NSA_DK + NSA_DV)), D_MODEL),
        "s_cmp_pe_k": 0.02 * jax.random.normal(ks[11], (CMP_BLOCK, NSA_DK), jnp.float32),
        "s_cmp_w1_k": w(ks[12], (CMP_BLOCK * NSA_DK, NSA_DK), CMP_BLOCK * NSA_DK),
        "s_cmp_w2_k": w(ks[13], (NSA_DK, NSA_DK), NSA_DK),
        "s_cmp_pe_v": 0.02 * jax.random.normal(ks[14], (CMP_BLOCK, NSA_DV), jnp.float32),
        "s_cmp_w1_v": w(ks[15], (CMP_BLOCK * NSA_DV, NSA_DV), CMP_BLOCK * NSA_DV),
        "s_cmp_w2_v": w(ks[16], (NSA_DV, NSA_DV), NSA_DV),
        "b_norm": gain(ks[17], (nb, D_MODEL)),
        "b_w_in": w(ks[18], (nb, D_MODEL, NSA_HEADS * NSA_DK + 3 * NSA_HEADS), D_MODEL),
        "b_w_o": w(ks[19], (nb, NSA_HEADS * NSA_DV, D_MODEL), NSA_HEADS * NSA_DV),
        "f_norm": gain(ks[20], (DEPTH, D_MODEL)),
        "f_w_in": w(ks[21], (DEPTH, D_MODEL, 2 * D_FF), D_MODEL),
        "f_w_out": w(ks[22], (DEPTH, D_FF, D_MODEL), D_FF),
        "final_norm": gain(ks[23], (D_MODEL,)),
    }


def reference(x, positions, a_norm, a_w_in, a_q_norm, a_w_uq, a_kv_norm, a_w_ukv, a_w_o,
              s_norm, s_w_kv, s_cmp_pe_k, s_cmp_w1_k, s_cmp_w2_k, s_cmp_pe_v, s_cmp_w1_v,
              s_cmp_w2_v, b_norm, b_w_in, b_w_o, f_norm, f_w_in, f_w_out, final_norm):
    h = x
    shared = None
    for layer in range(DEPTH):
        if layer < N_A_LAYERS:
            i = layer
            h = h + mla_mixer(rms_norm(h, a_norm[i]), positions, a_w_in[i], a_q_norm[i],
                              a_w_uq[i], a_kv_norm[i], a_w_ukv[i], a_w_o[i])
        else:
            if layer == N_A_LAYERS:
                shared = nsa_shared_kv(h, positions, s_norm, s_w_kv, s_cmp_pe_k, s_cmp_w1_k,
                                       s_cmp_w2_k, s_cmp_pe_v, s_cmp_w1_v, s_cmp_w2_v)
            j = layer - N_A_LAYERS
            h = h + nsa_mixer(rms_norm(h, b_norm[j]), positions, shared, b_w_in[j], b_w_o[j])
        h = h + swiglu(rms_norm(h, f_norm[layer]), f_w_in[layer], f_w_out[layer])
    return rms_norm(h, final_norm)
```

```python
import numpy as np
import concourse.bass as bass
import concourse.mybir as mybir
from concourse.bass_utils import run_bass_kernel_spmd

F32 = mybir.dt.float32
BF16 = mybir.dt.bfloat16
I32 = mybir.dt.int32
AF = mybir.ActivationFunctionType
ALU = mybir.AluOpType
AX = mybir.AxisListType

SAME_ENG_SYNC = False
SEM_LIMIT = 20000


class Res:
    __slots__ = ("name", "ws", "r", "dsem", "dcount", "const", "acc", "last_dma")

    def __init__(self, name, const=False, acc=False):
        self.name = name
        self.ws = []
        self.acc = acc
        self.r = []
        self.dsem = None
        self.dcount = 0
        self.const = const
        self.last_dma = None


class Op:
    __slots__ = ("eng", "fn", "deps", "dma", "sem", "val", "signal", "ndma", "name", "dma_inc")


class Sched:
    ENGS = ("sync", "act", "dve", "pool", "pe")

    def __init__(self, nc):
        self.nc = nc
        self.q = {e: [] for e in self.ENGS}
        self.nsem = 0
        self.allres = []
        self.sem_pool = []
        self.live_dsem = []

    def res(self, name, const=False, acc=False):
        r = Res(name, const, acc)
        self.allres.append(r)
        return r

    def _newsem(self, name):
        self.nsem += 1
        return self.nc.alloc_semaphore(f"s{self.nsem}_{name}")

    def op(self, eng, fn, reads=(), writes=(), dma=False, dsem=None, ndma=1, name="", sync_same=(), dma_inc=16):
        op = Op()
        op.eng = eng
        op.fn = fn
        op.dma = dma
        op.signal = False
        op.sem = None
        op.val = 0
        op.ndma = ndma
        op.name = name
        op.dma_inc = dma_inc
        deps = []
        seen = set()

        def add(d, force=False):
            if d is None or id(d) in seen:
                return
            seen.add(id(d))
            if (not d.dma) and d.eng == eng and not SAME_ENG_SYNC and not force:
                return
            deps.append(d)

        for r in sync_same:
            for ww in r.ws:
                add(ww, True)
        for r in reads:
            for ww in r.ws:
                add(ww)
        for w in writes:
            if not w.acc:
                for ww in w.ws:
                    add(ww)
            for rr in w.r:
                add(rr)
        op.deps = deps
        for w in writes:
            if w.acc:
                w.ws.append(op)
            else:
                w.ws = [op]
            w.r = []
        for r in reads:
            if not r.const:
                r.r.append(op)
        if dma:
            if dsem is None:
                raise ValueError("dma needs dsem")
            if dsem.dsem is None:
                if self.sem_pool:
                    dsem.dsem, dsem.dcount = self.sem_pool.pop()
                else:
                    dsem.dsem = self._newsem("d_" + dsem.name)
                self.live_dsem.append(dsem)
            dsem.last_dma = None
            dsem.dcount += dma_inc * ndma
            op.sem = dsem.dsem
            op.val = dsem.dcount
            op.signal = True
            dsem.last_dma = op
        self.q[eng].append(op)
        return op

    def barrier(self):
        lasts = []
        for e in self.ENGS:
            for op in reversed(self.q[e]):
                if op.fn is not None and not op.dma:
                    lasts.append(op)
                    break
        dmal = [r.last_dma for r in self.live_dsem if r.last_dma is not None]
        for e in self.ENGS:
            op = Op()
            op.eng = e
            op.fn = None
            op.dma = False
            op.signal = False
            op.sem = None
            op.val = 0
            op.ndma = 0
            op.name = "barrier"
            op.dma_inc = 16
            op.deps = [d for d in lasts if d.eng != e] + dmal
            self.q[e].append(op)
        for r in self.live_dsem:
            self.sem_pool.append((r.dsem, r.dcount))
            r.dsem = None
            r.last_dma = None
        self.live_dsem = []
        for r in self.allres:
            r.ws = []
            r.r = []

    def emit(self):
        nc = self.nc
        for e in self.ENGS:
            for op in self.q[e]:
                for d in op.deps:
                    d.signal = True
        for e in self.ENGS:
            cur = None
            cnt = 0
            for op in self.q[e]:
                if op.dma or not op.signal or op.fn is None:
                    continue
                if cur is None or cnt >= SEM_LIMIT:
                    cur = self._newsem("e_" + e)
                    cnt = 0
                cnt += 1
                op.sem = cur
                op.val = cnt

        def run(e):
            def body(eng):
                waited = {}
                for op in self.q[e]:
                    for d in op.deps:
                        if d.sem is None:
                            continue
                        k = id(d.sem)
                        if waited.get(k, 0) < d.val:
                            eng.wait_ge(d.sem, d.val)
                            waited[k] = d.val
                    if op.fn is None:
                        continue
                    ins = op.fn(eng)
                    if ins is None:
                        continue
                    if not isinstance(ins, (list, tuple)):
                        ins = [ins]
                    if op.dma:
                        assert len(ins) == op.ndma, (op.name, len(ins), op.ndma)
                        for i in ins:
                            i.then_inc(op.sem, op.dma_inc)
                    elif op.signal:
                        ins[-1].then_inc(op.sem, 1)
            return body

        with nc.Block() as block:
            block.sync(run("sync"))
            block.scalar(run("act"))
            block.vector(run("dve"))
            block.gpsimd(run("pool"))
            block.tensor(run("pe"))
        print("nsem", self.nsem, {e: len(self.q[e]) for e in self.ENGS}, flush=True)


def mkap(t_ap, offset_elems, pattern):
    return bass.AP(t_ap.tensor, offset_elems, pattern)


import math

EPS = 1e-6
TWO_PI = 2.0 * math.pi


DT_SIZE = {F32: 4, BF16: 2, I32: 4}
SB_BASE = 16384 + 512
SB_TOP = 229344


class Ctx:
    def __init__(self, nc, n_rr=8):
        self.nc = nc
        self.S = Sched(nc)
        self.ps = [nc.alloc_psum_tensor(f"ps{i}", [128, 512], F32) for i in range(8)]
        self.psr = [self.S.res(f"ps{i}") for i in range(8)]
        self.psi = 0
        self.n_rr = n_rr
        self.uid = 0
        self.cur = SB_BASE
        self.prefix = ""
        self.io = {}
        self.fused = False
        ones_f, r1 = self.tile("ones_f", [128, 128], F32)
        ones_b, r2 = self.tile("ones_b", [128, 128], BF16)
        eps_t, r3 = self.tile("eps_t", [128, 1], F32)
        self.ones_f, self.ones_b, self.eps_t = ones_f, ones_b, eps_t
        self.r_const = self.S.res("consts", const=False)
        S = self.S
        S.op("dve", lambda e: [e.memset(ones_f[:], 1.0), e.memset(ones_b[:], 1.0), e.memset(eps_t[:], EPS)],
             writes=[self.r_const])
        self.phase_base = self.cur

    def begin_phase(self, prefix, n_rr, io):
        self.prefix = prefix
        self.n_rr = n_rr
        self.psi = 0
        self.io = io
        self.cur = self.phase_base
        for a in ("rn_tiles", "rt_extra"):
            if hasattr(self, a):
                delattr(self, a)

    def end_phase(self):
        self.S.barrier()

    def psum(self):
        i = self.psi
        self.psi = (i + 1) % self.n_rr
        return self.ps[i], self.psr[i]

    def tile(self, name, shape, dt):
        nb = DT_SIZE[dt]
        for d in shape[1:]:
            nb *= d
        nb = (nb + 63) // 64 * 64
        t = self.nc.alloc_sbuf_tensor_at(self.prefix + name, list(shape), dt, offset=self.cur)
        self.cur += nb
        assert self.cur <= SB_TOP, ("SBUF overflow", self.prefix + name, self.cur)
        return t, self.S.res(self.prefix + name)

    def dram_in(self, name, shape, dt):
        if name in self.io:
            return self.io[name]
        return (self.nc.dram_tensor(self.prefix + name, list(shape), dt, kind="ExternalInput").ap(),
                self.S.res(self.prefix + name, const=True))

    def dram_out(self, name, shape, dt):
        if name in self.io:
            return self.io[name]
        return (self.nc.dram_tensor(self.prefix + name, list(shape), dt, kind="ExternalOutput").ap(),
                self.S.res(self.prefix + name, acc=True))

    def dram_tmp(self, name, shape, dt):
        return self.nc.dram_tensor(name, list(shape), dt, kind="Internal").ap(), self.S.res(name, acc=True)


WT_ELEMS = 16384


def wview(wt, ncols, kc0, kc1, c0, c1):
    rs = wt[:].ap[0][0]
    return bass.AP(wt, kc0 * ncols + c0, [[rs, 128], [ncols, kc1 - kc0], [1, c1 - c0]])


def wslice(wt, ncols, kc, c0, c1, p0=0, p1=128):
    rs = wt[:].ap[0][0]
    return bass.AP(wt, p0 * rs + kc * ncols + c0, [[rs, p1 - p0], [1, c1 - c0]])


def wblock_load(C, wt, wres, W, KC, c0, ncols, kstep=8):
    Wv = W[:, c0:c0 + ncols].rearrange("(kc p) n -> p kc n", p=128)
    pieces = [(k0, min(KC, k0 + kstep)) for k0 in range(0, KC, kstep)]

    def fn(e):
        return [e.dma_start(out=wview(wt, ncols, k0, k1, 0, ncols), in_=Wv[:, k0:k1, :]) for (k0, k1) in pieces]
    C.S.op("pool", fn, writes=[wres], dma=True, dsem=wres, ndma=len(pieces), name="wload")


def linear_T(C, W, KC, rhs_fn, chunks, halves, epi, wtiles, colblock=None, wstate=None):
    S = C.S
    if wstate is None:
        wstate = [0]
    if colblock is None:
        colblock = min(512, (wtiles[0][0][:].ap[0][0] // KC) // 128 * 128)
    blocks = []
    cur = None
    for ci, (c0, M) in enumerate(chunks):
        if cur is not None and cur["c1"] == c0 and (c0 + M - cur["c0"]) <= colblock:
            cur["items"].append((ci, c0, M))
            cur["c1"] = c0 + M
        else:
            cur = {"c0": c0, "c1": c0 + M, "items": [(ci, c0, M)]}
            blocks.append(cur)
    for b in blocks:
        wt, wres = wtiles[wstate[0] % len(wtiles)]
        wstate[0] += 1
        ncols = b["c1"] - b["c0"]
        wblock_load(C, wt, wres, W, KC, b["c0"], ncols)
        for (ci, c0, M) in b["items"]:
            off = c0 - b["c0"]
            for hf in halves:
                ps, psr = C.psum()
                rhs_list = [rhs_fn(kc, hf) for kc in range(KC)]
                rres = []
                for (_, r) in rhs_list:
                    if r not in rres:
                        rres.append(r)

                def mm(e, ps=ps, wt=wt, off=off, M=M, rhs_list=rhs_list, ncols=ncols):
                    return [e.matmul(ps[0:M, :], wslice(wt, ncols, kc, off, off + M), rhs_list[kc][0],
                                     start=(kc == 0), stop=(kc == KC - 1)) for kc in range(KC)]
                S.op("pe", mm, reads=[wres] + rres, writes=[psr], name="lin_mm")
                epi(ci, hf, ps, psr)


def linear_T_pairs(C, W, KC, rhs_fn, chunks, halves, epi, wtiles, wstate, group=2, colblock=None):
    S = C.S
    if colblock is None:
        colblock = min(512, (wtiles[0][0][:].ap[0][0] // KC) // 128 * 128)
    groups = [list(range(i, min(i + group, len(chunks)))) for i in range(0, len(chunks), group)]
    blocks = []
    cur = None
    for grp in groups:
        c0 = chunks[grp[0]][0]
        c1 = chunks[grp[-1]][0] + chunks[grp[-1]][1]
        contiguous = all(chunks[grp[i]][0] + chunks[grp[i]][1] == chunks[grp[i + 1]][0] for i in range(len(grp) - 1))
        assert contiguous
        if cur is not None and cur["c1"] == c0 and (c1 - cur["c0"]) <= colblock:
            cur["groups"].append(grp)
            cur["c1"] = c1
        else:
            cur = {"c0": c0, "c1": c1, "groups": [grp]}
            blocks.append(cur)
    for b in blocks:
        wt, wres = wtiles[wstate[0] % len(wtiles)]
        wstate[0] += 1
        ncols = b["c1"] - b["c0"]
        wblock_load(C, wt, wres, W, KC, b["c0"], ncols)
        for grp in b["groups"]:
            for hf in halves:
                for ci in grp:
                    c0, M = chunks[ci]
                    off = c0 - b["c0"]
                    ps, psr = C.psum()
                    rhs_list = [rhs_fn(kc, hf) for kc in range(KC)]
                    rres = []
                    for (_, r) in rhs_list:
                        if r not in rres:
                            rres.append(r)

                    def mm(e, ps=ps, wt=wt, off=off, M=M, rhs_list=rhs_list, ncols=ncols):
                        return [e.matmul(ps[0:M, :], wslice(wt, ncols, kc, off, off + M), rhs_list[kc][0],
                                         start=(kc == 0), stop=(kc == KC - 1)) for kc in range(KC)]
                    S.op("pe", mm, reads=[wres] + rres, writes=[psr], name="lin_mm")
                    epi(ci, hf, ps, psr)


def rmsnorm_T(C, nch, src_fn, g_col, g_res, out_fn, D, tag, post=None):
    S = C.S
    C.uid += 1
    if not hasattr(C, "rn_tiles"):
        C.rn_tiles = {
            "sq": [C.tile(f"rn_sq{i}", [128, 512], BF16) for i in range(3)],
            "rstd": C.tile("rn_rstd", [128, 512], F32),
            "i": 0,
        }
    T = C.rn_tiles
    ps, psr = C.psum()
    for kc in range(nch):
        src, sres = src_fn(kc)
        sq, sqr = T["sq"][T["i"] % 3]
        T["i"] += 1
        S.op("act", lambda e, sq=sq, src=src: e.activation(out=sq[:], in_=src, func=AF.Square),
             reads=[sres], writes=[sqr])
        S.op("pe", lambda e, sq=sq, kc=kc: e.matmul(ps[:], C.ones_b[:], sq[:], start=(kc == 0), stop=(kc == nch - 1)),
             reads=[sqr, C.r_const], writes=[psr])
    rstd, rr = T["rstd"]
    S.op("act", lambda e: e.activation(out=rstd[:], in_=ps[:], func=AF.Sqrt, bias=C.eps_t[:, 0:1], scale=1.0 / D),
         reads=[psr, C.r_const], writes=[rr])
    S.op("dve", lambda e: e.reciprocal(out=rstd[:], in_=rstd[:]), reads=[rr], writes=[rr])
    for kc in range(nch):
        src, sres = src_fn(kc)
        dst, dres = out_fn(kc)
        S.op("dve", lambda e, src=src, dst=dst, kc=kc: e.scalar_tensor_tensor(
            out=dst, in0=src, scalar=g_col[:, kc:kc + 1], in1=rstd[:], op0=ALU.mult, op1=ALU.mult),
            reads=[sres, rr, g_res], writes=[dres])
        if post is not None:
            post(kc, dst, dres)


def rope_tables(C, pos_f, pos_res, inv_col, inv_res, cos_t, sin_t, tres, ncols, nparts, tmp):
    S = C.S
    tmp_t, tmp_r = tmp
    if not hasattr(C, "rt_extra"):
        C.rt_extra = (C.tile("rt_a", [128, ncols], F32), C.tile("rt_i", [128, ncols], I32), C.tile("rt_m", [128, ncols], F32))
    (A, Ar), (II, IIr), (M, Mr) = C.rt_extra
    P = slice(0, nparts)
    N = slice(0, ncols)
    for (dst, phase) in ((sin_t, 0.0), (cos_t, math.pi / 2)):
        def f(e, dst=dst, phase=phase):
            R = tmp_t
            ins = []
            ins.append(e.tensor_scalar(out=A[P, N], in0=pos_f[P, N], scalar1=inv_col[P, 0:1], scalar2=phase,
                                       op0=ALU.mult, op1=ALU.add))
            ins.append(e.tensor_scalar(out=M[P, N], in0=A[P, N], scalar1=1.0 / TWO_PI, scalar2=None, op0=ALU.mult))
            ins.append(e.tensor_copy(out=II[P, N], in_=M[P, N]))
            ins.append(e.tensor_copy(out=M[P, N], in_=II[P, N]))
            ins.append(e.scalar_tensor_tensor(out=R[P, N], in0=M[P, N], scalar=-TWO_PI, in1=A[P, N], op0=ALU.mult, op1=ALU.add))
            ins.append(e.tensor_single_scalar(out=M[P, N], in_=R[P, N], scalar=math.pi, op=ALU.is_gt))
            ins.append(e.scalar_tensor_tensor(out=R[P, N], in0=M[P, N], scalar=-TWO_PI, in1=R[P, N], op0=ALU.mult, op1=ALU.add))
            ins.append(e.tensor_single_scalar(out=M[P, N], in_=R[P, N], scalar=-math.pi, op=ALU.is_lt))
            ins.append(e.scalar_tensor_tensor(out=R[P, N], in0=M[P, N], scalar=TWO_PI, in1=R[P, N], op0=ALU.mult, op1=ALU.add))
            return ins
        S.op("dve", f, reads=[pos_res, inv_res], writes=[tmp_r, Ar, IIr, Mr])
        S.op("act", lambda e, dst=dst: e.activation(out=dst[P, N], in_=tmp_t[P, N], func=AF.Sin),
             reads=[tmp_r], writes=[tres])


def load_pos(C, pos_d, pos_dres, ntok):
    S = C.S
    pi_t, pir = C.tile("pos_i", [128, ntok], I32)
    pf_t, pfr = C.tile("pos_f", [128, ntok], F32)
    src = bass.AP(pos_d.tensor, pos_d.offset, [[0, 128], [1, ntok]])
    S.op("sync", lambda e: e.dma_start(out=pi_t[:], in_=src), reads=[pos_dres], writes=[pir], dma=True, dsem=pir)
    S.op("dve", lambda e: e.tensor_copy(out=pf_t[:], in_=pi_t[:]), reads=[pir], writes=[pfr])
    return pf_t, pfr


def build_p1(NT=1024, C=None, io=None):
    global WT_ELEMS
    WT_ELEMS = 16384
    standalone = C is None
    if standalone:
        nc = bass.Bass("TRN2", target_bir_lowering=False)
        C = Ctx(nc)
        C.begin_phase("", 8, {})
    else:
        nc = C.nc
        C.begin_phase("p1_", 8, io or {})
    S = C.S
    NH = NT // 512
    xT, r_xT = C.dram_in("xT", [32, 128, NT], F32)
    pos, r_pos = C.dram_in("pos", [1, NT], I32)
    W1, r_W1 = C.dram_in("W1", [4096, 2304], F32)
    Wq, r_Wq = C.dram_in("Wq", [1536, 8192], F32)
    sgn, r_sgn = C.dram_in("sgn64", [128, 1], F32)
    gA, r_gA = C.dram_in("gA", [128, 32], F32)
    gQ, r_gQ = C.dram_in("gQ", [128, 12], F32)
    gKV, r_gKV = C.dram_in("gKV", [128, 4], F32)
    inv32, r_inv = C.dram_in("inv32", [128, 1], F32)
    o_qn, r_oqn = C.dram_out("o_qn", [32, 128, NT], BF16)
    o_qr, r_oqr = C.dram_out("o_qr", [16, 128, NT], BF16)
    o_ckv, r_ockv = C.dram_out("o_ckv", [4, 128, NT], BF16)
    o_kr, r_okr = C.dram_out("o_kr", [128, NT], BF16)

    gA_t, gA_r = C.tile("gA_t", [128, 32], F32)
    gQ_t, gQ_r = C.tile("gQ_t", [128, 12], F32)
    gKV_t, gKV_r = C.tile("gKV_t", [128, 4], F32)
    inv_t, inv_r = C.tile("inv_t", [128, 1], F32)
    sgn_t, sgn_r = C.tile("sgn_t", [128, 1], F32)
    for (t, r, d) in ((gA_t, gA_r, gA), (gQ_t, gQ_r, gQ), (gKV_t, gKV_r, gKV), (inv_t, inv_r, inv32), (sgn_t, sgn_r, sgn)):
        S.op("sync", lambda e, t=t, d=d: e.dma_start(out=t[:], in_=d), writes=[r], dma=True, dsem=r)
    pos_f, pos_fr = load_pos(C, pos, r_pos, NT)
    cos_t, cs_r = C.tile("cos_t", [128, NT], F32)
    sin_t, _ = C.tile("sin_t", [128, NT], F32)
    tmp = C.tile("rt_tmp", [128, NT], F32)
    rope_tables(C, pos_f, pos_fr, inv_t, inv_r, cos_t, sin_t, cs_r, NT, 128, tmp)
    S.op("dve", lambda e: e.tensor_scalar(out=sin_t[:], in0=sin_t[:], scalar1=sgn_t[:, 0:1], scalar2=None, op0=ALU.mult),
         reads=[cs_r, sgn_r], writes=[cs_r])

    wtiles = [C.tile(f"w{i}", [128, WT_ELEMS], BF16) for i in range(2)]
    wstate = [0]
    xs = [C.tile(f"xs{i}", [128, 512], F32) for i in range(4)]
    xsi = [0]
    xnT, xn_r = C.tile("xnT", [128, 32, 512], BF16)
    uT, uT_r = C.tile("uT", [128, 16, 512], F32)
    kraw = [C.tile(f"kraw{i}", [128, 512], F32) for i in range(2)]
    cqT, cq_r = C.tile("cqT", [128, 12, 512], BF16)
    ckvT, ckv_r = C.tile("ckvT", [128, 4, 512], BF16)
    st = [C.tile(f"st{i}", [128, 512], BF16) for i in range(4)]
    sti = [0]
    rtmp = [C.tile(f"rtmp{i}", [128, 512], F32) for i in range(2)]
    qraw = C.tile("qraw", [128, 512], F32)

    def rope1(X, Xr, Xs, Xsr, hs, out_d, r_o):
        c = cos_t[:, hs]
        s_ = sin_t[:, hs]
        (ta, tar), (tb, tbr) = rtmp
        o, o_r = st[sti[0] % 4]
        sti[0] += 1
        S.op("dve", lambda e: e.tensor_tensor(out=ta[:], in0=X, in1=c, op=ALU.mult), reads=[Xr, cs_r], writes=[tar])
        S.op("dve", lambda e: e.tensor_tensor(out=tb[:], in0=Xs, in1=s_, op=ALU.mult), reads=[Xsr, cs_r], writes=[tbr])
        S.op("dve", lambda e: e.tensor_tensor(out=o[:], in0=ta[:], in1=tb[:], op=ALU.add), reads=[tar, tbr], writes=[o_r])
        S.op("sync", lambda e: e.dma_start(out=out_d, in_=o[:]), reads=[o_r], writes=[r_o], dma=True, dsem=o_r)

    def do_half(hf):
        hs = slice(hf * 512, (hf + 1) * 512)

        def src_x(kc):
            t, r = xs[xsi[0] % 4]
            xsi[0] += 1
            S.op("sync", lambda e, t=t, kc=kc: e.dma_start(out=t[:], in_=xT[kc, :, hs]), writes=[r], dma=True, dsem=r)
            return t[:], r
        rmsnorm_T(C, 32, src_x, gA_t, gA_r, lambda kc: (xnT[:, kc, :], xn_r), 4096.0, "a")

        def epi1(ci, h_, ps, psr):
            if ci < 16:
                S.op("act", lambda e: e.activation(out=uT[:, ci, :], in_=ps[:], func=AF.Copy), reads=[psr], writes=[uT_r])
            else:
                t, r = kraw[ci - 16]
                S.op("act", lambda e: e.activation(out=t[:], in_=ps[:], func=AF.Copy), reads=[psr], writes=[r])
        linear_T(C, W1, 32, lambda kc, h_: (xnT[:, kc, :], xn_r), [(i * 128, 128) for i in range(18)], [0], epi1,
                 wtiles, wstate=wstate)
        rope1(kraw[0][0][:], kraw[0][1], kraw[1][0][:], kraw[1][1], hs, o_kr[:, hs], r_okr)
        rmsnorm_T(C, 12, lambda kc: (uT[:, kc, :], uT_r), gQ_t, gQ_r, lambda kc: (cqT[:, kc, :], cq_r), 1536.0, "q")
        rmsnorm_T(C, 4, lambda kc: (uT[:, 12 + kc, :], uT_r), gKV_t, gKV_r, lambda kc: (ckvT[:, kc, :], ckv_r), 512.0, "kv")
        S.op("sync", lambda e: e.dma_start(out=o_ckv[:, :, hs].rearrange("c p t -> p c t"), in_=ckvT[:]),
             reads=[ckv_r], writes=[r_ockv], dma=True, dsem=ckv_r)

        order = [(i * 128, 128) for i in range(32)]
        for Pp in range(16):
            order.append((4096 + Pp * 256, 128))
            order.append((4096 + Pp * 256 + 128, 128))

        def epi2(ci, h_, ps, psr):
            if ci < 32:
                o, o_r = st[sti[0] % 4]
                sti[0] += 1
                S.op("act", lambda e: e.activation(out=o[:], in_=ps[:], func=AF.Copy), reads=[psr], writes=[o_r])
                S.op("sync", lambda e: e.dma_start(out=o_qn[ci, :, hs], in_=o[:]), reads=[o_r], writes=[r_oqn],
                     dma=True, dsem=o_r)
            else:
                Q = (ci - 32) // 2
                if (ci - 32) % 2 == 0:
                    S.op("act", lambda e: e.activation(out=qraw[0][:], in_=ps[:], func=AF.Copy), reads=[psr], writes=[qraw[1]])
                else:
                    t, r = kraw[0]
                    S.op("act", lambda e: e.activation(out=t[:], in_=ps[:], func=AF.Copy), reads=[psr], writes=[r])
                    rope1(qraw[0][:], qraw[1], t[:], r, hs, o_qr[Q, :, hs], r_oqr)
        linear_T(C, Wq, 12, lambda kc, h_: (cqT[:, kc, :], cq_r), order, [0], epi2, wtiles, wstate=wstate)

    for hf in range(NH):
        do_half(hf)
    if standalone:
        S.op("sync", None, reads=[r_oqn, r_oqr, r_ockv, r_okr])
        S.emit()
        return nc
    C.end_phase()


def build_p2a(NT=1024, NS=2048, C=None, io=None):
    global WT_ELEMS
    WT_ELEMS = 8192
    standalone = C is None
    if standalone:
        nc = bass.Bass("TRN2", target_bir_lowering=False)
        C = Ctx(nc)
        C.begin_phase("", 4, {})
    else:
        nc = C.nc
        C.begin_phase("p2a_", 4, io or {})
    S = C.S
    scale = 192.0 ** -0.5
    xT, r_xT = C.dram_in("xT", [32, 128, NT], F32)
    if "ex1" not in (io or {}):
        ckv_d, _ = C.dram_in("ckv_full", [4, 128, NS], BF16)
        kr1_d, _ = C.dram_in("kr_full", [128, NS], BF16)
    qn_d, _ = C.dram_in("qn", [32, 128, NT], BF16)
    qr1_d, _ = C.dram_in("qr", [16, 128, NT], BF16)
    Wk, _ = C.dram_in("Wk", [512, 4096], F32)
    Wv, _ = C.dram_in("Wv", [512, 4096], F32)
    Wo, _ = C.dram_in("Wo", [4096, 4096], F32)
    tri_d, _ = C.dram_in("maskAB", [128, 256], BF16)
    o_hA, r_ohA = C.dram_out("o_hA", [32, 128, NT], F32)
    o_dbg, r_odbg = C.dram_out("o_attn", [32, 128, NT], BF16)

    ckv, ckv_r = C.tile("ckv", [128, 4, NS], BF16)
    kr1, kr1_r = C.tile("kr1", [128, NS], BF16)
    tri, tri_r = C.tile("tri", [128, 256], BF16)
    if "ex1" in C.io:
        ex1, r_ex1 = C.io["ex1"]

        def ld_ckv(e):
            ins = []
            for r in range(2):
                for c in range(4):
                    dst = ckv[:, c, :].rearrange("p (j r t) -> p j r t", r=2, t=128)[:, :, r, :]
                    src = ex1[r * 640 + c * 128:r * 640 + (c + 1) * 128, :].rearrange("p (j t) -> p j t", t=128)
                    ins.append(e.dma_start(out=dst, in_=src))
            return ins
        S.op("sync", ld_ckv, reads=[r_ex1], writes=[ckv_r], dma=True, dsem=ckv_r, ndma=8)
        S.op("sync", lambda e: [e.dma_start(out=kr1[:, :].rearrange("p (j r t) -> p j r t", r=2, t=128)[:, :, r, :],
                                            in_=ex1[r * 640 + 512:r * 640 + 640, :].rearrange("p (j t) -> p j t", t=128)) for r in range(2)],
             reads=[r_ex1], writes=[kr1_r], dma=True, dsem=kr1_r, ndma=2)
    else:
        S.op("sync", lambda e: e.dma_start(out=ckv[:], in_=ckv_d.rearrange("c p t -> p c t")), writes=[ckv_r], dma=True, dsem=ckv_r)
        S.op("sync", lambda e: e.dma_start(out=kr1[:], in_=kr1_d), writes=[kr1_r], dma=True, dsem=kr1_r)
    S.op("sync", lambda e: e.dma_start(out=tri[:], in_=tri_d), writes=[tri_r], dma=True, dsem=tri_r)

    wtiles = [C.tile(f"w{i}", [128, WT_ELEMS], BF16) for i in range(2)]
    wstate = [0]
    kTq, kTq_r = C.tile("kTq", [128, 4, NS], BF16)
    Vq, Vq_r = C.tile("Vq", [128, 16, 512], BF16)
    qn_t = [C.tile(f"qn{i}", [128, NT], BF16) for i in range(2)]
    qr1_t = [C.tile(f"qr1_{i}", [128, NT], BF16) for i in range(2)]
    pt_t = [C.tile(f"pt{i}", [128, 512], BF16) for i in range(4)]
    pti = [0]
    rec_t = [C.tile(f"rec{i}", [128, 512], F32) for i in range(2)]
    oT, oT_r = C.tile("oT", [128, 32, NT], BF16)
    acc = [((C.ps[4], C.psr[4]), (C.ps[5], C.psr[5])), ((C.ps[6], C.psr[6]), (C.ps[7], C.psr[7]))]
    acci = [0]

    for Q in range(8):
        def epik(ci, tchunk, ps, psr):
            S.op("act", lambda e: e.activation(out=kTq[:, ci, tchunk * 512:(tchunk + 1) * 512], in_=ps[:], func=AF.Copy),
                 reads=[psr], writes=[kTq_r])
        linear_T(C, Wk, 4, lambda kc, tch: (ckv[:, kc, tch * 512:(tch + 1) * 512], ckv_r),
                 [(Q * 512 + i * 128, 128) for i in range(4)], [0, 1, 2, 3], epik, wtiles, wstate=wstate)
        wt, wres = wtiles[wstate[0] % 2]
        wstate[0] += 1
        wblock_load(C, wt, wres, Wv, 4, Q * 512, 512)
        for kt in range(NS // 128):
            ps, psr = C.psum()
            S.op("pe", lambda e, ps=ps, wt=wt, kt=kt: [e.matmul(ps[:], ckv[:, kc, kt * 128:(kt + 1) * 128], wslice(wt, 512, kc, 0, 512),
                                                                 start=(kc == 0), stop=(kc == 3)) for kc in range(4)],
                 reads=[wres, ckv_r], writes=[psr])
            S.op("dve", lambda e, ps=ps, kt=kt: e.tensor_copy(out=Vq[:, kt, :], in_=ps[:]), reads=[psr], writes=[Vq_r])
        for hq in range(4):
            h = 4 * Q + hq
            if hq % 2 == 0:
                q1, q1r = qr1_t[(h // 2) % 2]
                S.op("sync", lambda e, q1=q1, h=h: e.dma_start(out=q1[:], in_=qr1_d[h // 2]), writes=[q1r], dma=True, dsem=q1r)
            qn, qnr = qn_t[h % 2]
            S.op("sync", lambda e, qn=qn, h=h: e.dma_start(out=qn[:], in_=qn_d[h]), writes=[qnr], dma=True, dsem=qnr)
            P = slice(64 * (h % 2), 64 * (h % 2) + 64)
            for G in range(2):
                j0 = 4 * G
                (po, por), (pd, pdr) = acc[acci[0] % 2]
                acci[0] += 1
                kt_last = 2 * (j0 + 3) + 1
                stt = {}

                def stA(kt, j0=j0, qn=qn, qnr=qnr, q1=q1, q1r=q1r, P=P, hq=hq):
                    ja = max(j0, kt // 2)
                    c0 = (ja - j0) * 128
                    N = 512 - c0
                    qc0 = j0 * 128 + c0
                    qc1 = (j0 + 4) * 128
                    diag = (kt // 2) >= j0
                    mcol = (kt % 2) * 128
                    ps, psr = C.psum()
                    ks = slice(kt * 128, (kt + 1) * 128)
                    S.op("pe", lambda e: [e.matmul(ps[:, 0:N], kTq[:, hq, ks], qn[:, qc0:qc1], start=True, stop=False),
                                          e.matmul(ps[:, 0:N], kr1[P, ks], q1[P, qc0:qc1], start=False, stop=True)],
                         reads=[kTq_r, qnr, q1r, kr1_r], writes=[psr])
                    pt, ptr = pt_t[pti[0] % 4]
                    pti[0] += 1
                    S.op("act", lambda e: e.activation(out=pt[:, 0:N], in_=ps[:, 0:N], func=AF.Exp, scale=scale), reads=[psr], writes=[ptr])
                    if diag:
                        S.op("dve", lambda e: e.tensor_tensor(out=pt[:, 0:128], in0=pt[:, 0:128], in1=tri[:, mcol:mcol + 128], op=ALU.mult),
                             reads=[ptr, tri_r], writes=[ptr])
                    stt[kt] = (pt, ptr, c0, N)

                def stB(kt, po=po, por=por, pd=pd, pdr=pdr, hq=hq, kt_last=kt_last):
                    pt, ptr, c0, N = stt.pop(kt)
                    S.op("pe", lambda e: [
                        e.matmul(po[:, c0:512], Vq[:, kt, hq * 128:(hq + 1) * 128], pt[:, 0:N], start=(kt == 0), stop=(kt == kt_last)),
                        e.matmul(pd[:, c0:512], C.ones_b[:], pt[:, 0:N], start=(kt == 0), stop=(kt == kt_last))],
                        reads=[ptr, Vq_r, C.r_const], writes=[por, pdr])
                LA = 3
                for n in range(kt_last + 1 + LA):
                    if n <= kt_last:
                        stA(n)
                    if n - LA >= 0:
                        stB(n - LA)
                rec, recr = rec_t[G]
                S.op("dve", lambda e, rec=rec, pd=pd: e.reciprocal(out=rec[:], in_=pd[:]), reads=[pdr], writes=[recr])
                S.op("dve", lambda e, rec=rec, po=po, h=h, j0=j0: e.tensor_tensor(out=oT[:, h, j0 * 128:(j0 + 4) * 128], in0=po[:], in1=rec[:],
                                                                                 op=ALU.mult), reads=[por, recr], writes=[oT_r])
    S.op("sync", lambda e: e.dma_start(out=o_dbg.rearrange("c p t -> p c t"), in_=oT[:]), reads=[oT_r], writes=[r_odbg], dma=True, dsem=oT_r)

    xs = [C.tile(f"xs{i}", [128, 512], F32) for i in range(4)]
    xsi = [0]

    def epio(ci, hf, ps, psr):
        t, r = xs[xsi[0] % 4]
        xsi[0] += 1
        hs = slice(hf * 512, (hf + 1) * 512)
        S.op("sync", lambda e: e.dma_start(out=t[:], in_=xT[ci, :, hs]), writes=[r], dma=True, dsem=r)
        S.op("dve", lambda e: e.tensor_tensor(out=t[:], in0=ps[:], in1=t[:], op=ALU.add), reads=[psr, r], writes=[r])
        S.op("sync", lambda e: e.dma_start(out=o_hA[ci, :, hs], in_=t[:]), reads=[r], writes=[r_ohA], dma=True, dsem=r)
    linear_T(C, Wo, 32, lambda kc, hf: (oT[:, kc, hf * 512:(hf + 1) * 512], oT_r), [(i * 128, 128) for i in range(32)],
             [0, 1], epio, wtiles, wstate=wstate)
    if standalone:
        S.op("sync", None, reads=[r_ohA, r_odbg])
        S.emit()
        return nc
    C.end_phase()


def build_ffn(final_norm=False, NT=1024, DFF=11008, C=None, io=None, tag='ffn'):
    global WT_ELEMS
    WT_ELEMS = 16384
    standalone = C is None
    if standalone:
        nc = bass.Bass("TRN2", target_bir_lowering=False)
        C = Ctx(nc)
        C.begin_phase("", 8, {})
    else:
        nc = C.nc
        C.begin_phase(tag + "_", 8, io or {})
    S = C.S
    NJ = DFF // 128
    hT, r_hT = C.dram_in("hT", [32, 128, NT], F32)
    Win, _ = C.dram_in("Win", [4096, 2 * DFF], F32)
    Wout, _ = C.dram_in("Wout", [DFF, 4096], F32)
    gF, _ = C.dram_in("gF", [128, 32], F32)
    o_h, r_oh = C.dram_out("o_h", [32, 128, NT], F32)
    gF_t, gF_r = C.tile("gF_t", [128, 32], F32)
    S.op("sync", lambda e: e.dma_start(out=gF_t[:], in_=gF), writes=[gF_r], dma=True, dsem=gF_r)
    if final_norm:
        gN, _ = C.dram_in("gN", [128, 32], F32)
        o_fin, r_ofin = C.dram_out("o_fin", [32, 128, NT], F32)
        gN_t, gN_r = C.tile("gN_t", [128, 32], F32)
        S.op("sync", lambda e: e.dma_start(out=gN_t[:], in_=gN), writes=[gN_r], dma=True, dsem=gN_r)
    WTE = 12288
    wtiles = [C.tile(f"w{i}", [128, WTE], BF16) for i in range(2)]
    wstate = [0]
    xs = [C.tile(f"xs{i}", [128, 512], F32) for i in range(4)]
    xsi = [0]
    NHF = NT // 512
    NTH = 3
    JT = (NJ + NTH - 1) // NTH
    xnT, xn_r = C.tile("xnT", [128, 32, NT], BF16)
    gT, gT_r = C.tile("gT", [128, JT, NT], BF16)
    sa_t = [C.tile(f"sa{i}", [128, 512], F32) for i in range(2)]
    fo_t = [C.tile(f"fo{i}", [128, 512], F32) for i in range(2)]
    foi = [0]

    for hf in range(NHF):
        hs = slice(hf * 512, (hf + 1) * 512)

        def src_x(kc, hs=hs):
            t, r = xs[xsi[0] % 4]
            xsi[0] += 1
            S.op("sync", lambda e, t=t, kc=kc: e.dma_start(out=t[:], in_=hT[kc, :, hs]), writes=[r], dma=True, dsem=r)
            return t[:], r
        rmsnorm_T(C, 32, src_x, gF_t, gF_r, lambda kc, hs=hs: (xnT[:, kc, hs], xn_r), 4096.0, "f")

    def do_third(t3):
        jlo = t3 * JT
        jhi = min(NJ, jlo + JT)
        nj = jhi - jlo

        def epi1(ci, hf, ps, psr):
            hs = slice(hf * 512, (hf + 1) * 512)
            jj = ci // 2
            sa, sar = sa_t[hf % 2]
            if ci % 2 == 0:
                S.op("act", lambda e: e.activation(out=sa[:], in_=ps[:], func=AF.Silu), reads=[psr], writes=[sar])
            else:
                S.op("dve", lambda e: e.tensor_tensor(out=gT[:, jj, hs], in0=ps[:], in1=sa[:], op=ALU.mult),
                     reads=[psr, sar], writes=[gT_r])
        linear_T_pairs(C, Win, 32, lambda kc, hf: (xnT[:, kc, hf * 512:(hf + 1) * 512], xn_r),
                       [(jlo * 256 + i * 128, 128) for i in range(2 * nj)], list(range(NHF)), epi1, wtiles, wstate, group=2)

        def epi2(ci, hf, ps, psr):
            hs = slice(hf * 512, (hf + 1) * 512)
            t, r = xs[xsi[0] % 4]
            xsi[0] += 1
            if t3 == 0:
                S.op("sync", lambda e: e.dma_start(out=t[:], in_=hT[ci, :, hs]), writes=[r], dma=True, dsem=r)
            else:
                S.op("sync", lambda e: e.dma_start(out=t[:], in_=o_h[ci, :, hs]), reads=[r_oh], writes=[r], dma=True, dsem=r)
            S.op("dve", lambda e: e.tensor_tensor(out=t[:], in0=ps[:], in1=t[:], op=ALU.add), reads=[psr, r], writes=[r])
            S.op("sync", lambda e: e.dma_start(out=o_h[ci, :, hs], in_=t[:]), reads=[r], writes=[r_oh], dma=True, dsem=r)
        linear_T(C, Wout[jlo * 128:jhi * 128, :], nj, lambda kc, hf: (gT[:, kc, hf * 512:(hf + 1) * 512], gT_r),
                 [(i * 128, 128) for i in range(32)], list(range(NHF)), epi2, wtiles, wstate=wstate)
    for t3 in range(NTH):
        do_third(t3)

    if final_norm:
        for hf in range(NHF):
            hs = slice(hf * 512, (hf + 1) * 512)

            def src_h(kc, hs=hs):
                t, r = xs[xsi[0] % 4]
                xsi[0] += 1
                S.op("sync", lambda e, t=t, kc=kc: e.dma_start(out=t[:], in_=o_h[kc, :, hs]), reads=[r_oh], writes=[r],
                     dma=True, dsem=r)
                return t[:], r
            rmsnorm_store(C, 32, src_h, gN_t, gN_r, fo_t, foi, 4096.0, lambda kc, hs=hs: o_fin[kc, :, hs], r_ofin)
    if standalone:
        S.op("sync", None, reads=[r_oh] + ([r_ofin] if final_norm else []))
        S.emit()
        return nc
    C.end_phase()


def rmsnorm_store(C, nch, src_fn, g_col, g_res, fo_t, foi, D, dst_fn, dst_res):
    S = C.S

    def out_fn(kc):
        t, r = fo_t[foi[0] % 2]
        return t[:], r
    T_store = []

    def out_fn2(kc):
        t, r = fo_t[foi[0] % 2]
        foi[0] += 1
        T_store.append((kc, t, r))
        return t[:], r
    rmsnorm_T(C, nch, src_fn, g_col, g_res, out_fn2, D, "fin",
              post=lambda kc, t_ap, r: S.op("sync", lambda e: e.dma_start(out=dst_fn(kc), in_=t_ap), reads=[r], writes=[dst_res],
                                           dma=True, dsem=r))


def build_p3a(NT=1024, C=None, io=None):
    global WT_ELEMS
    WT_ELEMS = 16384
    standalone = C is None
    if standalone:
        nc = bass.Bass("TRN2", target_bir_lowering=False)
        C = Ctx(nc)
        C.begin_phase("", 8, {})
    else:
        nc = C.nc
        C.begin_phase("p3a_", 8, io or {})
    S = C.S
    hT, _ = C.dram_in("hT", [32, 128, NT], F32)
    pos, r_pos = C.dram_in("pos", [1, NT], I32)
    Ws, _ = C.dram_in("Ws", [4096, 3840], F32)
    Wb, _ = C.dram_in("Wb", [4096, 6240], F32)
    gS, _ = C.dram_in("gS", [128, 32], F32)
    gB, _ = C.dram_in("gB", [128, 32], F32)
    inv96, _ = C.dram_in("inv96", [128, 1], F32)
    if "o_kv_fn" in (io or {}):
        okf, ovf, r_ok = io["o_kv_fn"]
        r_ov = r_ok
    else:
        o_k, r_ok = C.dram_out("o_k", [12, 2, 96, NT], BF16)
        o_v, r_ov = C.dram_out("o_v", [12, 128, NT], BF16)
        okf = lambda bg, pc: o_k[bg, pc]
        ovf = lambda bg: o_v[bg]
    o_q, r_oq = C.dram_out("o_q", [32, 2, 96, NT], BF16)
    o_g, r_og = C.dram_out("o_g", [96, NT], F32)
    gS_t, gS_r = C.tile("gS_t", [128, 32], F32)
    gB_t, gB_r = C.tile("gB_t", [128, 32], F32)
    inv_t, inv_r = C.tile("inv_t", [128, 1], F32)
    for (t, r, d) in ((gS_t, gS_r, gS), (gB_t, gB_r, gB), (inv_t, inv_r, inv96)):
        S.op("sync", lambda e, t=t, d=d: e.dma_start(out=t[:], in_=d), writes=[r], dma=True, dsem=r)
    pos_f, pos_fr = load_pos(C, pos, r_pos, NT)
    cos_t, cs_r = C.tile("cos_t", [128, NT], F32)
    sin_t, _ = C.tile("sin_t", [128, NT], F32)
    tmp = C.tile("rt_tmp", [128, NT], F32)
    rope_tables(C, pos_f, pos_fr, inv_t, inv_r, cos_t, sin_t, cs_r, NT, 96, tmp)

    wtiles = [C.tile(f"w{i}", [128, WT_ELEMS], BF16) for i in range(2)]
    wstate = [0]
    xs = [C.tile(f"xs{i}", [128, 512], F32) for i in range(4)]
    xsi = [0]
    xnT, xn_r = C.tile("xnT", [128, 32, NT], BF16)
    raw1 = C.tile("raw1", [128, 512], F32)
    raw2 = C.tile("raw2", [128, 512], F32)
    rt = [C.tile(f"rtmp{i}", [128, 512], F32) for i in range(4)]
    st = [C.tile(f"st{i}", [128, 512], BF16) for i in range(4)]
    sti = [0]
    gt = C.tile("gt", [128, 512], F32)
    P = slice(0, 96)

    def ropeN(hs, d1, d2, rd):
        (x1, x1r), (x2, x2r) = raw1, raw2
        c = cos_t[P, hs]
        s_ = sin_t[P, hs]
        (ta, tar), (tb, tbr), (tc, tcr), (td, tdr) = rt
        o1, o1r = st[sti[0] % 4]
        o2, o2r = st[(sti[0] + 1) % 4]
        sti[0] += 2
        S.op("dve", lambda e: e.tensor_tensor(out=ta[P, :], in0=x1[P, :], in1=c, op=ALU.mult), reads=[x1r, cs_r], writes=[tar])
        S.op("dve", lambda e: e.tensor_tensor(out=tb[P, :], in0=x2[P, :], in1=s_, op=ALU.mult), reads=[x2r, cs_r], writes=[tbr])
        S.op("dve", lambda e: e.tensor_tensor(out=o1[P, :], in0=ta[P, :], in1=tb[P, :], op=ALU.subtract), reads=[tar, tbr], writes=[o1r])
        S.op("dve", lambda e: e.tensor_tensor(out=tc[P, :], in0=x2[P, :], in1=c, op=ALU.mult), reads=[x2r, cs_r], writes=[tcr])
        S.op("dve", lambda e: e.tensor_tensor(out=td[P, :], in0=x1[P, :], in1=s_, op=ALU.mult), reads=[x1r, cs_r], writes=[tdr])
        S.op("dve", lambda e: e.tensor_tensor(out=o2[P, :], in0=tc[P, :], in1=td[P, :], op=ALU.add), reads=[tcr, tdr], writes=[o2r])
        S.op("sync", lambda e: e.dma_start(out=d1, in_=o1[P, :]), reads=[o1r], writes=[rd], dma=True, dsem=o1r)
        S.op("sync", lambda e: e.dma_start(out=d2, in_=o2[P, :]), reads=[o2r], writes=[rd], dma=True, dsem=o2r)

    NHF = NT // 512

    def norm_all(g_t, g_r, tag):
        for hf in range(NHF):
            hs = slice(hf * 512, (hf + 1) * 512)

            def src_x(kc, hs=hs):
                t, r = xs[xsi[0] % 4]
                xsi[0] += 1
                S.op("sync", lambda e, t=t, kc=kc: e.dma_start(out=t[:], in_=hT[kc, :, hs]), writes=[r], dma=True, dsem=r)
                return t[:], r
            rmsnorm_T(C, 32, src_x, g_t, g_r, lambda kc, hs=hs: (xnT[:, kc, hs], xn_r), 4096.0, tag)

    def rhs(kc, hf):
        return (xnT[:, kc, hf * 512:(hf + 1) * 512], xn_r)

    norm_all(gS_t, gS_r, "s")
    chunks = []
    for bg in range(12):
        chunks += [(bg * 320, 96), (bg * 320 + 96, 96), (bg * 320 + 192, 128)]

    def epis(ci, hf, ps, psr):
        hs = slice(hf * 512, (hf + 1) * 512)
        bg, k = ci // 3, ci % 3
        if k == 0:
            S.op("act", lambda e: e.activation(out=raw1[0][P, :], in_=ps[P, :], func=AF.Copy), reads=[psr], writes=[raw1[1]])
        elif k == 1:
            S.op("act", lambda e: e.activation(out=raw2[0][P, :], in_=ps[P, :], func=AF.Copy), reads=[psr], writes=[raw2[1]])
            ropeN(hs, okf(bg, 0)[:, hs], okf(bg, 1)[:, hs], r_ok)
        else:
            o, o_r = st[sti[0] % 4]
            sti[0] += 1
            S.op("act", lambda e: e.activation(out=o[:], in_=ps[:], func=AF.Copy), reads=[psr], writes=[o_r])
            S.op("sync", lambda e: e.dma_start(out=ovf(bg)[:, hs], in_=o[:]), reads=[o_r], writes=[r_ov], dma=True, dsem=o_r)
    linear_T_pairs(C, Ws, 32, rhs, chunks, list(range(NHF)), epis, wtiles, wstate, group=3)

    norm_all(gB_t, gB_r, "b")
    chunks = []
    for h in range(32):
        chunks += [(h * 192, 96), (h * 192 + 96, 96)]
    chunks.append((6144, 96))

    def epib(ci, hf, ps, psr):
        hs = slice(hf * 512, (hf + 1) * 512)
        if ci == 64:
            g_t, g_r = gt
            S.op("act", lambda e: e.activation(out=g_t[P, :], in_=ps[P, :], func=AF.Sigmoid), reads=[psr], writes=[g_r])
            S.op("sync", lambda e: e.dma_start(out=o_g[:, hs], in_=g_t[P, :]), reads=[g_r], writes=[r_og], dma=True, dsem=g_r)
            return
        h, k = ci // 2, ci % 2
        if k == 0:
            S.op("act", lambda e: e.activation(out=raw1[0][P, :], in_=ps[P, :], func=AF.Copy), reads=[psr], writes=[raw1[1]])
        else:
            S.op("act", lambda e: e.activation(out=raw2[0][P, :], in_=ps[P, :], func=AF.Copy), reads=[psr], writes=[raw2[1]])
            ropeN(hs, o_q[h, 0, :, hs], o_q[h, 1, :, hs], r_oq)
    linear_T_pairs(C, Wb, 32, rhs, chunks, list(range(NHF)), epib, wtiles, wstate, group=2)
    if standalone:
        S.op("sync", None, reads=[r_ok, r_ov, r_oq, r_og])
        S.emit()
        return nc
    C.end_phase()


GELU_C = 1.5957691216057308


def build_p3b1(NS=2048, C=None, io=None):
    standalone = C is None
    if standalone:
        nc = bass.Bass("TRN2", target_bir_lowering=False)
        C = Ctx(nc)
        C.begin_phase("", 8, {})
    else:
        nc = C.nc
        C.begin_phase("p3b1_", 8, io or {})
    S = C.S
    NCMP = 127
    if "ex2" not in (io or {}):
        kTc_d, _ = C.dram_in("kTc", [4, 2, 96, NS], BF16)
        vTc_d, _ = C.dram_in("vTc", [4, 128, NS], BF16)
    w1k_d, _ = C.dram_in("w1k", [6144, 192], F32)
    w2k_d, _ = C.dram_in("w2k", [192, 192], F32)
    pek_d, _ = C.dram_in("pekT", [96, 64], F32)
    w1v_d, _ = C.dram_in("w1v", [4096, 128], F32)
    w2v_d, _ = C.dram_in("w2v", [128, 128], F32)
    pev_d, _ = C.dram_in("pevT", [128, 32], F32)
    o_kc, r_okc = C.dram_out("o_kcT", [4, 2, 96, NCMP], BF16)
    o_vc, r_ovc = C.dram_out("o_vc", [NCMP, 4, 128], BF16)

    w1k, w1k_r = C.tile("w1k_t", [96, 64, 192], BF16)
    w2k, w2k_r = C.tile("w2k_t", [96, 2, 192], BF16)
    pek, pek_r = C.tile("pek_t", [96, 64], BF16)
    w1v, w1v_r = C.tile("w1v_t", [128, 32, 128], BF16)
    w2v, w2v_r = C.tile("w2v_t", [128, 128], BF16)
    pev, pev_r = C.tile("pev_t", [128, 32], BF16)
    w1k_src = w1k_d.rearrange("(i d) n -> d i n", d=96)
    S.op("pool", lambda e: [e.dma_start(out=w1k[:, i * 16:(i + 1) * 16, :], in_=w1k_src[:, i * 16:(i + 1) * 16, :]) for i in range(4)],
         writes=[w1k_r], dma=True, dsem=w1k_r, ndma=4)
    S.op("pool", lambda e: e.dma_start(out=w2k[:], in_=w2k_d.rearrange("(pc d) n -> d pc n", d=96)), writes=[w2k_r], dma=True, dsem=w2k_r)
    S.op("pool", lambda e: e.dma_start(out=pek[:], in_=pek_d), writes=[pek_r], dma=True, dsem=pek_r)
    S.op("pool", lambda e: e.dma_start(out=w1v[:], in_=w1v_d.rearrange("(l d) n -> d l n", d=128)), writes=[w1v_r], dma=True, dsem=w1v_r)
    S.op("pool", lambda e: e.dma_start(out=w2v[:], in_=w2v_d), writes=[w2v_r], dma=True, dsem=w2v_r)
    S.op("pool", lambda e: e.dma_start(out=pev[:], in_=pev_d), writes=[pev_r], dma=True, dsem=pev_r)

    kt_t = [C.tile(f"ktg{i}", [96, 2, NS], BF16) for i in range(2)]
    vt_t = [C.tile(f"vtg{i}", [128, NS], BF16) for i in range(2)]
    bias_t = [C.tile(f"bias{i}", [128, 1], F32) for i in range(2)]
    xg_t = [C.tile(f"xg{i}", [128, 128], F32) for i in range(2)]
    u_t = [C.tile(f"ug{i}", [128, 128], F32) for i in range(2)]
    g_t = [C.tile(f"gg{i}", [128, 128], BF16) for i in range(3)]
    out_t = [C.tile(f"og{i}", [128, 128], BF16) for i in range(2)]
    cnt = [0]

    def gelu_from_psum(ps, psr, bps, bpsr, np_, dst, dst_r):
        i = cnt[0] % 2
        cnt[0] += 1
        (bt, btr), (xg, xgr), (u, ur) = bias_t[i], xg_t[i], u_t[i]
        P = slice(0, np_)
        N = slice(0, NCMP)
        S.op("dve", lambda e: e.tensor_copy(out=bt[P, :], in_=bps[P, 0:1]), reads=[bpsr], writes=[btr])
        S.op("act", lambda e: e.activation(out=xg[P, N], in_=ps[P, N], func=AF.Identity, bias=bt[P, 0:1]), reads=[psr, btr], writes=[xgr])

        def f(e):
            return [e.tensor_tensor(out=u[P, N], in0=xg[P, N], in1=xg[P, N], op=ALU.mult),
                    e.tensor_scalar(out=u[P, N], in0=u[P, N], scalar1=0.044715, scalar2=1.0, op0=ALU.mult, op1=ALU.add),
                    e.tensor_tensor(out=u[P, N], in0=u[P, N], in1=xg[P, N], op=ALU.mult)]
        S.op("dve", f, reads=[xgr], writes=[ur])
        S.op("act", lambda e: e.activation(out=u[P, N], in_=u[P, N], func=AF.Sigmoid, scale=GELU_C), reads=[ur], writes=[ur])
        S.op("dve", lambda e: e.tensor_tensor(out=dst[P, N], in0=xg[P, N], in1=u[P, N], op=ALU.mult), reads=[xgr, ur], writes=[dst_r])

    for g in range(4):
        ktg, ktr = kt_t[g % 2]
        vtg, vtr = vt_t[g % 2]
        if "ex2" in C.io:
            exk, exv, r_ex2 = C.io["ex2"]
            S.op("sync", lambda e, ktg=ktg, g=g: [e.dma_start(
                out=ktg[:, pc, :].rearrange("p (j r t) -> p j r t", r=2, t=128)[:, :, r, :],
                in_=exk(r, g * 2 + pc).rearrange("p (j t) -> p j t", t=128))
                for r in range(2) for pc in range(2)], reads=[r_ex2], writes=[ktr], dma=True, dsem=ktr, ndma=4)
            S.op("sync", lambda e, vtg=vtg, g=g: [e.dma_start(
                out=vtg[:, :].rearrange("p (j r t) -> p j r t", r=2, t=128)[:, :, r, :],
                in_=exv(r, g).rearrange("p (j t) -> p j t", t=128))
                for r in range(2)], reads=[r_ex2], writes=[vtr], dma=True, dsem=vtr, ndma=2)
        else:
            S.op("sync", lambda e, ktg=ktg, g=g: e.dma_start(out=ktg[:], in_=kTc_d[g].rearrange("pc d t -> d pc t")), writes=[ktr], dma=True, dsem=ktr)
            S.op("sync", lambda e, vtg=vtg, g=g: e.dma_start(out=vtg[:], in_=vTc_d[g]), writes=[vtr], dma=True, dsem=vtr)
        gts = []
        for npc in range(2):
            ps, psr = C.psum()
            bps, bpsr = C.psum()
            ns = slice(npc * 96, (npc + 1) * 96)

            def mm(e, ps=ps, ktg=ktg, ns=ns):
                ins = []
                for i in range(64):
                    l, pc = i // 2, i % 2
                    rhs = bass.AP(ktg, pc * NS + l, [[2 * NS, 96], [16, NCMP]])
                    ins.append(e.matmul(ps[0:96, 0:NCMP], w1k[:, i, ns], rhs, start=(i == 0), stop=(i == 63)))
                return ins
            S.op("pe", mm, reads=[w1k_r, ktr], writes=[psr])
            S.op("pe", lambda e, bps=bps, ns=ns: [e.matmul(bps[0:96, 0:1], w1k[:, i, ns], pek[:, i:i + 1], start=(i == 0), stop=(i == 63))
                                                  for i in range(64)], reads=[w1k_r, pek_r], writes=[bpsr])
            gt, gtr = g_t[npc]
            gelu_from_psum(ps, psr, bps, bpsr, 96, gt, gtr)
            gts.append((gt, gtr))
        for n2 in range(2):
            ps, psr = C.psum()
            S.op("pe", lambda e, ps=ps, n2=n2: [e.matmul(ps[0:96, 0:NCMP], w2k[:, npc, n2 * 96:(n2 + 1) * 96], gts[npc][0][0:96, 0:NCMP],
                                                          start=(npc == 0), stop=(npc == 1)) for npc in range(2)],
                 reads=[w2k_r, gts[0][1], gts[1][1]], writes=[psr])
            o, o_r = out_t[n2]
            S.op("act", lambda e, o=o, ps=ps: e.activation(out=o[0:96, 0:NCMP], in_=ps[0:96, 0:NCMP], func=AF.Copy), reads=[psr], writes=[o_r])
            S.op("sync", lambda e, o=o, g=g, n2=n2: e.dma_start(out=o_kc[g, n2], in_=o[0:96, 0:NCMP]), reads=[o_r], writes=[r_okc],
                 dma=True, dsem=o_r)
        ps, psr = C.psum()
        bps, bpsr = C.psum()
        S.op("pe", lambda e, ps=ps, vtg=vtg: [e.matmul(ps[:, 0:NCMP], w1v[:, l, :], bass.AP(vtg, l, [[NS, 128], [16, NCMP]]),
                                                       start=(l == 0), stop=(l == 31)) for l in range(32)],
             reads=[w1v_r, vtr], writes=[psr])
        S.op("pe", lambda e, bps=bps: [e.matmul(bps[:, 0:1], w1v[:, l, :], pev[:, l:l + 1], start=(l == 0), stop=(l == 31)) for l in range(32)],
             reads=[w1v_r, pev_r], writes=[bpsr])
        gt, gtr = g_t[2]
        gelu_from_psum(ps, psr, bps, bpsr, 128, gt, gtr)
        ps, psr = C.psum()
        S.op("pe", lambda e, ps=ps, gt=gt: e.matmul(ps[0:NCMP, 0:128], gt[:, 0:NCMP], w2v[:, :], start=True, stop=True),
             reads=[gtr, w2v_r], writes=[psr])
        o, o_r = out_t[0]
        S.op("act", lambda e, o=o, ps=ps: e.activation(out=o[0:NCMP, :], in_=ps[0:NCMP, 0:128], func=AF.Copy), reads=[psr], writes=[o_r])
        S.op("sync", lambda e, o=o, g=g: e.dma_start(out=o_vc[:, g, :], in_=o[0:NCMP, :]), reads=[o_r], writes=[r_ovc], dma=True, dsem=o_r)
    if standalone:
        S.op("sync", None, reads=[r_okc, r_ovc])
        S.emit()
        return nc
    C.end_phase()


def build_p3b2(NT=1024, NS=2048, C=None, io=None):
    standalone = C is None
    if standalone:
        nc = bass.Bass("TRN2", target_bir_lowering=False)
        C = Ctx(nc)
        C.begin_phase("", 4, {})
    else:
        nc = C.nc
        C.begin_phase("p3b2_", 4, io or {})
    S = C.S
    scale = 192.0 ** -0.5
    NCMP = 127
    kcT_d, _ = C.dram_in("kcT", [4, 2, 96, NCMP], BF16)
    vc_d, _ = C.dram_in("vc", [NCMP, 4, 128], BF16)
    if "ex2" not in (io or {}):
        kTs_d, _ = C.dram_in("kTs", [4, 2, 96, NS], BF16)
        kTw_d, _ = C.dram_in("kTw", [4, 2, 96, NS], BF16)
        Vs_d, _ = C.dram_in("Vs", [128, 16, 4, 128], BF16)
        Vw_d, _ = C.dram_in("Vw", [128, 16, 4, 128], BF16)
    qT_d, _ = C.dram_in("qT", [32, 2, 96, NT], BF16)
    gates_d, _ = C.dram_in("gates", [96, NT], F32)
    maskc_d, _ = C.dram_in("mask_c", [NCMP, NT], F32)
    bonus_d, _ = C.dram_in("bonus", [128, 8, 32], F32)
    mAB_d, _ = C.dram_in("maskAB", [128, 256], BF16)
    mW_d, _ = C.dram_in("maskW", [128, 6, 128], BF16)
    E_d, _ = C.dram_in("Eexp", [32, 16, 128], BF16)
    ovl_d, _ = C.dram_in("ovl", [NCMP, 32], F32)
    idf_d, _ = C.dram_in("ident_f", [128, 128], F32)
    idb_d, _ = C.dram_in("ident_b", [128, 128], BF16)
    o_attn, r_oattn = C.dram_out("o_attn", [32, 128, NT], BF16)

    def const(name, shape, dt, src):
        t, r = C.tile(name, shape, dt)
        S.op("sync", lambda e: e.dma_start(out=t[:], in_=src), writes=[r], dma=True, dsem=r)
        return t, r
    kcT, kcT_r = const("kcT_t", [96, 4, 2, NCMP], BF16, kcT_d.rearrange("g pc d c -> d g pc c"))
    vc, vc_r = const("vc_t", [NCMP, 4, 128], BF16, vc_d)
    gates, gates_r = const("gates_t", [96, NT], F32, gates_d)
    maskc, maskc_r = const("maskc_t", [NCMP, NT], F32, maskc_d)
    bonus, bonus_r = const("bonus_t", [128, 8, 32], F32, bonus_d)
    mAB, mAB_r = const("mAB_t", [128, 256], BF16, mAB_d)
    mW, mW_r = const("mW_t", [128, 6, 128], BF16, mW_d)
    Ee, Ee_r = const("E_t", [32, 16, 128], BF16, E_d)
    ovl, ovl_r = const("ovl_t", [NCMP, 32], F32, ovl_d)
    idf, idf_r = const("idf_t", [128, 128], F32, idf_d)
    idb, idb_r = const("idb_t", [128, 128], BF16, idb_d)

    q_t = [C.tile(f"q{pc}", [96, 8, NT], BF16) for pc in range(2)]
    kTs, kTs_r = C.tile("kTs_t", [96, 2, NS], BF16)
    kTw, kTw_r = C.tile("kTw_t", [96, 2, NS], BF16)
    Vs, Vs_r = C.tile("Vs_t", [128, 16, 128], BF16)
    Vw, Vw_r = C.tile("Vw_t", [128, 16, 128], BF16)
    vT_t, vT_r = C.tile("vT_t", [128, NS], BF16)
    pcf_t = [C.tile(f"pcf{i}", [128, 512], F32) for i in range(2)]
    pcb_t = [C.tile(f"pcb{i}", [128, 512], BF16) for i in range(2)]
    pn_t = [C.tile(f"pn{i}", [128, 512], F32) for i in range(2)]
    rec_t = [C.tile(f"rec{i}", [128, 512], F32) for i in range(2)]
    reci = [0]
    oc, oc_r = C.tile("oc", [128, 8, 128], F32)
    os_, os_r = C.tile("os", [128, 8, 128], F32)
    ow, ow_r = C.tile("ow", [128, 8, 128], F32)
    ob_t = [C.tile(f"ob{i}", [128, 8, 128], BF16) for i in range(2)]
    impT, impT_r = C.tile("impT", [32, 128], F32)
    score, score_r = C.tile("score", [128, 32], F32)
    sc2, sc2_r = C.tile("sc2", [128, 32], F32)
    m8a, m8a_r = C.tile("m8a", [128, 8], F32)
    m8b, m8b_r = C.tile("m8b", [128, 8], F32)
    selb, selb_r = C.tile("selb", [128, 32], BF16)
    selT, selT_r = C.tile("selT", [32, 128], BF16)
    msk_t = [C.tile(f"msk{i}", [128, 128], BF16) for i in range(2)]
    mski = [0]
    pt_t = [C.tile(f"pt{i}", [128, 512], BF16) for i in range(4)]
    pti = [0]
    gd_t = [C.tile(f"gd{i}", [96, 3, 128], F32) for i in range(4)]
    t1_t = [C.tile(f"t1_{i}", [128, 128], F32) for i in range(2)]
    t2_t = [C.tile(f"t2_{i}", [128, 128], F32) for i in range(2)]
    acc = [((C.ps[4], C.psr[4]), (C.ps[5], C.psr[5])), ((C.ps[6], C.psr[6]), (C.ps[7], C.psr[7]))]
    cnt = [0]

    def do_block(g, j):
        js = slice(j * 128, (j + 1) * 128)
        pi, pir = C.ps[4], C.psr[4]
        PC = slice(0, NCMP)
        mcb = bass.AP(maskc, j * 128, [[NT, NCMP], [0, 4], [1, 128]])
        st = {}
        for hx in range(2):
            rec, recr = rec_t[reci[0] % 2]
            reci[0] += 1
            st[hx] = dict(pcf=pcf_t[hx], pcb=pcb_t[hx], pn=pn_t[hx], rec=(rec, recr))

        def c1(hx):
            ps, psr = C.psum()
            st[hx]["ps"] = (ps, psr)
            S.op("pe", lambda e: [
                e.matmul(ps[PC, :], kcT[:, g, pc, :], q_t[pc][0][:, hx * 4:(hx + 1) * 4, js], start=(pc == 0), stop=(pc == 1))
                for pc in range(2)], reads=[kcT_r, q_t[0][1], q_t[1][1]], writes=[psr])

        def c2(hx):
            ps, psr = st[hx]["ps"]
            pcf, pcfr = st[hx]["pcf"]
            S.op("act", lambda e: e.activation(out=pcf[PC, :], in_=ps[PC, :], func=AF.Exp, scale=scale), reads=[psr], writes=[pcfr])
            S.op("dve", lambda e: e.tensor_tensor(out=pcf[PC, :].rearrange("p (h t) -> p h t", h=4),
                                                  in0=pcf[PC, :].rearrange("p (h t) -> p h t", h=4), in1=mcb, op=ALU.mult),
                 reads=[pcfr, maskc_r], writes=[pcfr])

        def c3(hx):
            pcf, pcfr = st[hx]["pcf"]
            pcb, pcbr = st[hx]["pcb"]
            pd, pdr = C.psum()
            st[hx]["pd"] = (pd, pdr)
            S.op("pe", lambda e: e.matmul(pd[:, :], C.ones_f[PC, :], pcf[PC, :], start=True, stop=True), reads=[pcfr, C.r_const], writes=[pdr])
            S.op("pool", lambda e: e.tensor_copy(out=pcb[PC, :], in_=pcf[PC, :]), reads=[pcfr], writes=[pcbr])
            po, por = C.psum()
            st[hx]["po"] = (po, por)
            S.op("pe", lambda e: e.matmul(po[:, :], vc[:, g, :], pcb[PC, :], start=True, stop=True), reads=[pcbr, vc_r], writes=[por])

        def c4(hx):
            pd, pdr = st[hx]["pd"]
            po, por = st[hx]["po"]
            rec, recr = st[hx]["rec"]
            pcf, pcfr = st[hx]["pcf"]
            pn, pnr = st[hx]["pn"]
            S.op("dve", lambda e: [e.tensor_scalar(out=rec[:], in0=pd[:], scalar1=1e-30, scalar2=None, op0=ALU.max),
                                   e.reciprocal(out=rec[:], in_=rec[:])], reads=[pdr], writes=[recr])
            S.op("dve", lambda e: e.tensor_tensor(out=pn[PC, :], in0=pcf[PC, :], in1=rec[PC, :], op=ALU.mult), reads=[pcfr, recr], writes=[pnr])
            S.op("dve", lambda e: e.tensor_tensor(out=oc[:, hx * 4:(hx + 1) * 4, :].rearrange("p h t -> p (h t)"), in0=po[:], in1=rec[:],
                                                  op=ALU.mult), reads=[por, recr], writes=[oc_r])

        def c5(hx):
            pn, pnr = st[hx]["pn"]
            S.op("pe", lambda e: [e.matmul(pi[0:32, 0:128], ovl[:, :], pn[PC, hh * 128:(hh + 1) * 128],
                                           start=(hx == 0 and hh == 0), stop=(hx == 1 and hh == 3)) for hh in range(4)],
                 reads=[pnr, ovl_r], writes=[pir])
        for stage in (c1, c2, c3, c4, c5):
            for hx in range(2):
                stage(hx)
        S.op("act", lambda e: e.activation(out=impT[:], in_=pi[0:32, 0:128], func=AF.Copy), reads=[pir], writes=[impT_r])
        ps, psr = C.psum()
        S.op("pe", lambda e, ps=ps: e.matmul(ps[:, 0:32], impT[:, :], idf[0:32, 0:32], start=True, stop=True),
             reads=[impT_r, idf_r], writes=[psr])
        S.op("dve", lambda e, ps=ps, j=j: e.tensor_tensor(out=score[:], in0=ps[:, 0:32], in1=bonus[:, j, :], op=ALU.add),
             reads=[psr, bonus_r], writes=[score_r])
        S.op("dve", lambda e: e.max(out=m8a[:], in_=score[:]), reads=[score_r], writes=[m8a_r], sync_same=[score_r])
        S.op("dve", lambda e: e.match_replace(out=sc2[:], in_to_replace=m8a[:], in_values=score[:], imm_value=-3.0e38),
             reads=[m8a_r, score_r], writes=[sc2_r], sync_same=[m8a_r])
        S.op("dve", lambda e: e.max(out=m8b[:], in_=sc2[:]), reads=[sc2_r], writes=[m8b_r], sync_same=[sc2_r])
        S.op("dve", lambda e: e.tensor_scalar(out=selb[:], in0=score[:], scalar1=m8b[:, 7:8], scalar2=None, op0=ALU.is_ge),
             reads=[score_r, m8b_r], writes=[selb_r], sync_same=[m8b_r])
        ps, psr = C.psum()
        S.op("pe", lambda e, ps=ps: e.matmul(ps[0:32, 0:128], selb[:, :], idb[:, :], start=True, stop=True),
             reads=[selb_r, idb_r], writes=[psr])
        S.op("act", lambda e, ps=ps: e.activation(out=selT[:], in_=ps[0:32, 0:128], func=AF.Copy), reads=[psr], writes=[selT_r])
        kts = list(range(2 * j + 2))
        attn_sel(C, S, j, kts, kTs, kTs_r, Vs, Vs_r, os_, os_r, q_t, pt_t, pti, acc, rec_t, reci, scale,
                 lambda kt: sel_mask(C, S, kt, j, Ee, Ee_r, selT, selT_r, mAB, mAB_r, msk_t, mski))
        ktw = [(2 * j - 4 + r, r) for r in range(6) if 2 * j - 4 + r >= 0]
        attn_sel(C, S, j, [k for (k, r) in ktw], kTw, kTw_r, Vw, Vw_r, ow, ow_r, q_t, pt_t, pti, acc, rec_t, reci, scale,
                 lambda kt, j=j: (mW[:, kt - (2 * j - 4), :], mW_r))
        ob, obr = ob_t[cnt[0] % 2]
        cnt[0] += 1
        pgs = {}

        def g1(hh):
            h = 8 * g + hh
            gd, gdr = gd_t[hh % len(gd_t)]
            S.op("pool", lambda e: [e.tensor_scalar(out=gd[:, br, :], in0=gates[:, js], scalar1=idf[0:96, h * 3 + br:h * 3 + br + 1],
                                                    scalar2=None, op0=ALU.mult) for br in range(3)],
                 reads=[gates_r, idf_r], writes=[gdr])
            pg, pgr = C.psum()
            S.op("pe", lambda e: [e.matmul(pg[:, br * 128:(br + 1) * 128], C.ones_f[0:96, :], gd[:, br, :], start=True, stop=True)
                                  for br in range(3)], reads=[gdr, C.r_const], writes=[pgr])
            pgs[hh] = (pg, pgr)

        def g2(hh):
            pg, pgr = pgs.pop(hh)
            t1, t1r = t1_t[hh % 2]
            t2, t2r = t2_t[hh % 2]

            def comb(e):
                return [e.tensor_tensor(out=t1[:], in0=pg[:, 0:128], in1=oc[:, hh, :], op=ALU.mult),
                        e.tensor_tensor(out=t2[:], in0=pg[:, 128:256], in1=os_[:, hh, :], op=ALU.mult),
                        e.tensor_tensor(out=t1[:], in0=t1[:], in1=t2[:], op=ALU.add),
                        e.tensor_tensor(out=t2[:], in0=pg[:, 256:384], in1=ow[:, hh, :], op=ALU.mult),
                        e.tensor_tensor(out=ob[:, hh, :], in0=t1[:], in1=t2[:], op=ALU.add)]
            S.op("dve", comb, reads=[pgr, oc_r, os_r, ow_r], writes=[t1r, t2r, obr])
        for n in range(8 + 2):
            if n < 8:
                g1(n)
            if n - 2 >= 0:
                g2(n - 2)
        S.op("sync", lambda e, ob=ob, g=g, js=js: e.dma_start(out=o_attn[8 * g:8 * g + 8, :, js].rearrange("h p t -> p h t"), in_=ob[:]),
             reads=[obr], writes=[r_oattn], dma=True, dsem=obr)

    for g in range(4):
        for pc in range(2):
            qt, qr = q_t[pc]
            S.op("sync", lambda e, qt=qt, pc=pc, g=g: e.dma_start(out=qt[:], in_=qT_d[8 * g:8 * g + 8, pc].rearrange("h d t -> d h t")),
                 writes=[qr], dma=True, dsem=qr)
        if "ex2" in C.io:
            exk, exv, r_ex2 = C.io["ex2"]
            for (kt_t, kt_r, bg0) in ((kTs, kTs_r, 4), (kTw, kTw_r, 8)):
                S.op("sync", lambda e, kt_t=kt_t, g=g, bg0=bg0: [e.dma_start(
                    out=kt_t[:, pc, :].rearrange("p (j r t) -> p j r t", r=2, t=128)[:, :, r, :],
                    in_=exk(r, (bg0 + g) * 2 + pc).rearrange("p (j t) -> p j t", t=128))
                    for r in range(2) for pc in range(2)], reads=[r_ex2], writes=[kt_r], dma=True, dsem=kt_r, ndma=4)
            for (V_t, V_r, bg0) in ((Vs, Vs_r, 4), (Vw, Vw_r, 8)):
                S.op("sync", lambda e, g=g, bg0=bg0: [e.dma_start(
                    out=vT_t[:, :].rearrange("p (j r t) -> p j r t", r=2, t=128)[:, :, r, :],
                    in_=exv(r, bg0 + g).rearrange("p (j t) -> p j t", t=128))
                    for r in range(2)], reads=[r_ex2], writes=[vT_r], dma=True, dsem=vT_r, ndma=2)
                for k4 in range(4):
                    ps, psr = C.psum()
                    S.op("pe", lambda e, ps=ps, k4=k4: [e.matmul(ps[:, i * 128:(i + 1) * 128], vT_t[:, (k4 * 4 + i) * 128:(k4 * 4 + i + 1) * 128],
                                                                 idb[:, :], start=True, stop=True) for i in range(4)],
                         reads=[vT_r, idb_r], writes=[psr])
                    S.op("act", lambda e, ps=ps, k4=k4, V_t=V_t: e.activation(out=V_t[:, k4 * 4:(k4 + 1) * 4, :].rearrange("p k d -> p (k d)"),
                                                                             in_=ps[:], func=AF.Copy), reads=[psr], writes=[V_r])
        else:
            S.op("sync", lambda e, g=g: e.dma_start(out=kTs[:], in_=kTs_d[g].rearrange("pc d t -> d pc t")), writes=[kTs_r], dma=True, dsem=kTs_r)
            S.op("sync", lambda e, g=g: e.dma_start(out=kTw[:], in_=kTw_d[g].rearrange("pc d t -> d pc t")), writes=[kTw_r], dma=True, dsem=kTw_r)
            S.op("sync", lambda e, g=g: e.dma_start(out=Vs[:], in_=Vs_d[:, :, g, :]), writes=[Vs_r], dma=True, dsem=Vs_r)
            S.op("sync", lambda e, g=g: e.dma_start(out=Vw[:], in_=Vw_d[:, :, g, :]), writes=[Vw_r], dma=True, dsem=Vw_r)
        for j in range(8):
            do_block(g, j)
    if standalone:
        S.op("sync", None, reads=[r_oattn])
        S.emit()
        return nc
    C.end_phase()


def sel_mask(C, S, kt, j, Ee, Ee_r, selT, selT_r, mAB, mAB_r, msk_t, mski):
    ps, psr = C.psum()
    S.op("pe", lambda e: e.matmul(ps[:, 0:128], Ee[:, kt, :], selT[:, :], start=True, stop=True), reads=[Ee_r, selT_r], writes=[psr])
    m, mr = msk_t[mski[0] % 2]
    mski[0] += 1
    if kt >= 2 * j:
        mc = (kt % 2) * 128
        S.op("dve", lambda e: e.tensor_tensor(out=m[:], in0=ps[:, 0:128], in1=mAB[:, mc:mc + 128], op=ALU.mult), reads=[psr, mAB_r], writes=[mr])
    else:
        S.op("dve", lambda e: e.tensor_copy(out=m[:], in_=ps[:, 0:128]), reads=[psr], writes=[mr])
    return m[:], mr


def attn_sel(C, S, j, kts, kT, kT_r, V, V_r, out_t, out_r, q_t, pt_t, pti, acc, rec_t, reci, scale, mask_fn, L=2):
    js = slice(j * 128, (j + 1) * 128)
    last = len(kts) - 1
    items = [(i, kt, hx) for i, kt in enumerate(kts) for hx in range(2)]
    masks = {}
    state = {}

    def stageA(n):
        i, kt, hx = items[n]
        if kt not in masks:
            m_ap, m_r = mask_fn(kt)
            masks[kt] = (bass.AP(m_ap.tensor, m_ap.offset, [list(m_ap.ap[0]), [0, 4], [1, 128]]), m_r)
        mb, m_r = masks[kt]
        ks = slice(kt * 128, (kt + 1) * 128)
        ps, psr = C.psum()
        S.op("pe", lambda e: [
            e.matmul(ps[:, :], kT[:, pc, ks], q_t[pc][0][:, hx * 4:(hx + 1) * 4, js], start=(pc == 0), stop=(pc == 1))
            for pc in range(2)], reads=[kT_r, q_t[0][1], q_t[1][1]], writes=[psr])
        pt, ptr = pt_t[pti[0] % len(pt_t)]
        pti[0] += 1
        S.op("act", lambda e: e.activation(out=pt[:], in_=ps[:], func=AF.Exp, scale=scale), reads=[psr], writes=[ptr])
        S.op("dve", lambda e: e.tensor_tensor(out=pt[:].rearrange("p (h t) -> p h t", h=4),
                                              in0=pt[:].rearrange("p (h t) -> p h t", h=4), in1=mb, op=ALU.mult),
             reads=[ptr, m_r], writes=[ptr])
        state[n] = (pt, ptr)

    def stageB(n):
        i, kt, hx = items[n]
        pt, ptr = state.pop(n)
        (po, por), (pd, pdr) = acc[hx]
        S.op("pe", lambda e: [
            e.matmul(po[:, :], V[:, kt, :], pt[:, :], start=(i == 0), stop=(i == last)),
            e.matmul(pd[:, :], C.ones_b[:], pt[:, :], start=(i == 0), stop=(i == last))],
            reads=[ptr, V_r, C.r_const], writes=[por, pdr])

    for n in range(len(items) + L):
        if n < len(items):
            stageA(n)
        if n - L >= 0:
            stageB(n - L)
    for hx in range(2):
        (po, por), (pd, pdr) = acc[hx]
        rec, recr = rec_t[reci[0] % 2]
        reci[0] += 1
        S.op("dve", lambda e, rec=rec, pd=pd: e.reciprocal(out=rec[:], in_=pd[:]), reads=[pdr], writes=[recr])
        S.op("dve", lambda e, rec=rec, po=po, hx=hx: e.tensor_tensor(
            out=out_t[:, hx * 4:(hx + 1) * 4, :].rearrange("p h t -> p (h t)"), in0=po[:], in1=rec[:], op=ALU.mult),
            reads=[por, recr], writes=[out_r])


def build_oproj(NT=1024, C=None, io=None):
    global WT_ELEMS
    WT_ELEMS = 16384
    standalone = C is None
    if standalone:
        nc = bass.Bass("TRN2", target_bir_lowering=False)
        C = Ctx(nc)
        C.begin_phase("", 8, {})
    else:
        nc = C.nc
        C.begin_phase("opj_", 8, io or {})
    S = C.S
    hT, _ = C.dram_in("hT", [32, 128, NT], F32)
    oT_d, _ = C.dram_in("oT", [32, 128, NT], BF16)
    Wo, _ = C.dram_in("Wo", [4096, 4096], F32)
    o_h, r_oh = C.dram_out("o_h", [32, 128, NT], F32)
    oT, oT_r = C.tile("oT_t", [128, 32, NT], BF16)
    S.op("sync", lambda e: [e.dma_start(out=oT[:, i * 8:(i + 1) * 8, :], in_=oT_d[i * 8:(i + 1) * 8].rearrange("c p t -> p c t")) for i in range(4)],
         writes=[oT_r], dma=True, dsem=oT_r, ndma=4)
    wtiles = [C.tile(f"w{i}", [128, WT_ELEMS], BF16) for i in range(2)]
    xs = [C.tile(f"xs{i}", [128, 512], F32) for i in range(4)]
    xsi = [0]

    def epio(ci, hf, ps, psr):
        t, r = xs[xsi[0] % 4]
        xsi[0] += 1
        hs = slice(hf * 512, (hf + 1) * 512)
        S.op("sync", lambda e: e.dma_start(out=t[:], in_=hT[ci, :, hs]), writes=[r], dma=True, dsem=r)
        S.op("dve", lambda e: e.tensor_tensor(out=t[:], in0=ps[:], in1=t[:], op=ALU.add), reads=[psr, r], writes=[r])
        S.op("sync", lambda e: e.dma_start(out=o_h[ci, :, hs], in_=t[:]), reads=[r], writes=[r_oh], dma=True, dsem=r)
    linear_T(C, Wo, 32, lambda kc, hf: (oT[:, kc, hf * 512:(hf + 1) * 512], oT_r), [(i * 128, 128) for i in range(32)],
             [0, 1], epio, wtiles)
    if standalone:
        S.op("sync", None, reads=[r_oh])
        S.emit()
        return nc
    C.end_phase()


PAIRS = [[0, 1], [2, 3], [4, 5], [6, 7]]


def build_fused(NT=1024, NS=2048, upto=None, ncores=8):
    nc = bass.Bass("TRN2", target_bir_lowering=False, num_devices=ncores)
    pairs = PAIRS[:ncores // 2]
    C = Ctx(nc)
    C.fused = True
    S = C.S
    xT = C.dram_in("xT", [32, 128, NT], F32)
    pos = C.dram_in("pos", [1, NT], I32)
    qn_s = C.dram_tmp("s_qn", [32, 128, NT], BF16)
    qr_s = C.dram_tmp("s_qr", [16, 128, NT], BF16)
    ex1s, r_ex1s = C.dram_tmp("ex1_src", [640, NT], BF16)
    ex1d = C.dram_tmp("ex1_dst", [1280, NT], BF16)
    def T(name, key, shape=None, dt=F32):
        shape = shape or [32, 128, NT]
        if upto == key:
            return C.dram_out("o_fin", shape, dt)
        return C.dram_tmp(name, shape, dt)
    hA = T("s_hA", "p2a")
    h1 = T("s_h1", "f0")
    hB = T("s_hB", "opj")
    h2 = C.dram_tmp("s_h2", [32, 128, NT], F32)
    dbg = C.dram_tmp("s_dbg", [32, 128, NT], BF16)
    EX2 = [960, 960, 896, 1024]
    ex2s = [C.dram_tmp(f"ex2_src{i}", [n, NT], BF16)[0] for i, n in enumerate(EX2)]
    ex2d = [C.dram_tmp(f"ex2_dst{i}", [2 * n, NT], BF16)[0] for i, n in enumerate(EX2)]
    r_ex2s = S.res("ex2_src", acc=True)
    r_ex2d = S.res("ex2_dst", acc=True)

    def kloc(kp):
        return (0, kp * 96) if kp < 10 else ((1, (kp - 10) * 96) if kp < 20 else (2, (kp - 20) * 96))

    def vloc(bg):
        return (2, 384 + bg * 128) if bg < 4 else (3, (bg - 4) * 128)

    def okf(bg, pc):
        t, o = kloc(bg * 2 + pc)
        return ex2s[t][o:o + 96, :]

    def ovf(bg):
        t, o = vloc(bg)
        return ex2s[t][o:o + 128, :]

    def exk(r, kp):
        t, o = kloc(kp)
        return ex2d[t][r * EX2[t] + o:r * EX2[t] + o + 96, :]

    def exv(r, bg):
        t, o = vloc(bg)
        return ex2d[t][r * EX2[t] + o:r * EX2[t] + o + 128, :]
    q_s = T("s_q", "p3a", [32, 2, 96, NT], BF16)
    g_s = C.dram_tmp("s_g", [96, NT], F32)
    kc_s = T("s_kc", "p3b1", [4, 2, 96, 127], BF16)
    vc_s = C.dram_tmp("s_vc", [127, 4, 128], BF16)
    at_s = T("s_attn", "p3b2", [32, 128, NT], BF16)
    o_fin = C.dram_out("o_fin", [32, 128, NT], F32) if upto is None else None

    def fin(key, r):
        if upto == key:
            S.op("sync", None, reads=[r[1]])
            S.emit()
            return True
        return False

    build_p1(NT, C=C, io={"xT": xT, "pos": pos, "o_qn": qn_s, "o_qr": qr_s,
                          "o_ckv": (ex1s[0:512, :].rearrange("(c p) t -> c p t", p=128), r_ex1s),
                          "o_kr": (ex1s[512:640, :], r_ex1s)})
    S.op("pool", lambda e: e.collective_compute("AllGather", ALU.bypass, replica_groups=pairs, ins=[ex1s], outs=[ex1d[0]]),
         reads=[r_ex1s], writes=[ex1d[1]], dma=True, dsem=ex1d[1], dma_inc=1)
    build_p2a(NT, NS, C=C, io={"xT": xT, "ex1": ex1d, "qn": qn_s, "qr": qr_s, "o_hA": hA, "o_attn": dbg})
    if fin("p2a", hA):
        return nc
    build_ffn(False, NT, C=C, io={"hT": hA, "o_h": h1}, tag="f0")
    if fin("f0", h1):
        return nc
    build_p3a(NT, C=C, io={"hT": h1, "pos": pos,
                           "o_kv_fn": (okf, ovf, r_ex2s),
                           "o_q": q_s, "o_g": g_s})
    if fin("p3a", q_s):
        return nc
    S.op("pool", lambda e: [e.collective_compute("AllGather", ALU.bypass, replica_groups=pairs, ins=[ex2s[i]], outs=[ex2d[i]])
                            for i in range(4)], reads=[r_ex2s], writes=[r_ex2d], dma=True, dsem=r_ex2d, dma_inc=1, ndma=4)
    build_p3b1(NS, C=C, io={"ex2": (exk, exv, r_ex2d), "o_kcT": kc_s, "o_vc": vc_s})
    if fin("p3b1", kc_s):
        return nc
    build_p3b2(NT, NS, C=C, io={"ex2": (exk, exv, r_ex2d), "kcT": kc_s, "vc": vc_s, "qT": q_s, "gates": g_s, "o_attn": at_s})
    if fin("p3b2", at_s):
        return nc
    build_oproj(NT, C=C, io={"hT": h1, "oT": at_s, "o_h": hB})
    if fin("opj", hB):
        return nc
    build_ffn(True, NT, C=C, io={"hT": hB, "o_h": h2, "o_fin": o_fin}, tag="f1")
    S.op("sync", None, reads=[o_fin[1]])
    S.emit()
    return nc


THETA=10000.0
def tok_idx(hf):
    return np.concatenate([np.arange((2*j+hf)*128,(2*j+hf)*128+128) for j in range(8)])
def colT(g):
    return np.ascontiguousarray(g.reshape(-1,128).T)
def featmajor(x):
    T,D=x.shape
    return np.ascontiguousarray(x.T.reshape(D//128,128,T))
def p1_weights(I):
    w=I["a_w_in"][0]
    x1=w[:,2048:2080]; x2=w[:,2080:2112]
    W1=np.concatenate([w[:,:2048], x1,x2,x1,x2, x2,x1,x2,x1],axis=1)
    uq=I["a_w_uq"][0]
    cols=[]
    for h in range(32): cols.append(uq[:,h*192:h*192+128])
    for Pp in range(16):
        h0=2*Pp; h1=2*Pp+1
        a1=uq[:,h0*192+128:h0*192+160]; a2=uq[:,h0*192+160:h0*192+192]
        b1=uq[:,h1*192+128:h1*192+160]; b2=uq[:,h1*192+160:h1*192+192]
        cols += [a1,a2,b1,b2, a2,a1,b2,b1]
    Wq=np.concatenate(cols,axis=1)
    inv=(THETA ** (-np.arange(0,64,2,dtype=np.float32)/np.float32(64))).astype(np.float32)
    inv32=np.tile(inv,4).reshape(128,1).astype(np.float32)
    sgn=np.tile(np.concatenate([-np.ones(32),np.ones(32)]),2).reshape(128,1).astype(np.float32)
    return dict(W1=np.ascontiguousarray(W1),Wq=np.ascontiguousarray(Wq),gA=colT(I["a_norm"][0]),gQ=colT(I["a_q_norm"][0]),gKV=colT(I["a_kv_norm"][0]),inv32=inv32,sgn64=sgn)
def p1_core_inputs(I,b,hf,shared):
    idx=tok_idx(hf)
    d=dict(shared)
    d["xT"]=featmajor(I["x"][b][idx])
    d["pos"]=np.ascontiguousarray(I["positions"][b][idx].reshape(1,-1))
    return d
import ml_dtypes
BF=ml_dtypes.bfloat16
def seq_order(a0,a1,axis=-1):
    a0=np.moveaxis(a0,axis,-1); a1=np.moveaxis(a1,axis,-1)
    sh=a0.shape[:-1]
    b0=a0.reshape(sh+(8,128)); b1=a1.reshape(sh+(8,128))
    full=np.stack([b0,b1],axis=-2).reshape(sh+(2048,))
    return np.ascontiguousarray(np.moveaxis(full,-1,axis))
def p2a_weights(I):
    kv=I["a_w_ukv"][0].reshape(512,32,256)
    Wk=np.ascontiguousarray(kv[:,:,:128].reshape(512,4096)); Wv=np.ascontiguousarray(kv[:,:,128:].reshape(512,4096))
    return dict(Wk=Wk,Wv=Wv,Wo=np.ascontiguousarray(I["a_w_o"][0]))
def maskAB(hf):
    tri=(np.arange(128)[:,None]<=np.arange(128)[None,:]).astype(np.float32)
    if hf==0: m=np.concatenate([tri,np.zeros((128,128),np.float32)],axis=1)
    else: m=np.concatenate([np.ones((128,128),np.float32),tri],axis=1)
    return m.astype(BF)

def ffn_weights(I,layer,final=False):
    w=I["f_w_in"][layer]; DFF=11008
    a=w[:,:DFF].reshape(4096,86,128); b=w[:,DFF:].reshape(4096,86,128)
    Win=np.ascontiguousarray(np.stack([a,b],axis=2).reshape(4096,2*DFF))
    d=dict(Win=Win,Wout=np.ascontiguousarray(I["f_w_out"][layer]),gF=colT(I["f_norm"][layer]))
    if final: d["gN"]=colT(I["final_norm"])
    return d
def p3a_weights(I):
    inv=(THETA ** (-np.arange(0,192,2,dtype=np.float32)/np.float32(192))).astype(np.float32)
    inv96=np.zeros((128,1),np.float32); inv96[:96,0]=inv
    return dict(Ws=np.ascontiguousarray(I["s_w_kv"]),Wb=np.ascontiguousarray(I["b_w_in"][0]),gS=colT(I["s_norm"]),gB=colT(I["b_norm"][0]),inv96=inv96)
def p3b1_weights(I):
    pek=I["s_cmp_pe_k"]
    pekT=np.ascontiguousarray(pek.reshape(32,2,96).transpose(2,0,1).reshape(96,64))
    pevT=np.ascontiguousarray(I["s_cmp_pe_v"].T)
    return dict(w1k=np.ascontiguousarray(I["s_cmp_w1_k"]),w2k=np.ascontiguousarray(I["s_cmp_w2_k"]),pekT=pekT,
                w1v=np.ascontiguousarray(I["s_cmp_w1_v"]),w2v=np.ascontiguousarray(I["s_cmp_w2_v"]),pevT=pevT)
def p3b2_consts(hf):
    t=tok_idx(hf)
    c=np.arange(127)
    mask_c=((c[:,None]*16+31)<=t[None,:]).astype(np.float32)
    bonus=np.zeros((128,8,32),np.float32)
    jj=np.arange(32)
    for j in range(8):
        tt=(2*j+hf)*128+np.arange(128)
        cur=(tt//64)[:,None]
        valid=(jj[None,:]*64)<=tt[:,None]
        forced=valid&((jj[None,:]==0)|(jj[None,:]==cur)|(jj[None,:]==cur-1))
        bonus[:,j,:]=np.where(valid,np.where(forced,1e4,0.0),-1e30)
    kl=np.arange(128)[:,None]; ql=np.arange(128)[None,:]
    su=(kl>ql).astype(np.float32); tri=(kl<=ql).astype(np.float32); one=np.ones((128,128),np.float32); zero=np.zeros((128,128),np.float32)
    mw=[su,one,one,one,tri,zero] if hf==0 else [zero,su,one,one,one,tri]
    maskW=np.stack(mw,axis=1).astype(BF)
    E=np.zeros((32,16,128),np.float32)
    for kt in range(16):
        for key in range(128):
            E[2*kt+key//64,kt,key]=1
    c_start=np.arange(127)*16; j_start=np.arange(32)*64
    ovl=((c_start[:,None]<j_start[None,:]+64)&(c_start[:,None]+32>j_start[None,:])).astype(np.float32)
    return dict(mask_c=mask_c,bonus=bonus,maskAB=maskAB(hf),maskW=maskW,Eexp=E.astype(BF),ovl=ovl,
                ident_f=np.eye(128,dtype=np.float32),ident_b=np.eye(128,dtype=np.float32).astype(BF))


def kernel(**I):
    I = {k: np.asarray(v) for k, v in I.items()}
    NCORE = 8
    cores = [(c // 2, c % 2) for c in range(NCORE)]
    shared = {}
    for pref, d in (("p1_", p1_weights(I)), ("p2a_", p2a_weights(I)), ("f0_", ffn_weights(I, 0, False)),
                    ("p3a_", p3a_weights(I)), ("p3b1_", p3b1_weights(I)), ("opj_", {"Wo": np.ascontiguousarray(I["b_w_o"][0])}),
                    ("f1_", ffn_weights(I, 1, True))):
        for k, v in d.items():
            shared[pref + k] = v
    maps = []
    for (b, hf) in cores:
        d = dict(shared)
        idx = tok_idx(hf)
        d["xT"] = featmajor(I["x"][b][idx])
        d["pos"] = np.ascontiguousarray(I["positions"][b][idx].reshape(1, -1))
        d["p2a_maskAB"] = maskAB(hf)
        for k, v in p3b2_consts(hf).items():
            d["p3b2_" + k] = v
        maps.append(d)
    nc = build_fused()
    res = run_bass_kernel_spmd(nc, maps, core_ids=list(range(NCORE))).results
    out = np.zeros((4, 2048, 4096), np.float32)
    for c, (b, hf) in enumerate(cores):
        out[b, tok_idx(hf)] = np.asarray(res[c]["o_fin"]).transpose(2, 0, 1).reshape(1024, 4096)
    return out
```

```python
import numpy as np
import concourse.bass as bass
import concourse.mybir as mybir
from concourse.bass_utils import run_bass_kernel_spmd

F32 = mybir.dt.float32
BF16 = mybir.dt.bfloat16
I32 = mybir.dt.int32
AF = mybir.ActivationFunctionType
ALU = mybir.AluOpType
AX = mybir.AxisListType

SAME_ENG_SYNC = False
SEM_LIMIT = 20000


class Res:
    __slots__ = ("name", "ws", "r", "dsem", "dcount", "const", "acc", "last_dma")

    def __init__(self, name, const=False, acc=False):
        self.name = name
        self.ws = []
        self.acc = acc
        self.r = []
        self.dsem = None
        self.dcount = 0
        self.const = const
        self.last_dma = None


class Op:
    __slots__ = ("eng", "fn", "deps", "dma", "sem", "val", "signal", "ndma", "name", "dma_inc")


class Sched:
    ENGS = ("sync", "act", "dve", "pool", "pe")

    def __init__(self, nc):
        self.nc = nc
        self.q = {e: [] for e in self.ENGS}
        self.nsem = 0
        self.allres = []
        self.sem_pool = []
        self.live_dsem = []

    def res(self, name, const=False, acc=False):
        r = Res(name, const, acc)
        self.allres.append(r)
        return r

    def _newsem(self, name):
        self.nsem += 1
        return self.nc.alloc_semaphore(f"s{self.nsem}_{name}")

    def op(self, eng, fn, reads=(), writes=(), dma=False, dsem=None, ndma=1, name="", sync_same=(), dma_inc=16):
        op = Op()
        op.eng = eng
        op.fn = fn
        op.dma = dma
        op.signal = False
        op.sem = None
        op.val = 0
        op.ndma = ndma
        op.name = name
        op.dma_inc = dma_inc
        deps = []
        seen = set()

        def add(d, force=False):
            if d is None or id(d) in seen:
                return
            seen.add(id(d))
            if (not d.dma) and d.eng == eng and not SAME_ENG_SYNC and not force:
                return
            deps.append(d)

        for r in sync_same:
            for ww in r.ws:
                add(ww, True)
        for r in reads:
            for ww in r.ws:
                add(ww)
        for w in writes:
            if not w.acc:
                for ww in w.ws:
                    add(ww)
            for rr in w.r:
                add(rr)
        op.deps = deps
        for w in writes:
            if w.acc:
                w.ws.append(op)
            else:
                w.ws = [op]
            w.r = []
        for r in reads:
            if not r.const:
                r.r.append(op)
        if dma:
            if dsem is None:
                raise ValueError("dma needs dsem")
            if dsem.dsem is None:
                if self.sem_pool:
                    dsem.dsem, dsem.dcount = self.sem_pool.pop()
                else:
                    dsem.dsem = self._newsem("d_" + dsem.name)
                self.live_dsem.append(dsem)
            dsem.last_dma = None
            dsem.dcount += dma_inc * ndma
            op.sem = dsem.dsem
            op.val = dsem.dcount
            op.signal = True
            dsem.last_dma = op
        self.q[eng].append(op)
        return op

    def barrier(self):
        lasts = []
        for e in self.ENGS:
            for op in reversed(self.q[e]):
                if op.fn is not None and not op.dma:
                    lasts.append(op)
                    break
        dmal = [r.last_dma for r in self.live_dsem if r.last_dma is not None]
        for e in self.ENGS:
            op = Op()
            op.eng = e
            op.fn = None
            op.dma = False
            op.signal = False
            op.sem = None
            op.val = 0
            op.ndma = 0
            op.name = "barrier"
            op.dma_inc = 16
            op.deps = [d for d in lasts if d.eng != e] + dmal
            self.q[e].append(op)
        for r in self.live_dsem:
            self.sem_pool.append((r.dsem, r.dcount))
            r.dsem = None
            r.last_dma = None
        self.live_dsem = []
        for r in self.allres:
            r.ws = []
            r.r = []

    def emit(self):
        nc = self.nc
        for e in self.ENGS:
            for op in self.q[e]:
                for d in op.deps:
                    d.signal = True
        for e in self.ENGS:
            cur = None
            cnt = 0
            for op in self.q[e]:
                if op.dma or not op.signal or op.fn is None:
                    continue
                if cur is None or cnt >= SEM_LIMIT:
                    cur = self._newsem("e_" + e)
                    cnt = 0
                cnt += 1
                op.sem = cur
                op.val = cnt

        def run(e):
            def body(eng):
                waited = {}
                for op in self.q[e]:
                    for d in op.deps:
                        if d.sem is None:
                            continue
                        k = id(d.sem)
                        if waited.get(k, 0) < d.val:
                            eng.wait_ge(d.sem, d.val)
                            waited[k] = d.val
                    if op.fn is None:
                        continue
                    ins = op.fn(eng)
                    if ins is None:
                        continue
                    if not isinstance(ins, (list, tuple)):
                        ins = [ins]
                    if op.dma:
                        assert len(ins) == op.ndma, (op.name, len(ins), op.ndma)
                        for i in ins:
                            i.then_inc(op.sem, op.dma_inc)
                    elif op.signal:
                        ins[-1].then_inc(op.sem, 1)
            return body

        with nc.Block() as block:
            block.sync(run("sync"))
            block.scalar(run("act"))
            block.vector(run("dve"))
            block.gpsimd(run("pool"))
            block.tensor(run("pe"))
        print("nsem", self.nsem, {e: len(self.q[e]) for e in self.ENGS}, flush=True)


def mkap(t_ap, offset_elems, pattern):
    return bass.AP(t_ap.tensor, offset_elems, pattern)


import math

EPS = 1e-6
TWO_PI = 2.0 * math.pi


DT_SIZE = {F32: 4, BF16: 2, I32: 4}
SB_BASE = 16384 + 512
SB_TOP = 229344


class Ctx:
    def __init__(self, nc, n_rr=8):
        self.nc = nc
        self.S = Sched(nc)
        self.ps = [nc.alloc_psum_tensor(f"ps{i}", [128, 512], F32) for i in range(8)]
        self.psr = [self.S.res(f"ps{i}") for i in range(8)]
        self.psi = 0
        self.n_rr = n_rr
        self.uid = 0
        self.cur = SB_BASE
        self.prefix = ""
        self.io = {}
        self.fused = False
        ones_f, r1 = self.tile("ones_f", [128, 128], F32)
        ones_b, r2 = self.tile("ones_b", [128, 128], BF16)
        eps_t, r3 = self.tile("eps_t", [128, 1], F32)
        self.ones_f, self.ones_b, self.eps_t = ones_f, ones_b, eps_t
        self.r_const = self.S.res("consts", const=False)
        S = self.S
        S.op("dve", lambda e: [e.memset(ones_f[:], 1.0), e.memset(ones_b[:], 1.0), e.memset(eps_t[:], EPS)],
             writes=[self.r_const])
        self.phase_base = self.cur

    def begin_phase(self, prefix, n_rr, io):
        self.prefix = prefix
        self.n_rr = n_rr
        self.psi = 0
        self.io = io
        self.cur = self.phase_base
        for a in ("rn_tiles", "rt_extra"):
            if hasattr(self, a):
                delattr(self, a)

    def end_phase(self):
        self.S.barrier()

    def psum(self):
        i = self.psi
        self.psi = (i + 1) % self.n_rr
        return self.ps[i], self.psr[i]

    def tile(self, name, shape, dt):
        nb = DT_SIZE[dt]
        for d in shape[1:]:
            nb *= d
        nb = (nb + 63) // 64 * 64
        t = self.nc.alloc_sbuf_tensor_at(self.prefix + name, list(shape), dt, offset=self.cur)
        self.cur += nb
        assert self.cur <= SB_TOP, ("SBUF overflow", self.prefix + name, self.cur)
        return t, self.S.res(self.prefix + name)

    def dram_in(self, name, shape, dt):
        if name in self.io:
            return self.io[name]
        return (self.nc.dram_tensor(self.prefix + name, list(shape), dt, kind="ExternalInput").ap(),
                self.S.res(self.prefix + name, const=True))

    def dram_out(self, name, shape, dt):
        if name in self.io:
            return self.io[name]
        return (self.nc.dram_tensor(self.prefix + name, list(shape), dt, kind="ExternalOutput").ap(),
                self.S.res(self.prefix + name, acc=True))

    def dram_tmp(self, name, shape, dt):
        return self.nc.dram_tensor(name, list(shape), dt, kind="Internal").ap(), self.S.res(name, acc=True)


WT_ELEMS = 16384


def wview(wt, ncols, kc0, kc1, c0, c1):
    rs = wt[:].ap[0][0]
    return bass.AP(wt, kc0 * ncols + c0, [[rs, 128], [ncols, kc1 - kc0], [1, c1 - c0]])


def wslice(wt, ncols, kc, c0, c1, p0=0, p1=128):
    rs = wt[:].ap[0][0]
    return bass.AP(wt, p0 * rs + kc * ncols + c0, [[rs, p1 - p0], [1, c1 - c0]])


def wblock_load(C, wt, wres, W, KC, c0, ncols, kstep=8):
    Wv = W[:, c0:c0 + ncols].rearrange("(kc p) n -> p kc n", p=128)
    pieces = [(k0, min(KC, k0 + kstep)) for k0 in range(0, KC, kstep)]

    def fn(e):
        return [e.dma_start(out=wview(wt, ncols, k0, k1, 0, ncols), in_=Wv[:, k0:k1, :]) for (k0, k1) in pieces]
    C.S.op("pool", fn, writes=[wres], dma=True, dsem=wres, ndma=len(pieces), name="wload")


def linear_T(C, W, KC, rhs_fn, chunks, halves, epi, wtiles, colblock=None, wstate=None):
    S = C.S
    if wstate is None:
        wstate = [0]
    if colblock is None:
        colblock = min(512, (wtiles[0][0][:].ap[0][0] // KC) // 128 * 128)
    blocks = []
    cur = None
    for ci, (c0, M) in enumerate(chunks):
        if cur is not None and cur["c1"] == c0 and (c0 + M - cur["c0"]) <= colblock:
            cur["items"].append((ci, c0, M))
            cur["c1"] = c0 + M
        else:
            cur = {"c0": c0, "c1": c0 + M, "items": [(ci, c0, M)]}
            blocks.append(cur)
    for b in blocks:
        wt, wres = wtiles[wstate[0] % len(wtiles)]
        wstate[0] += 1
        ncols = b["c1"] - b["c0"]
        wblock_load(C, wt, wres, W, KC, b["c0"], ncols)
        for (ci, c0, M) in b["items"]:
            off = c0 - b["c0"]
            for hf in halves:
                ps, psr = C.psum()
                rhs_list = [rhs_fn(kc, hf) for kc in range(KC)]
                rres = []
                for (_, r) in rhs_list:
                    if r not in rres:
                        rres.append(r)

                def mm(e, ps=ps, wt=wt, off=off, M=M, rhs_list=rhs_list, ncols=ncols):
                    return [e.matmul(ps[0:M, :], wslice(wt, ncols, kc, off, off + M), rhs_list[kc][0],
                                     start=(kc == 0), stop=(kc == KC - 1)) for kc in range(KC)]
                S.op("pe", mm, reads=[wres] + rres, writes=[psr], name="lin_mm")
                epi(ci, hf, ps, psr)


def linear_T_pairs(C, W, KC, rhs_fn, chunks, halves, epi, wtiles, wstate, group=2, colblock=None):
    S = C.S
    if colblock is None:
        colblock = min(512, (wtiles[0][0][:].ap[0][0] // KC) // 128 * 128)
    groups = [list(range(i, min(i + group, len(chunks)))) for i in range(0, len(chunks), group)]
    blocks = []
    cur = None
    for grp in groups:
        c0 = chunks[grp[0]][0]
        c1 = chunks[grp[-1]][0] + chunks[grp[-1]][1]
        contiguous = all(chunks[grp[i]][0] + chunks[grp[i]][1] == chunks[grp[i + 1]][0] for i in range(len(grp) - 1))
        assert contiguous
        if cur is not None and cur["c1"] == c0 and (c1 - cur["c0"]) <= colblock:
            cur["groups"].append(grp)
            cur["c1"] = c1
        else:
            cur = {"c0": c0, "c1": c1, "groups": [grp]}
            blocks.append(cur)
    for b in blocks:
        wt, wres = wtiles[wstate[0] % len(wtiles)]
        wstate[0] += 1
        ncols = b["c1"] - b["c0"]
        wblock_load(C, wt, wres, W, KC, b["c0"], ncols)
        for grp in b["groups"]:
            for hf in halves:
                for ci in grp:
                    c0, M = chunks[ci]
                    off = c0 - b["c0"]
                    ps, psr = C.psum()
                    rhs_list = [rhs_fn(kc, hf) for kc in range(KC)]
                    rres = []
                    for (_, r) in rhs_list:
                        if r not in rres:
                            rres.append(r)

                    def mm(e, ps=ps, wt=wt, off=off, M=M, rhs_list=rhs_list, ncols=ncols):
                        return [e.matmul(ps[0:M, :], wslice(wt, ncols, kc, off, off + M), rhs_list[kc][0],
                                         start=(kc == 0), stop=(kc == KC - 1)) for kc in range(KC)]
                    S.op("pe", mm, reads=[wres] + rres, writes=[psr], name="lin_mm")
                    epi(ci, hf, ps, psr)


def rmsnorm_T(C, nch, src_fn, g_col, g_res, out_fn, D, tag, post=None):
    S = C.S
    C.uid += 1
    if not hasattr(C, "rn_tiles"):
        C.rn_tiles = {
            "sq": [C.tile(f"rn_sq{i}", [128, 512], BF16) for i in range(3)],
            "rstd": C.tile("rn_rstd", [128, 512], F32),
            "i": 0,
        }
    T = C.rn_tiles
    ps, psr = C.psum()
    for kc in range(nch):
        src, sres = src_fn(kc)
        sq, sqr = T["sq"][T["i"] % 3]
        T["i"] += 1
        S.op("act", lambda e, sq=sq, src=src: e.activation(out=sq[:], in_=src, func=AF.Square),
             reads=[sres], writes=[sqr])
        S.op("pe", lambda e, sq=sq, kc=kc: e.matmul(ps[:], C.ones_b[:], sq[:], start=(kc == 0), stop=(kc == nch - 1)),
             reads=[sqr, C.r_const], writes=[psr])
    rstd, rr = T["rstd"]
    S.op("act", lambda e: e.activation(out=rstd[:], in_=ps[:], func=AF.Sqrt, bias=C.eps_t[:, 0:1], scale=1.0 / D),
         reads=[psr, C.r_const], writes=[rr])
    S.op("dve", lambda e: e.reciprocal(out=rstd[:], in_=rstd[:]), reads=[rr], writes=[rr])
    for kc in range(nch):
        src, sres = src_fn(kc)
        dst, dres = out_fn(kc)
        S.op("dve", lambda e, src=src, dst=dst, kc=kc: e.scalar_tensor_tensor(
            out=dst, in0=src, scalar=g_col[:, kc:kc + 1], in1=rstd[:], op0=ALU.mult, op1=ALU.mult),
            reads=[sres, rr, g_res], writes=[dres])
        if post is not None:
            post(kc, dst, dres)


def rope_tables(C, pos_f, pos_res, inv_col, inv_res, cos_t, sin_t, tres, ncols, nparts, tmp):
    S = C.S
    tmp_t, tmp_r = tmp
    if not hasattr(C, "rt_extra"):
        C.rt_extra = (C.tile("rt_a", [128, ncols], F32), C.tile("rt_i", [128, ncols], I32), C.tile("rt_m", [128, ncols], F32))
    (A, Ar), (II, IIr), (M, Mr) = C.rt_extra
    P = slice(0, nparts)
    N = slice(0, ncols)
    for (dst, phase) in ((sin_t, 0.0), (cos_t, math.pi / 2)):
        def f(e, dst=dst, phase=phase):
            R = tmp_t
            ins = []
            ins.append(e.tensor_scalar(out=A[P, N], in0=pos_f[P, N], scalar1=inv_col[P, 0:1], scalar2=phase,
                                       op0=ALU.mult, op1=ALU.add))
            ins.append(e.tensor_scalar(out=M[P, N], in0=A[P, N], scalar1=1.0 / TWO_PI, scalar2=None, op0=ALU.mult))
            ins.append(e.tensor_copy(out=II[P, N], in_=M[P, N]))
            ins.append(e.tensor_copy(out=M[P, N], in_=II[P, N]))
            ins.append(e.scalar_tensor_tensor(out=R[P, N], in0=M[P, N], scalar=-TWO_PI, in1=A[P, N], op0=ALU.mult, op1=ALU.add))
            ins.append(e.tensor_single_scalar(out=M[P, N], in_=R[P, N], scalar=math.pi, op=ALU.is_gt))
            ins.append(e.scalar_tensor_tensor(out=R[P, N], in0=M[P, N], scalar=-TWO_PI, in1=R[P, N], op0=ALU.mult, op1=ALU.add))
            ins.append(e.tensor_single_scalar(out=M[P, N], in_=R[P, N], scalar=-math.pi, op=ALU.is_lt))
            ins.append(e.scalar_tensor_tensor(out=R[P, N], in0=M[P, N], scalar=TWO_PI, in1=R[P, N], op0=ALU.mult, op1=ALU.add))
            return ins
        S.op("dve", f, reads=[pos_res, inv_res], writes=[tmp_r, Ar, IIr, Mr])
        S.op("act", lambda e, dst=dst: e.activation(out=dst[P, N], in_=tmp_t[P, N], func=AF.Sin),
             reads=[tmp_r], writes=[tres])


def load_pos(C, pos_d, pos_dres, ntok):
    S = C.S
    pi_t, pir = C.tile("pos_i", [128, ntok], I32)
    pf_t, pfr = C.tile("pos_f", [128, ntok], F32)
    src = bass.AP(pos_d.tensor, pos_d.offset, [[0, 128], [1, ntok]])
    S.op("sync", lambda e: e.dma_start(out=pi_t[:], in_=src), reads=[pos_dres], writes=[pir], dma=True, dsem=pir)
    S.op("dve", lambda e: e.tensor_copy(out=pf_t[:], in_=pi_t[:]), reads=[pir], writes=[pfr])
    return pf_t, pfr


def build_p1(NT=1024, C=None, io=None):
    global WT_ELEMS
    WT_ELEMS = 16384
    standalone = C is None
    if standalone:
        nc = bass.Bass("TRN2", target_bir_lowering=False)
        C = Ctx(nc)
        C.begin_phase("", 8, {})
    else:
        nc = C.nc
        C.begin_phase("p1_", 8, io or {})
    S = C.S
    NH = NT // 512
    xT, r_xT = C.dram_in("xT", [32, 128, NT], F32)
    pos, r_pos = C.dram_in("pos", [1, NT], I32)
    W1, r_W1 = C.dram_in("W1", [4096, 2304], F32)
    Wq, r_Wq = C.dram_in("Wq", [1536, 8192], F32)
    sgn, r_sgn = C.dram_in("sgn64", [128, 1], F32)
    gA, r_gA = C.dram_in("gA", [128, 32], F32)
    gQ, r_gQ = C.dram_in("gQ", [128, 12], F32)
    gKV, r_gKV = C.dram_in("gKV", [128, 4], F32)
    inv32, r_inv = C.dram_in("inv32", [128, 1], F32)
    o_qn, r_oqn = C.dram_out("o_qn", [32, 128, NT], BF16)
    o_qr, r_oqr = C.dram_out("o_qr", [16, 128, NT], BF16)
    o_ckv, r_ockv = C.dram_out("o_ckv", [4, 128, NT], BF16)
    o_kr, r_okr = C.dram_out("o_kr", [128, NT], BF16)

    gA_t, gA_r = C.tile("gA_t", [128, 32], F32)
    gQ_t, gQ_r = C.tile("gQ_t", [128, 12], F32)
    gKV_t, gKV_r = C.tile("gKV_t", [128, 4], F32)
    inv_t, inv_r = C.tile("inv_t", [128, 1], F32)
    sgn_t, sgn_r = C.tile("sgn_t", [128, 1], F32)
    for (t, r, d) in ((gA_t, gA_r, gA), (gQ_t, gQ_r, gQ), (gKV_t, gKV_r, gKV), (inv_t, inv_r, inv32), (sgn_t, sgn_r, sgn)):
        S.op("sync", lambda e, t=t, d=d: e.dma_start(out=t[:], in_=d), writes=[r], dma=True, dsem=r)
    pos_f, pos_fr = load_pos(C, pos, r_pos, NT)
    cos_t, cs_r = C.tile("cos_t", [128, NT], F32)
    sin_t, _ = C.tile("sin_t", [128, NT], F32)
    tmp = C.tile("rt_tmp", [128, NT], F32)
    rope_tables(C, pos_f, pos_fr, inv_t, inv_r, cos_t, sin_t, cs_r, NT, 128, tmp)
    S.op("dve", lambda e: e.tensor_scalar(out=sin_t[:], in0=sin_t[:], scalar1=sgn_t[:, 0:1], scalar2=None, op0=ALU.mult),
         reads=[cs_r, sgn_r], writes=[cs_r])

    wtiles = [C.tile(f"w{i}", [128, WT_ELEMS], BF16) for i in range(2)]
    wstate = [0]
    xs = [C.tile(f"xs{i}", [128, 512], F32) for i in range(4)]
    xsi = [0]
    xnT, xn_r = C.tile("xnT", [128, 32, 512], BF16)
    uT, uT_r = C.tile("uT", [128, 16, 512], F32)
    kraw = [C.tile(f"kraw{i}", [128, 512], F32) for i in range(2)]
    cqT, cq_r = C.tile("cqT", [128, 12, 512], BF16)
    ckvT, ckv_r = C.tile("ckvT", [128, 4, 512], BF16)
    st = [C.tile(f"st{i}", [128, 512], BF16) for i in range(4)]
    sti = [0]
    rtmp = [C.tile(f"rtmp{i}", [128, 512], F32) for i in range(2)]
    qraw = C.tile("qraw", [128, 512], F32)

    def rope1(X, Xr, Xs, Xsr, hs, out_d, r_o):
        c = cos_t[:, hs]
        s_ = sin_t[:, hs]
        (ta, tar), (tb, tbr) = rtmp
        o, o_r = st[sti[0] % 4]
        sti[0] += 1
        S.op("dve", lambda e: e.tensor_tensor(out=ta[:], in0=X, in1=c, op=ALU.mult), reads=[Xr, cs_r], writes=[tar])
        S.op("dve", lambda e: e.tensor_tensor(out=tb[:], in0=Xs, in1=s_, op=ALU.mult), reads=[Xsr, cs_r], writes=[tbr])
        S.op("dve", lambda e: e.tensor_tensor(out=o[:], in0=ta[:], in1=tb[:], op=ALU.add), reads=[tar, tbr], writes=[o_r])
        S.op("sync", lambda e: e.dma_start(out=out_d, in_=o[:]), reads=[o_r], writes=[r_o], dma=True, dsem=o_r)

    def do_half(hf):
        hs = slice(hf * 512, (hf + 1) * 512)

        def src_x(kc):
            t, r = xs[xsi[0] % 4]
            xsi[0] += 1
            S.op("sync", lambda e, t=t, kc=kc: e.dma_start(out=t[:], in_=xT[kc, :, hs]), writes=[r], dma=True, dsem=r)
            return t[:], r
        rmsnorm_T(C, 32, src_x, gA_t, gA_r, lambda kc: (xnT[:, kc, :], xn_r), 4096.0, "a")

        def epi1(ci, h_, ps, psr):
            if ci < 16:
                S.op("act", lambda e: e.activation(out=uT[:, ci, :], in_=ps[:], func=AF.Copy), reads=[psr], writes=[uT_r])
            else:
                t, r = kraw[ci - 16]
                S.op("act", lambda e: e.activation(out=t[:], in_=ps[:], func=AF.Copy), reads=[psr], writes=[r])
        linear_T(C, W1, 32, lambda kc, h_: (xnT[:, kc, :], xn_r), [(i * 128, 128) for i in range(18)], [0], epi1,
                 wtiles, wstate=wstate)
        rope1(kraw[0][0][:], kraw[0][1], kraw[1][0][:], kraw[1][1], hs, o_kr[:, hs], r_okr)
        rmsnorm_T(C, 12, lambda kc: (uT[:, kc, :], uT_r), gQ_t, gQ_r, lambda kc: (cqT[:, kc, :], cq_r), 1536.0, "q")
        rmsnorm_T(C, 4, lambda kc: (uT[:, 12 + kc, :], uT_r), gKV_t, gKV_r, lambda kc: (ckvT[:, kc, :], ckv_r), 512.0, "kv")
        S.op("sync", lambda e: e.dma_start(out=o_ckv[:, :, hs].rearrange("c p t -> p c t"), in_=ckvT[:]),
             reads=[ckv_r], writes=[r_ockv], dma=True, dsem=ckv_r)

        order = [(i * 128, 128) for i in range(32)]
        for Pp in range(16):
            order.append((4096 + Pp * 256, 128))
            order.append((4096 + Pp * 256 + 128, 128))

        def epi2(ci, h_, ps, psr):
            if ci < 32:
                o, o_r = st[sti[0] % 4]
                sti[0] += 1
                S.op("act", lambda e: e.activation(out=o[:], in_=ps[:], func=AF.Copy), reads=[psr], writes=[o_r])
                S.op("sync", lambda e: e.dma_start(out=o_qn[ci, :, hs], in_=o[:]), reads=[o_r], writes=[r_oqn],
                     dma=True, dsem=o_r)
            else:
                Q = (ci - 32) // 2
                if (ci - 32) % 2 == 0:
                    S.op("act", lambda e: e.activation(out=qraw[0][:], in_=ps[:], func=AF.Copy), reads=[psr], writes=[qraw[1]])
                else:
                    t, r = kraw[0]
                    S.op("act", lambda e: e.activation(out=t[:], in_=ps[:], func=AF.Copy), reads=[psr], writes=[r])
                    rope1(qraw[0][:], qraw[1], t[:], r, hs, o_qr[Q, :, hs], r_oqr)
        linear_T(C, Wq, 12, lambda kc, h_: (cqT[:, kc, :], cq_r), order, [0], epi2, wtiles, wstate=wstate)

    for hf in range(NH):
        do_half(hf)
    if standalone:
        S.op("sync", None, reads=[r_oqn, r_oqr, r_ockv, r_okr])
        S.emit()
        return nc
    C.end_phase()


def build_p2a(NT=1024, NS=2048, C=None, io=None):
    global WT_ELEMS
    WT_ELEMS = 8192
    standalone = C is None
    if standalone:
        nc = bass.Bass("TRN2", target_bir_lowering=False)
        C = Ctx(nc)
        C.begin_phase("", 4, {})
    else:
        nc = C.nc
        C.begin_phase("p2a_", 4, io or {})
    S = C.S
    scale = 192.0 ** -0.5
    xT, r_xT = C.dram_in("xT", [32, 128, NT], F32)
    if "ex1" not in (io or {}):
        ckv_d, _ = C.dram_in("ckv_full", [4, 128, NS], BF16)
        kr1_d, _ = C.dram_in("kr_full", [128, NS], BF16)
    qn_d, _ = C.dram_in("qn", [32, 128, NT], BF16)
    qr1_d, _ = C.dram_in("qr", [16, 128, NT], BF16)
    Wk, _ = C.dram_in("Wk", [512, 4096], F32)
    Wv, _ = C.dram_in("Wv", [512, 4096], F32)
    Wo, _ = C.dram_in("Wo", [4096, 4096], F32)
    tri_d, _ = C.dram_in("maskAB", [128, 256], BF16)
    o_hA, r_ohA = C.dram_out("o_hA", [32, 128, NT], F32)
    o_dbg, r_odbg = C.dram_out("o_attn", [32, 128, NT], BF16)

    ckv, ckv_r = C.tile("ckv", [128, 4, NS], BF16)
    kr1, kr1_r = C.tile("kr1", [128, NS], BF16)
    tri, tri_r = C.tile("tri", [128, 256], BF16)
    if "ex1" in C.io:
        ex1, r_ex1 = C.io["ex1"]

        def ld_ckv(e):
            ins = []
            for r in range(2):
                for c in range(4):
                    dst = ckv[:, c, :].rearrange("p (j r t) -> p j r t", r=2, t=128)[:, :, r, :]
                    src = ex1[r * 640 + c * 128:r * 640 + (c + 1) * 128, :].rearrange("p (j t) -> p j t", t=128)
                    ins.append(e.dma_start(out=dst, in_=src))
            return ins
        S.op("sync", ld_ckv, reads=[r_ex1], writes=[ckv_r], dma=True, dsem=ckv_r, ndma=8)
        S.op("sync", lambda e: [e.dma_start(out=kr1[:, :].rearrange("p (j r t) -> p j r t", r=2, t=128)[:, :, r, :],
                                            in_=ex1[r * 640 + 512:r * 640 + 640, :].rearrange("p (j t) -> p j t", t=128)) for r in range(2)],
             reads=[r_ex1], writes=[kr1_r], dma=True, dsem=kr1_r, ndma=2)
    else:
        S.op("sync", lambda e: e.dma_start(out=ckv[:], in_=ckv_d.rearrange("c p t -> p c t")), writes=[ckv_r], dma=True, dsem=ckv_r)
        S.op("sync", lambda e: e.dma_start(out=kr1[:], in_=kr1_d), writes=[kr1_r], dma=True, dsem=kr1_r)
    S.op("sync", lambda e: e.dma_start(out=tri[:], in_=tri_d), writes=[tri_r], dma=True, dsem=tri_r)

    wtiles = [C.tile(f"w{i}", [128, WT_ELEMS], BF16) for i in range(2)]
    wstate = [0]
    kTq, kTq_r = C.tile("kTq", [128, 4, NS], BF16)
    Vq, Vq_r = C.tile("Vq", [128, 16, 512], BF16)
    qn_t = [C.tile(f"qn{i}", [128, NT], BF16) for i in range(2)]
    qr1_t = [C.tile(f"qr1_{i}", [128, NT], BF16) for i in range(2)]
    pt_t = [C.tile(f"pt{i}", [128, 512], BF16) for i in range(4)]
    pti = [0]
    rec_t = [C.tile(f"rec{i}", [128, 512], F32) for i in range(2)]
    oT, oT_r = C.tile("oT", [128, 32, NT], BF16)
    acc = [((C.ps[4], C.psr[4]), (C.ps[5], C.psr[5])), ((C.ps[6], C.psr[6]), (C.ps[7], C.psr[7]))]
    acci = [0]

    for Q in range(8):
        def epik(ci, tchunk, ps, psr):
            S.op("act", lambda e: e.activation(out=kTq[:, ci, tchunk * 512:(tchunk + 1) * 512], in_=ps[:], func=AF.Copy),
                 reads=[psr], writes=[kTq_r])
        linear_T(C, Wk, 4, lambda kc, tch: (ckv[:, kc, tch * 512:(tch + 1) * 512], ckv_r),
                 [(Q * 512 + i * 128, 128) for i in range(4)], [0, 1, 2, 3], epik, wtiles, wstate=wstate)
        wt, wres = wtiles[wstate[0] % 2]
        wstate[0] += 1
        wblock_load(C, wt, wres, Wv, 4, Q * 512, 512)
        for kt in range(NS // 128):
            ps, psr = C.psum()
            S.op("pe", lambda e, ps=ps, wt=wt, kt=kt: [e.matmul(ps[:], ckv[:, kc, kt * 128:(kt + 1) * 128], wslice(wt, 512, kc, 0, 512),
                                                                 start=(kc == 0), stop=(kc == 3)) for kc in range(4)],
                 reads=[wres, ckv_r], writes=[psr])
            S.op("dve", lambda e, ps=ps, kt=kt: e.tensor_copy(out=Vq[:, kt, :], in_=ps[:]), reads=[psr], writes=[Vq_r])
        for hq in range(4):
            h = 4 * Q + hq
            if hq % 2 == 0:
                q1, q1r = qr1_t[(h // 2) % 2]
                S.op("sync", lambda e, q1=q1, h=h: e.dma_start(out=q1[:], in_=qr1_d[h // 2]), writes=[q1r], dma=True, dsem=q1r)
            qn, qnr = qn_t[h % 2]
            S.op("sync", lambda e, qn=qn, h=h: e.dma_start(out=qn[:], in_=qn_d[h]), writes=[qnr], dma=True, dsem=qnr)
            P = slice(64 * (h % 2), 64 * (h % 2) + 64)
            for G in range(2):
                j0 = 4 * G
                (po, por), (pd, pdr) = acc[acci[0] % 2]
                acci[0] += 1
                kt_last = 2 * (j0 + 3) + 1
                stt = {}

                def stA(kt, j0=j0, qn=qn, qnr=qnr, q1=q1, q1r=q1r, P=P, hq=hq):
                    ja = max(j0, kt // 2)
                    c0 = (ja - j0) * 128
                    N = 512 - c0
                    qc0 = j0 * 128 + c0
                    qc1 = (j0 + 4) * 128
                    diag = (kt // 2) >= j0
                    mcol = (kt % 2) * 128
                    ps, psr = C.psum()
                    ks = slice(kt * 128, (kt + 1) * 128)
                    S.op("pe", lambda e: [e.matmul(ps[:, 0:N], kTq[:, hq, ks], qn[:, qc0:qc1], start=True, stop=False),
                                          e.matmul(ps[:, 0:N], kr1[P, ks], q1[P, qc0:qc1], start=False, stop=True)],
                         reads=[kTq_r, qnr, q1r, kr1_r], writes=[psr])
                    pt, ptr = pt_t[pti[0] % 4]
                    pti[0] += 1
                    S.op("act", lambda e: e.activation(out=pt[:, 0:N], in_=ps[:, 0:N], func=AF.Exp, scale=scale), reads=[psr], writes=[ptr])
                    if diag:
                        S.op("dve", lambda e: e.tensor_tensor(out=pt[:, 0:128], in0=pt[:, 0:128], in1=tri[:, mcol:mcol + 128], op=ALU.mult),
                             reads=[ptr, tri_r], writes=[ptr])
                    stt[kt] = (pt, ptr, c0, N)

                def stB(kt, po=po, por=por, pd=pd, pdr=pdr, hq=hq, kt_last=kt_last):
                    pt, ptr, c0, N = stt.pop(kt)
                    S.op("pe", lambda e: [
                        e.matmul(po[:, c0:512], Vq[:, kt, hq * 128:(hq + 1) * 128], pt[:, 0:N], start=(kt == 0), stop=(kt == kt_last)),
                        e.matmul(pd[:, c0:512], C.ones_b[:], pt[:, 0:N], start=(kt == 0), stop=(kt == kt_last))],
                        reads=[ptr, Vq_r, C.r_const], writes=[por, pdr])
                LA = 2
                for n in range(kt_last + 1 + LA):
                    if n <= kt_last:
                        stA(n)
                    if n - LA >= 0:
                        stB(n - LA)
                rec, recr = rec_t[G]
                S.op("dve", lambda e, rec=rec, pd=pd: e.reciprocal(out=rec[:], in_=pd[:]), reads=[pdr], writes=[recr])
                S.op("dve", lambda e, rec=rec, po=po, h=h, j0=j0: e.tensor_tensor(out=oT[:, h, j0 * 128:(j0 + 4) * 128], in0=po[:], in1=rec[:],
                                                                                 op=ALU.mult), reads=[por, recr], writes=[oT_r])
    S.op("sync", lambda e: e.dma_start(out=o_dbg.rearrange("c p t -> p c t"), in_=oT[:]), reads=[oT_r], writes=[r_odbg], dma=True, dsem=oT_r)

    xs = [C.tile(f"xs{i}", [128, 512], F32) for i in range(4)]
    xsi = [0]

    def epio(ci, hf, ps, psr):
        t, r = xs[xsi[0] % 4]
        xsi[0] += 1
        hs = slice(hf * 512, (hf + 1) * 512)
        S.op("sync", lambda e: e.dma_start(out=t[:], in_=xT[ci, :, hs]), writes=[r], dma=True, dsem=r)
        S.op("dve", lambda e: e.tensor_tensor(out=t[:], in0=ps[:], in1=t[:], op=ALU.add), reads=[psr, r], writes=[r])
        S.op("sync", lambda e: e.dma_start(out=o_hA[ci, :, hs], in_=t[:]), reads=[r], writes=[r_ohA], dma=True, dsem=r)
    linear_T(C, Wo, 32, lambda kc, hf: (oT[:, kc, hf * 512:(hf + 1) * 512], oT_r), [(i * 128, 128) for i in range(32)],
             [0, 1], epio, wtiles, wstate=wstate)
    if standalone:
        S.op("sync", None, reads=[r_ohA, r_odbg])
        S.emit()
        return nc
    C.end_phase()


def build_ffn(final_norm=False, NT=1024, DFF=11008, C=None, io=None, tag='ffn'):
    global WT_ELEMS
    WT_ELEMS = 16384
    standalone = C is None
    if standalone:
        nc = bass.Bass("TRN2", target_bir_lowering=False)
        C = Ctx(nc)
        C.begin_phase("", 8, {})
    else:
        nc = C.nc
        C.begin_phase(tag + "_", 8, io or {})
    S = C.S
    NJ = DFF // 128
    hT, r_hT = C.dram_in("hT", [32, 128, NT], F32)
    Win, _ = C.dram_in("Win", [4096, 2 * DFF], F32)
    Wout, _ = C.dram_in("Wout", [DFF, 4096], F32)
    gF, _ = C.dram_in("gF", [128, 32], F32)
    o_h, r_oh = C.dram_out("o_h", [32, 128, NT], F32)
    gF_t, gF_r = C.tile("gF_t", [128, 32], F32)
    S.op("sync", lambda e: e.dma_start(out=gF_t[:], in_=gF), writes=[gF_r], dma=True, dsem=gF_r)
    if final_norm:
        gN, _ = C.dram_in("gN", [128, 32], F32)
        o_fin, r_ofin = C.dram_out("o_fin", [32, 128, NT], F32)
        gN_t, gN_r = C.tile("gN_t", [128, 32], F32)
        S.op("sync", lambda e: e.dma_start(out=gN_t[:], in_=gN), writes=[gN_r], dma=True, dsem=gN_r)
    WTE = 12288
    wtiles = [C.tile(f"w{i}", [128, WTE], BF16) for i in range(2)]
    wstate = [0]
    xs = [C.tile(f"xs{i}", [128, 512], F32) for i in range(4)]
    xsi = [0]
    NHF = NT // 512
    NTH = 3
    JT = (NJ + NTH - 1) // NTH
    xnT, xn_r = C.tile("xnT", [128, 32, NT], BF16)
    gT, gT_r = C.tile("gT", [128, JT, NT], BF16)
    sa_t = [C.tile(f"sa{i}", [128, 512], F32) for i in range(2)]
    fo_t = [C.tile(f"fo{i}", [128, 512], F32) for i in range(2)]
    foi = [0]

    for hf in range(NHF):
        hs = slice(hf * 512, (hf + 1) * 512)

        def src_x(kc, hs=hs):
            t, r = xs[xsi[0] % 4]
            xsi[0] += 1
            S.op("sync", lambda e, t=t, kc=kc: e.dma_start(out=t[:], in_=hT[kc, :, hs]), writes=[r], dma=True, dsem=r)
            return t[:], r
        rmsnorm_T(C, 32, src_x, gF_t, gF_r, lambda kc, hs=hs: (xnT[:, kc, hs], xn_r), 4096.0, "f")

    def do_third(t3):
        jlo = t3 * JT
        jhi = min(NJ, jlo + JT)
        nj = jhi - jlo

        def epi1(ci, hf, ps, psr):
            hs = slice(hf * 512, (hf + 1) * 512)
            jj = ci // 2
            sa, sar = sa_t[hf % 2]
            if ci % 2 == 0:
                S.op("act", lambda e: e.activation(out=sa[:], in_=ps[:], func=AF.Silu), reads=[psr], writes=[sar])
            else:
                S.op("dve", lambda e: e.tensor_tensor(out=gT[:, jj, hs], in0=ps[:], in1=sa[:], op=ALU.mult),
                     reads=[psr, sar], writes=[gT_r])
        linear_T_pairs(C, Win, 32, lambda kc, hf: (xnT[:, kc, hf * 512:(hf + 1) * 512], xn_r),
                       [(jlo * 256 + i * 128, 128) for i in range(2 * nj)], list(range(NHF)), epi1, wtiles, wstate, group=2)

        def epi2(ci, hf, ps, psr):
            hs = slice(hf * 512, (hf + 1) * 512)
            t, r = xs[xsi[0] % 4]
            xsi[0] += 1
            if t3 == 0:
                S.op("sync", lambda e: e.dma_start(out=t[:], in_=hT[ci, :, hs]), writes=[r], dma=True, dsem=r)
            else:
                S.op("sync", lambda e: e.dma_start(out=t[:], in_=o_h[ci, :, hs]), reads=[r_oh], writes=[r], dma=True, dsem=r)
            S.op("dve", lambda e: e.tensor_tensor(out=t[:], in0=ps[:], in1=t[:], op=ALU.add), reads=[psr, r], writes=[r])
            S.op("sync", lambda e: e.dma_start(out=o_h[ci, :, hs], in_=t[:]), reads=[r], writes=[r_oh], dma=True, dsem=r)
        linear_T(C, Wout[jlo * 128:jhi * 128, :], nj, lambda kc, hf: (gT[:, kc, hf * 512:(hf + 1) * 512], gT_r),
                 [(i * 128, 128) for i in range(32)], list(range(NHF)), epi2, wtiles, wstate=wstate)
    for t3 in range(NTH):
        do_third(t3)

    if final_norm:
        for hf in range(NHF):
            hs = slice(hf * 512, (hf + 1) * 512)

            def src_h(kc, hs=hs):
                t, r = xs[xsi[0] % 4]
                xsi[0] += 1
                S.op("sync", lambda e, t=t, kc=kc: e.dma_start(out=t[:], in_=o_h[kc, :, hs]), reads=[r_oh], writes=[r],
                     dma=True, dsem=r)
                return t[:], r
            rmsnorm_store(C, 32, src_h, gN_t, gN_r, fo_t, foi, 4096.0, lambda kc, hs=hs: o_fin[kc, :, hs], r_ofin)
    if standalone:
        S.op("sync", None, reads=[r_oh] + ([r_ofin] if final_norm else []))
        S.emit()
        return nc
    C.end_phase()


def rmsnorm_store(C, nch, src_fn, g_col, g_res, fo_t, foi, D, dst_fn, dst_res):
    S = C.S

    def out_fn(kc):
        t, r = fo_t[foi[0] % 2]
        return t[:], r
    T_store = []

    def out_fn2(kc):
        t, r = fo_t[foi[0] % 2]
        foi[0] += 1
        T_store.append((kc, t, r))
        return t[:], r
    rmsnorm_T(C, nch, src_fn, g_col, g_res, out_fn2, D, "fin",
              post=lambda kc, t_ap, r: S.op("sync", lambda e: e.dma_start(out=dst_fn(kc), in_=t_ap), reads=[r], writes=[dst_res],
                                           dma=True, dsem=r))


def build_p3a(NT=1024, C=None, io=None):
    global WT_ELEMS
    WT_ELEMS = 16384
    standalone = C is None
    if standalone:
        nc = bass.Bass("TRN2", target_bir_lowering=False)
        C = Ctx(nc)
        C.begin_phase("", 8, {})
    else:
        nc = C.nc
        C.begin_phase("p3a_", 8, io or {})
    S = C.S
    hT, _ = C.dram_in("hT", [32, 128, NT], F32)
    pos, r_pos = C.dram_in("pos", [1, NT], I32)
    Ws, _ = C.dram_in("Ws", [4096, 3840], F32)
    Wb, _ = C.dram_in("Wb", [4096, 6240], F32)
    gS, _ = C.dram_in("gS", [128, 32], F32)
    gB, _ = C.dram_in("gB", [128, 32], F32)
    inv96, _ = C.dram_in("inv96", [128, 1], F32)
    if "o_kv_fn" in (io or {}):
        okf, ovf, r_ok = io["o_kv_fn"]
        r_ov = r_ok
    else:
        o_k, r_ok = C.dram_out("o_k", [12, 2, 96, NT], BF16)
        o_v, r_ov = C.dram_out("o_v", [12, 128, NT], BF16)
        okf = lambda bg, pc: o_k[bg, pc]
        ovf = lambda bg: o_v[bg]
    o_q, r_oq = C.dram_out("o_q", [32, 2, 96, NT], BF16)
    o_g, r_og = C.dram_out("o_g", [96, NT], F32)
    gS_t, gS_r = C.tile("gS_t", [128, 32], F32)
    gB_t, gB_r = C.tile("gB_t", [128, 32], F32)
    inv_t, inv_r = C.tile("inv_t", [128, 1], F32)
    for (t, r, d) in ((gS_t, gS_r, gS), (gB_t, gB_r, gB), (inv_t, inv_r, inv96)):
        S.op("sync", lambda e, t=t, d=d: e.dma_start(out=t[:], in_=d), writes=[r], dma=True, dsem=r)
    pos_f, pos_fr = load_pos(C, pos, r_pos, NT)
    cos_t, cs_r = C.tile("cos_t", [128, NT], F32)
    sin_t, _ = C.tile("sin_t", [128, NT], F32)
    tmp = C.tile("rt_tmp", [128, NT], F32)
    rope_tables(C, pos_f, pos_fr, inv_t, inv_r, cos_t, sin_t, cs_r, NT, 96, tmp)

    wtiles = [C.tile(f"w{i}", [128, WT_ELEMS], BF16) for i in range(2)]
    wstate = [0]
    xs = [C.tile(f"xs{i}", [128, 512], F32) for i in range(4)]
    xsi = [0]
    xnT, xn_r = C.tile("xnT", [128, 32, NT], BF16)
    raw1 = C.tile("raw1", [128, 512], F32)
    raw2 = C.tile("raw2", [128, 512], F32)
    rt = [C.tile(f"rtmp{i}", [128, 512], F32) for i in range(4)]
    st = [C.tile(f"st{i}", [128, 512], BF16) for i in range(4)]
    sti = [0]
    gt = C.tile("gt", [128, 512], F32)
    P = slice(0, 96)

    def ropeN(hs, d1, d2, rd):
        (x1, x1r), (x2, x2r) = raw1, raw2
        c = cos_t[P, hs]
        s_ = sin_t[P, hs]
        (ta, tar), (tb, tbr), (tc, tcr), (td, tdr) = rt
        o1, o1r = st[sti[0] % 4]
        o2, o2r = st[(sti[0] + 1) % 4]
        sti[0] += 2
        S.op("dve", lambda e: e.tensor_tensor(out=ta[P, :], in0=x1[P, :], in1=c, op=ALU.mult), reads=[x1r, cs_r], writes=[tar])
        S.op("dve", lambda e: e.tensor_tensor(out=tb[P, :], in0=x2[P, :], in1=s_, op=ALU.mult), reads=[x2r, cs_r], writes=[tbr])
        S.op("dve", lambda e: e.tensor_tensor(out=o1[P, :], in0=ta[P, :], in1=tb[P, :], op=ALU.subtract), reads=[tar, tbr], writes=[o1r])
        S.op("dve", lambda e: e.tensor_tensor(out=tc[P, :], in0=x2[P, :], in1=c, op=ALU.mult), reads=[x2r, cs_r], writes=[tcr])
        S.op("dve", lambda e: e.tensor_tensor(out=td[P, :], in0=x1[P, :], in1=s_, op=ALU.mult), reads=[x1r, cs_r], writes=[tdr])
        S.op("dve", lambda e: e.tensor_tensor(out=o2[P, :], in0=tc[P, :], in1=td[P, :], op=ALU.add), reads=[tcr, tdr], writes=[o2r])
        S.op("sync", lambda e: e.dma_start(out=d1, in_=o1[P, :]), reads=[o1r], writes=[rd], dma=True, dsem=o1r)
        S.op("sync", lambda e: e.dma_start(out=d2, in_=o2[P, :]), reads=[o2r], writes=[rd], dma=True, dsem=o2r)

    NHF = NT // 512

    def norm_all(g_t, g_r, tag):
        for hf in range(NHF):
            hs = slice(hf * 512, (hf + 1) * 512)

            def src_x(kc, hs=hs):
                t, r = xs[xsi[0] % 4]
                xsi[0] += 1
                S.op("sync", lambda e, t=t, kc=kc: e.dma_start(out=t[:], in_=hT[kc, :, hs]), writes=[r], dma=True, dsem=r)
                return t[:], r
            rmsnorm_T(C, 32, src_x, g_t, g_r, lambda kc, hs=hs: (xnT[:, kc, hs], xn_r), 4096.0, tag)

    def rhs(kc, hf):
        return (xnT[:, kc, hf * 512:(hf + 1) * 512], xn_r)

    norm_all(gS_t, gS_r, "s")
    chunks = []
    for bg in range(12):
        chunks += [(bg * 320, 96), (bg * 320 + 96, 96), (bg * 320 + 192, 128)]

    def epis(ci, hf, ps, psr):
        hs = slice(hf * 512, (hf + 1) * 512)
        bg, k = ci // 3, ci % 3
        if k == 0:
            S.op("act", lambda e: e.activation(out=raw1[0][P, :], in_=ps[P, :], func=AF.Copy), reads=[psr], writes=[raw1[1]])
        elif k == 1:
            S.op("act", lambda e: e.activation(out=raw2[0][P, :], in_=ps[P, :], func=AF.Copy), reads=[psr], writes=[raw2[1]])
            ropeN(hs, okf(bg, 0)[:, hs], okf(bg, 1)[:, hs], r_ok)
        else:
            o, o_r = st[sti[0] % 4]
            sti[0] += 1
            S.op("act", lambda e: e.activation(out=o[:], in_=ps[:], func=AF.Copy), reads=[psr], writes=[o_r])
            S.op("sync", lambda e: e.dma_start(out=ovf(bg)[:, hs], in_=o[:]), reads=[o_r], writes=[r_ov], dma=True, dsem=o_r)
    linear_T_pairs(C, Ws, 32, rhs, chunks, list(range(NHF)), epis, wtiles, wstate, group=3)

    norm_all(gB_t, gB_r, "b")
    chunks = []
    for h in range(32):
        chunks += [(h * 192, 96), (h * 192 + 96, 96)]
    chunks.append((6144, 96))

    def epib(ci, hf, ps, psr):
        hs = slice(hf * 512, (hf + 1) * 512)
        if ci == 64:
            g_t, g_r = gt
            S.op("act", lambda e: e.activation(out=g_t[P, :], in_=ps[P, :], func=AF.Sigmoid), reads=[psr], writes=[g_r])
            S.op("sync", lambda e: e.dma_start(out=o_g[:, hs], in_=g_t[P, :]), reads=[g_r], writes=[r_og], dma=True, dsem=g_r)
            return
        h, k = ci // 2, ci % 2
        if k == 0:
            S.op("act", lambda e: e.activation(out=raw1[0][P, :], in_=ps[P, :], func=AF.Copy), reads=[psr], writes=[raw1[1]])
        else:
            S.op("act", lambda e: e.activation(out=raw2[0][P, :], in_=ps[P, :], func=AF.Copy), reads=[psr], writes=[raw2[1]])
            ropeN(hs, o_q[h, 0, :, hs], o_q[h, 1, :, hs], r_oq)
    linear_T_pairs(C, Wb, 32, rhs, chunks, list(range(NHF)), epib, wtiles, wstate, group=2)
    if standalone:
        S.op("sync", None, reads=[r_ok, r_ov, r_oq, r_og])
        S.emit()
        return nc
    C.end_phase()


GELU_C = 1.5957691216057308


def build_p3b1(NS=2048, C=None, io=None):
    standalone = C is None
    if standalone:
        nc = bass.Bass("TRN2", target_bir_lowering=False)
        C = Ctx(nc)
        C.begin_phase("", 8, {})
    else:
        nc = C.nc
        C.begin_phase("p3b1_", 8, io or {})
    S = C.S
    NCMP = 127
    if "ex2" not in (io or {}):
        kTc_d, _ = C.dram_in("kTc", [4, 2, 96, NS], BF16)
        vTc_d, _ = C.dram_in("vTc", [4, 128, NS], BF16)
    w1k_d, _ = C.dram_in("w1k", [6144, 192], F32)
    w2k_d, _ = C.dram_in("w2k", [192, 192], F32)
    pek_d, _ = C.dram_in("pekT", [96, 64], F32)
    w1v_d, _ = C.dram_in("w1v", [4096, 128], F32)
    w2v_d, _ = C.dram_in("w2v", [128, 128], F32)
    pev_d, _ = C.dram_in("pevT", [128, 32], F32)
    o_kc, r_okc = C.dram_out("o_kcT", [4, 2, 96, NCMP], BF16)
    o_vc, r_ovc = C.dram_out("o_vc", [NCMP, 4, 128], BF16)

    w1k, w1k_r = C.tile("w1k_t", [96, 64, 192], BF16)
    w2k, w2k_r = C.tile("w2k_t", [96, 2, 192], BF16)
    pek, pek_r = C.tile("pek_t", [96, 64], BF16)
    w1v, w1v_r = C.tile("w1v_t", [128, 32, 128], BF16)
    w2v, w2v_r = C.tile("w2v_t", [128, 128], BF16)
    pev, pev_r = C.tile("pev_t", [128, 32], BF16)
    w1k_src = w1k_d.rearrange("(i d) n -> d i n", d=96)
    S.op("pool", lambda e: [e.dma_start(out=w1k[:, i * 16:(i + 1) * 16, :], in_=w1k_src[:, i * 16:(i + 1) * 16, :]) for i in range(4)],
         writes=[w1k_r], dma=True, dsem=w1k_r, ndma=4)
    S.op("pool", lambda e: e.dma_start(out=w2k[:], in_=w2k_d.rearrange("(pc d) n -> d pc n", d=96)), writes=[w2k_r], dma=True, dsem=w2k_r)
    S.op("pool", lambda e: e.dma_start(out=pek[:], in_=pek_d), writes=[pek_r], dma=True, dsem=pek_r)
    S.op("pool", lambda e: e.dma_start(out=w1v[:], in_=w1v_d.rearrange("(l d) n -> d l n", d=128)), writes=[w1v_r], dma=True, dsem=w1v_r)
    S.op("pool", lambda e: e.dma_start(out=w2v[:], in_=w2v_d), writes=[w2v_r], dma=True, dsem=w2v_r)
    S.op("pool", lambda e: e.dma_start(out=pev[:], in_=pev_d), writes=[pev_r], dma=True, dsem=pev_r)

    kt_t = [C.tile(f"ktg{i}", [96, 2, NS], BF16) for i in range(2)]
    vt_t = [C.tile(f"vtg{i}", [128, NS], BF16) for i in range(2)]
    bias_t = [C.tile(f"bias{i}", [128, 1], F32) for i in range(2)]
    xg_t = [C.tile(f"xg{i}", [128, 128], F32) for i in range(2)]
    u_t = [C.tile(f"ug{i}", [128, 128], F32) for i in range(2)]
    g_t = [C.tile(f"gg{i}", [128, 128], BF16) for i in range(3)]
    out_t = [C.tile(f"og{i}", [128, 128], BF16) for i in range(2)]
    cnt = [0]

    def gelu_from_psum(ps, psr, bps, bpsr, np_, dst, dst_r):
        i = cnt[0] % 2
        cnt[0] += 1
        (bt, btr), (xg, xgr), (u, ur) = bias_t[i], xg_t[i], u_t[i]
        P = slice(0, np_)
        N = slice(0, NCMP)
        S.op("dve", lambda e: e.tensor_copy(out=bt[P, :], in_=bps[P, 0:1]), reads=[bpsr], writes=[btr])
        S.op("act", lambda e: e.activation(out=xg[P, N], in_=ps[P, N], func=AF.Identity, bias=bt[P, 0:1]), reads=[psr, btr], writes=[xgr])

        def f(e):
            return [e.tensor_tensor(out=u[P, N], in0=xg[P, N], in1=xg[P, N], op=ALU.mult),
                    e.tensor_scalar(out=u[P, N], in0=u[P, N], scalar1=0.044715, scalar2=1.0, op0=ALU.mult, op1=ALU.add),
                    e.tensor_tensor(out=u[P, N], in0=u[P, N], in1=xg[P, N], op=ALU.mult)]
        S.op("dve", f, reads=[xgr], writes=[ur])
        S.op("act", lambda e: e.activation(out=u[P, N], in_=u[P, N], func=AF.Sigmoid, scale=GELU_C), reads=[ur], writes=[ur])
        S.op("dve", lambda e: e.tensor_tensor(out=dst[P, N], in0=xg[P, N], in1=u[P, N], op=ALU.mult), reads=[xgr, ur], writes=[dst_r])

    for g in range(4):
        ktg, ktr = kt_t[g % 2]
        vtg, vtr = vt_t[g % 2]
        if "ex2" in C.io:
            exk, exv, r_ex2 = C.io["ex2"]
            S.op("sync", lambda e, ktg=ktg, g=g: [e.dma_start(
                out=ktg[:, pc, :].rearrange("p (j r t) -> p j r t", r=2, t=128)[:, :, r, :],
                in_=exk(r, g * 2 + pc).rearrange("p (j t) -> p j t", t=128))
                for r in range(2) for pc in range(2)], reads=[r_ex2], writes=[ktr], dma=True, dsem=ktr, ndma=4)
            S.op("sync", lambda e, vtg=vtg, g=g: [e.dma_start(
                out=vtg[:, :].rearrange("p (j r t) -> p j r t", r=2, t=128)[:, :, r, :],
                in_=exv(r, g).rearrange("p (j t) -> p j t", t=128))
                for r in range(2)], reads=[r_ex2], writes=[vtr], dma=True, dsem=vtr, ndma=2)
        else:
            S.op("sync", lambda e, ktg=ktg, g=g: e.dma_start(out=ktg[:], in_=kTc_d[g].rearrange("pc d t -> d pc t")), writes=[ktr], dma=True, dsem=ktr)
            S.op("sync", lambda e, vtg=vtg, g=g: e.dma_start(out=vtg[:], in_=vTc_d[g]), writes=[vtr], dma=True, dsem=vtr)
        gts = []
        for npc in range(2):
            ps, psr = C.psum()
            bps, bpsr = C.psum()
            ns = slice(npc * 96, (npc + 1) * 96)

            def mm(e, ps=ps, ktg=ktg, ns=ns):
                ins = []
                for i in range(64):
                    l, pc = i // 2, i % 2
                    rhs = bass.AP(ktg, pc * NS + l, [[2 * NS, 96], [16, NCMP]])
                    ins.append(e.matmul(ps[0:96, 0:NCMP], w1k[:, i, ns], rhs, start=(i == 0), stop=(i == 63)))
                return ins
            S.op("pe", mm, reads=[w1k_r, ktr], writes=[psr])
            S.op("pe", lambda e, bps=bps, ns=ns: [e.matmul(bps[0:96, 0:1], w1k[:, i, ns], pek[:, i:i + 1], start=(i == 0), stop=(i == 63))
                                                  for i in range(64)], reads=[w1k_r, pek_r], writes=[bpsr])
            gt, gtr = g_t[npc]
            gelu_from_psum(ps, psr, bps, bpsr, 96, gt, gtr)
            gts.append((gt, gtr))
        for n2 in range(2):
            ps, psr = C.psum()
            S.op("pe", lambda e, ps=ps, n2=n2: [e.matmul(ps[0:96, 0:NCMP], w2k[:, npc, n2 * 96:(n2 + 1) * 96], gts[npc][0][0:96, 0:NCMP],
                                                          start=(npc == 0), stop=(npc == 1)) for npc in range(2)],
                 reads=[w2k_r, gts[0][1], gts[1][1]], writes=[psr])
            o, o_r = out_t[n2]
            S.op("act", lambda e, o=o, ps=ps: e.activation(out=o[0:96, 0:NCMP], in_=ps[0:96, 0:NCMP], func=AF.Copy), reads=[psr], writes=[o_r])
            S.op("sync", lambda e, o=o, g=g, n2=n2: e.dma_start(out=o_kc[g, n2], in_=o[0:96, 0:NCMP]), reads=[o_r], writes=[r_okc],
                 dma=True, dsem=o_r)
        ps, psr = C.psum()
        bps, bpsr = C.psum()
        S.op("pe", lambda e, ps=ps, vtg=vtg: [e.matmul(ps[:, 0:NCMP], w1v[:, l, :], bass.AP(vtg, l, [[NS, 128], [16, NCMP]]),
                                                       start=(l == 0), stop=(l == 31)) for l in range(32)],
             reads=[w1v_r, vtr], writes=[psr])
        S.op("pe", lambda e, bps=bps: [e.matmul(bps[:, 0:1], w1v[:, l, :], pev[:, l:l + 1], start=(l == 0), stop=(l == 31)) for l in range(32)],
             reads=[w1v_r, pev_r], writes=[bpsr])
        gt, gtr = g_t[2]
        gelu_from_psum(ps, psr, bps, bpsr, 128, gt, gtr)
        ps, psr = C.psum()
        S.op("pe", lambda e, ps=ps, gt=gt: e.matmul(ps[0:NCMP, 0:128], gt[:, 0:NCMP], w2v[:, :], start=True, stop=True),
             reads=[gtr, w2v_r], writes=[psr])
        o, o_r = out_t[0]
        S.op("act", lambda e, o=o, ps=ps: e.activation(out=o[0:NCMP, :], in_=ps[0:NCMP, 0:128], func=AF.Copy), reads=[psr], writes=[o_r])
        S.op("sync", lambda e, o=o, g=g: e.dma_start(out=o_vc[:, g, :], in_=o[0:NCMP, :]), reads=[o_r], writes=[r_ovc], dma=True, dsem=o_r)
    if standalone:
        S.op("sync", None, reads=[r_okc, r_ovc])
        S.emit()
        return nc
    C.end_phase()


def build_p3b2(NT=1024, NS=2048, C=None, io=None):
    standalone = C is None
    if standalone:
        nc = bass.Bass("TRN2", target_bir_lowering=False)
        C = Ctx(nc)
        C.begin_phase("", 4, {})
    else:
        nc = C.nc
        C.begin_phase("p3b2_", 4, io or {})
    S = C.S
    scale = 192.0 ** -0.5
    NCMP = 127
    kcT_d, _ = C.dram_in("kcT", [4, 2, 96, NCMP], BF16)
    vc_d, _ = C.dram_in("vc", [NCMP, 4, 128], BF16)
    if "ex2" not in (io or {}):
        kTs_d, _ = C.dram_in("kTs", [4, 2, 96, NS], BF16)
        kTw_d, _ = C.dram_in("kTw", [4, 2, 96, NS], BF16)
        Vs_d, _ = C.dram_in("Vs", [128, 16, 4, 128], BF16)
        Vw_d, _ = C.dram_in("Vw", [128, 16, 4, 128], BF16)
    qT_d, _ = C.dram_in("qT", [32, 2, 96, NT], BF16)
    gates_d, _ = C.dram_in("gates", [96, NT], F32)
    maskc_d, _ = C.dram_in("mask_c", [NCMP, NT], F32)
    bonus_d, _ = C.dram_in("bonus", [128, 8, 32], F32)
    mAB_d, _ = C.dram_in("maskAB", [128, 256], BF16)
    mW_d, _ = C.dram_in("maskW", [128, 6, 128], BF16)
    E_d, _ = C.dram_in("Eexp", [32, 16, 128], BF16)
    ovl_d, _ = C.dram_in("ovl", [NCMP, 32], F32)
    idf_d, _ = C.dram_in("ident_f", [128, 128], F32)
    idb_d, _ = C.dram_in("ident_b", [128, 128], BF16)
    o_attn, r_oattn = C.dram_out("o_attn", [32, 128, NT], BF16)

    def const(name, shape, dt, src):
        t, r = C.tile(name, shape, dt)
        S.op("sync", lambda e: e.dma_start(out=t[:], in_=src), writes=[r], dma=True, dsem=r)
        return t, r
    kcT, kcT_r = const("kcT_t", [96, 4, 2, NCMP], BF16, kcT_d.rearrange("g pc d c -> d g pc c"))
    vc, vc_r = const("vc_t", [NCMP, 4, 128], BF16, vc_d)
    gates, gates_r = const("gates_t", [96, NT], F32, gates_d)
    maskc, maskc_r = const("maskc_t", [NCMP, NT], F32, maskc_d)
    bonus, bonus_r = const("bonus_t", [128, 8, 32], F32, bonus_d)
    mAB, mAB_r = const("mAB_t", [128, 256], BF16, mAB_d)
    mW, mW_r = const("mW_t", [128, 6, 128], BF16, mW_d)
    Ee, Ee_r = const("E_t", [32, 16, 128], BF16, E_d)
    ovl, ovl_r = const("ovl_t", [NCMP, 32], F32, ovl_d)
    idf, idf_r = const("idf_t", [128, 128], F32, idf_d)
    idb, idb_r = const("idb_t", [128, 128], BF16, idb_d)

    q_t = [C.tile(f"q{pc}", [96, 8, NT], BF16) for pc in range(2)]
    kTs, kTs_r = C.tile("kTs_t", [96, 2, NS], BF16)
    kTw, kTw_r = C.tile("kTw_t", [96, 2, NS], BF16)
    Vs, Vs_r = C.tile("Vs_t", [128, 16, 128], BF16)
    Vw, Vw_r = C.tile("Vw_t", [128, 16, 128], BF16)
    vT_t, vT_r = C.tile("vT_t", [128, NS], BF16)
    pcf_t = [C.tile(f"pcf{i}", [128, 512], F32) for i in range(2)]
    pcb_t = [C.tile(f"pcb{i}", [128, 512], BF16) for i in range(2)]
    pn_t = [C.tile(f"pn{i}", [128, 512], F32) for i in range(2)]
    rec_t = [C.tile(f"rec{i}", [128, 512], F32) for i in range(2)]
    reci = [0]
    oc, oc_r = C.tile("oc", [128, 8, 128], F32)
    os_, os_r = C.tile("os", [128, 8, 128], F32)
    ow, ow_r = C.tile("ow", [128, 8, 128], F32)
    ob_t = [C.tile(f"ob{i}", [128, 8, 128], BF16) for i in range(2)]
    impT, impT_r = C.tile("impT", [32, 128], F32)
    score, score_r = C.tile("score", [128, 32], F32)
    sc2, sc2_r = C.tile("sc2", [128, 32], F32)
    m8a, m8a_r = C.tile("m8a", [128, 8], F32)
    m8b, m8b_r = C.tile("m8b", [128, 8], F32)
    selb, selb_r = C.tile("selb", [128, 32], BF16)
    selT, selT_r = C.tile("selT", [32, 128], BF16)
    msk_t = [C.tile(f"msk{i}", [128, 128], BF16) for i in range(2)]
    mski = [0]
    pt_t = [C.tile(f"pt{i}", [128, 512], BF16) for i in range(4)]
    pti = [0]
    gd_t = [C.tile(f"gd{i}", [96, 3, 128], F32) for i in range(4)]
    t1_t = [C.tile(f"t1_{i}", [128, 128], F32) for i in range(2)]
    t2_t = [C.tile(f"t2_{i}", [128, 128], F32) for i in range(2)]
    acc = [((C.ps[4], C.psr[4]), (C.ps[5], C.psr[5])), ((C.ps[6], C.psr[6]), (C.ps[7], C.psr[7]))]
    cnt = [0]

    def do_block(g, j):
        js = slice(j * 128, (j + 1) * 128)
        pi, pir = C.ps[4], C.psr[4]
        PC = slice(0, NCMP)
        mcb = bass.AP(maskc, j * 128, [[NT, NCMP], [0, 4], [1, 128]])
        st = {}
        for hx in range(2):
            rec, recr = rec_t[reci[0] % 2]
            reci[0] += 1
            st[hx] = dict(pcf=pcf_t[hx], pcb=pcb_t[hx], pn=pn_t[hx], rec=(rec, recr))

        def c1(hx):
            ps, psr = C.psum()
            st[hx]["ps"] = (ps, psr)
            S.op("pe", lambda e: [
                e.matmul(ps[PC, :], kcT[:, g, pc, :], q_t[pc][0][:, hx * 4:(hx + 1) * 4, js], start=(pc == 0), stop=(pc == 1))
                for pc in range(2)], reads=[kcT_r, q_t[0][1], q_t[1][1]], writes=[psr])

        def c2(hx):
            ps, psr = st[hx]["ps"]
            pcf, pcfr = st[hx]["pcf"]
            S.op("act", lambda e: e.activation(out=pcf[PC, :], in_=ps[PC, :], func=AF.Exp, scale=scale), reads=[psr], writes=[pcfr])
            S.op("dve", lambda e: e.tensor_tensor(out=pcf[PC, :].rearrange("p (h t) -> p h t", h=4),
                                                  in0=pcf[PC, :].rearrange("p (h t) -> p h t", h=4), in1=mcb, op=ALU.mult),
                 reads=[pcfr, maskc_r], writes=[pcfr])

        def c3(hx):
            pcf, pcfr = st[hx]["pcf"]
            pcb, pcbr = st[hx]["pcb"]
            pd, pdr = C.psum()
            st[hx]["pd"] = (pd, pdr)
            S.op("pe", lambda e: e.matmul(pd[:, :], C.ones_f[PC, :], pcf[PC, :], start=True, stop=True), reads=[pcfr, C.r_const], writes=[pdr])
            S.op("pool", lambda e: e.tensor_copy(out=pcb[PC, :], in_=pcf[PC, :]), reads=[pcfr], writes=[pcbr])
            po, por = C.psum()
            st[hx]["po"] = (po, por)
            S.op("pe", lambda e: e.matmul(po[:, :], vc[:, g, :], pcb[PC, :], start=True, stop=True), reads=[pcbr, vc_r], writes=[por])

        def c4(hx):
            pd, pdr = st[hx]["pd"]
            po, por = st[hx]["po"]
            rec, recr = st[hx]["rec"]
            pcf, pcfr = st[hx]["pcf"]
            pn, pnr = st[hx]["pn"]
            S.op("dve", lambda e: [e.tensor_scalar(out=rec[:], in0=pd[:], scalar1=1e-30, scalar2=None, op0=ALU.max),
                                   e.reciprocal(out=rec[:], in_=rec[:])], reads=[pdr], writes=[recr])
            S.op("dve", lambda e: e.tensor_tensor(out=pn[PC, :], in0=pcf[PC, :], in1=rec[PC, :], op=ALU.mult), reads=[pcfr, recr], writes=[pnr])
            S.op("dve", lambda e: e.tensor_tensor(out=oc[:, hx * 4:(hx + 1) * 4, :].rearrange("p h t -> p (h t)"), in0=po[:], in1=rec[:],
                                                  op=ALU.mult), reads=[por, recr], writes=[oc_r])

        def c5(hx):
            pn, pnr = st[hx]["pn"]
            S.op("pe", lambda e: [e.matmul(pi[0:32, 0:128], ovl[:, :], pn[PC, hh * 128:(hh + 1) * 128],
                                           start=(hx == 0 and hh == 0), stop=(hx == 1 and hh == 3)) for hh in range(4)],
                 reads=[pnr, ovl_r], writes=[pir])
        for stage in (c1, c2, c3, c4, c5):
            for hx in range(2):
                stage(hx)
        S.op("act", lambda e: e.activation(out=impT[:], in_=pi[0:32, 0:128], func=AF.Copy), reads=[pir], writes=[impT_r])
        ps, psr = C.psum()
        S.op("pe", lambda e, ps=ps: e.matmul(ps[:, 0:32], impT[:, :], idf[0:32, 0:32], start=True, stop=True),
             reads=[impT_r, idf_r], writes=[psr])
        S.op("dve", lambda e, ps=ps, j=j: e.tensor_tensor(out=score[:], in0=ps[:, 0:32], in1=bonus[:, j, :], op=ALU.add),
             reads=[psr, bonus_r], writes=[score_r])
        S.op("dve", lambda e: e.max(out=m8a[:], in_=score[:]), reads=[score_r], writes=[m8a_r], sync_same=[score_r])
        S.op("dve", lambda e: e.match_replace(out=sc2[:], in_to_replace=m8a[:], in_values=score[:], imm_value=-3.0e38),
             reads=[m8a_r, score_r], writes=[sc2_r], sync_same=[m8a_r])
        S.op("dve", lambda e: e.max(out=m8b[:], in_=sc2[:]), reads=[sc2_r], writes=[m8b_r], sync_same=[sc2_r])
        S.op("dve", lambda e: e.tensor_scalar(out=selb[:], in0=score[:], scalar1=m8b[:, 7:8], scalar2=None, op0=ALU.is_ge),
             reads=[score_r, m8b_r], writes=[selb_r], sync_same=[m8b_r])
        ps, psr = C.psum()
        S.op("pe", lambda e, ps=ps: e.matmul(ps[0:32, 0:128], selb[:, :], idb[:, :], start=True, stop=True),
             reads=[selb_r, idb_r], writes=[psr])
        S.op("act", lambda e, ps=ps: e.activation(out=selT[:], in_=ps[0:32, 0:128], func=AF.Copy), reads=[psr], writes=[selT_r])
        kts = list(range(2 * j + 2))
        attn_sel(C, S, j, kts, kTs, kTs_r, Vs, Vs_r, os_, os_r, q_t, pt_t, pti, acc, rec_t, reci, scale,
                 lambda kt: sel_mask(C, S, kt, j, Ee, Ee_r, selT, selT_r, mAB, mAB_r, msk_t, mski))
        ktw = [(2 * j - 4 + r, r) for r in range(6) if 2 * j - 4 + r >= 0]
        attn_sel(C, S, j, [k for (k, r) in ktw], kTw, kTw_r, Vw, Vw_r, ow, ow_r, q_t, pt_t, pti, acc, rec_t, reci, scale,
                 lambda kt, j=j: (mW[:, kt - (2 * j - 4), :], mW_r))
        ob, obr = ob_t[cnt[0] % 2]
        cnt[0] += 1
        pgs = {}

        def g1(hh):
            h = 8 * g + hh
            gd, gdr = gd_t[hh % len(gd_t)]
            S.op("pool", lambda e: [e.tensor_scalar(out=gd[:, br, :], in0=gates[:, js], scalar1=idf[0:96, h * 3 + br:h * 3 + br + 1],
                                                    scalar2=None, op0=ALU.mult) for br in range(3)],
                 reads=[gates_r, idf_r], writes=[gdr])
            pg, pgr = C.psum()
            S.op("pe", lambda e: [e.matmul(pg[:, br * 128:(br + 1) * 128], C.ones_f[0:96, :], gd[:, br, :], start=True, stop=True)
                                  for br in range(3)], reads=[gdr, C.r_const], writes=[pgr])
            pgs[hh] = (pg, pgr)

        def g2(hh):
            pg, pgr = pgs.pop(hh)
            t1, t1r = t1_t[hh % 2]
            t2, t2r = t2_t[hh % 2]

            def comb(e):
                return [e.tensor_tensor(out=t1[:], in0=pg[:, 0:128], in1=oc[:, hh, :], op=ALU.mult),
                        e.tensor_tensor(out=t2[:], in0=pg[:, 128:256], in1=os_[:, hh, :], op=ALU.mult),
                        e.tensor_tensor(out=t1[:], in0=t1[:], in1=t2[:], op=ALU.add),
                        e.tensor_tensor(out=t2[:], in0=pg[:, 256:384], in1=ow[:, hh, :], op=ALU.mult),
                        e.tensor_tensor(out=ob[:, hh, :], in0=t1[:], in1=t2[:], op=ALU.add)]
            S.op("dve", comb, reads=[pgr, oc_r, os_r, ow_r], writes=[t1r, t2r, obr])
        for n in range(8 + 2):
            if n < 8:
                g1(n)
            if n - 2 >= 0:
                g2(n - 2)
        S.op("sync", lambda e, ob=ob, g=g, js=js: e.dma_start(out=o_attn[8 * g:8 * g + 8, :, js].rearrange("h p t -> p h t"), in_=ob[:]),
             reads=[obr], writes=[r_oattn], dma=True, dsem=obr)

    for g in range(4):
        for pc in range(2):
            qt, qr = q_t[pc]
            S.op("sync", lambda e, qt=qt, pc=pc, g=g: e.dma_start(out=qt[:], in_=qT_d[8 * g:8 * g + 8, pc].rearrange("h d t -> d h t")),
                 writes=[qr], dma=True, dsem=qr)
        if "ex2" in C.io:
            exk, exv, r_ex2 = C.io["ex2"]
            for (kt_t, kt_r, bg0) in ((kTs, kTs_r, 4), (kTw, kTw_r, 8)):
                S.op("sync", lambda e, kt_t=kt_t, g=g, bg0=bg0: [e.dma_start(
                    out=kt_t[:, pc, :].rearrange("p (j r t) -> p j r t", r=2, t=128)[:, :, r, :],
                    in_=exk(r, (bg0 + g) * 2 + pc).rearrange("p (j t) -> p j t", t=128))
                    for r in range(2) for pc in range(2)], reads=[r_ex2], writes=[kt_r], dma=True, dsem=kt_r, ndma=4)
            for (V_t, V_r, bg0) in ((Vs, Vs_r, 4), (Vw, Vw_r, 8)):
                S.op("sync", lambda e, g=g, bg0=bg0: [e.dma_start(
                    out=vT_t[:, :].rearrange("p (j r t) -> p j r t", r=2, t=128)[:, :, r, :],
                    in_=exv(r, bg0 + g).rearrange("p (j t) -> p j t", t=128))
                    for r in range(2)], reads=[r_ex2], writes=[vT_r], dma=True, dsem=vT_r, ndma=2)
                for k4 in range(4):
                    ps, psr = C.psum()
                    S.op("pe", lambda e, ps=ps, k4=k4: [e.matmul(ps[:, i * 128:(i + 1) * 128], vT_t[:, (k4 * 4 + i) * 128:(k4 * 4 + i + 1) * 128],
                                                                 idb[:, :], start=True, stop=True) for i in range(4)],
                         reads=[vT_r, idb_r], writes=[psr])
                    S.op("act", lambda e, ps=ps, k4=k4, V_t=V_t: e.activation(out=V_t[:, k4 * 4:(k4 + 1) * 4, :].rearrange("p k d -> p (k d)"),
                                                                             in_=ps[:], func=AF.Copy), reads=[psr], writes=[V_r])
        else:
            S.op("sync", lambda e, g=g: e.dma_start(out=kTs[:], in_=kTs_d[g].rearrange("pc d t -> d pc t")), writes=[kTs_r], dma=True, dsem=kTs_r)
            S.op("sync", lambda e, g=g: e.dma_start(out=kTw[:], in_=kTw_d[g].rearrange("pc d t -> d pc t")), writes=[kTw_r], dma=True, dsem=kTw_r)
            S.op("sync", lambda e, g=g: e.dma_start(out=Vs[:], in_=Vs_d[:, :, g, :]), writes=[Vs_r], dma=True, dsem=Vs_r)
            S.op("sync", lambda e, g=g: e.dma_start(out=Vw[:], in_=Vw_d[:, :, g, :]), writes=[Vw_r], dma=True, dsem=Vw_r)
        for j in range(8):
            do_block(g, j)
    if standalone:
        S.op("sync", None, reads=[r_oattn])
        S.emit()
        return nc
    C.end_phase()


def sel_mask(C, S, kt, j, Ee, Ee_r, selT, selT_r, mAB, mAB_r, msk_t, mski):
    ps, psr = C.psum()
    S.op("pe", lambda e: e.matmul(ps[:, 0:128], Ee[:, kt, :], selT[:, :], start=True, stop=True), reads=[Ee_r, selT_r], writes=[psr])
    m, mr = msk_t[mski[0] % 2]
    mski[0] += 1
    if kt >= 2 * j:
        mc = (kt % 2) * 128
        S.op("dve", lambda e: e.tensor_tensor(out=m[:], in0=ps[:, 0:128], in1=mAB[:, mc:mc + 128], op=ALU.mult), reads=[psr, mAB_r], writes=[mr])
    else:
        S.op("dve", lambda e: e.tensor_copy(out=m[:], in_=ps[:, 0:128]), reads=[psr], writes=[mr])
    return m[:], mr


def attn_sel(C, S, j, kts, kT, kT_r, V, V_r, out_t, out_r, q_t, pt_t, pti, acc, rec_t, reci, scale, mask_fn, L=2):
    js = slice(j * 128, (j + 1) * 128)
    last = len(kts) - 1
    items = [(i, kt, hx) for i, kt in enumerate(kts) for hx in range(2)]
    masks = {}
    state = {}

    def stageA(n):
        i, kt, hx = items[n]
        if kt not in masks:
            m_ap, m_r = mask_fn(kt)
            masks[kt] = (bass.AP(m_ap.tensor, m_ap.offset, [list(m_ap.ap[0]), [0, 4], [1, 128]]), m_r)
        mb, m_r = masks[kt]
        ks = slice(kt * 128, (kt + 1) * 128)
        ps, psr = C.psum()
        S.op("pe", lambda e: [
            e.matmul(ps[:, :], kT[:, pc, ks], q_t[pc][0][:, hx * 4:(hx + 1) * 4, js], start=(pc == 0), stop=(pc == 1))
            for pc in range(2)], reads=[kT_r, q_t[0][1], q_t[1][1]], writes=[psr])
        pt, ptr = pt_t[pti[0] % len(pt_t)]
        pti[0] += 1
        S.op("act", lambda e: e.activation(out=pt[:], in_=ps[:], func=AF.Exp, scale=scale), reads=[psr], writes=[ptr])
        S.op("dve", lambda e: e.tensor_tensor(out=pt[:].rearrange("p (h t) -> p h t", h=4),
                                              in0=pt[:].rearrange("p (h t) -> p h t", h=4), in1=mb, op=ALU.mult),
             reads=[ptr, m_r], writes=[ptr])
        state[n] = (pt, ptr)

    def stageB(n):
        i, kt, hx = items[n]
        pt, ptr = state.pop(n)
        (po, por), (pd, pdr) = acc[hx]
        S.op("pe", lambda e: [
            e.matmul(po[:, :], V[:, kt, :], pt[:, :], start=(i == 0), stop=(i == last)),
            e.matmul(pd[:, :], C.ones_b[:], pt[:, :], start=(i == 0), stop=(i == last))],
            reads=[ptr, V_r, C.r_const], writes=[por, pdr])

    for n in range(len(items) + L):
        if n < len(items):
            stageA(n)
        if n - L >= 0:
            stageB(n - L)
    for hx in range(2):
        (po, por), (pd, pdr) = acc[hx]
        rec, recr = rec_t[reci[0] % 2]
        reci[0] += 1
        S.op("dve", lambda e, rec=rec, pd=pd: e.reciprocal(out=rec[:], in_=pd[:]), reads=[pdr], writes=[recr])
        S.op("dve", lambda e, rec=rec, po=po, hx=hx: e.tensor_tensor(
            out=out_t[:, hx * 4:(hx + 1) * 4, :].rearrange("p h t -> p (h t)"), in0=po[:], in1=rec[:], op=ALU.mult),
            reads=[por, recr], writes=[out_r])


def build_oproj(NT=1024, C=None, io=None):
    global WT_ELEMS
    WT_ELEMS = 16384
    standalone = C is None
    if standalone:
        nc = bass.Bass("TRN2", target_bir_lowering=False)
        C = Ctx(nc)
        C.begin_phase("", 8, {})
    else:
        nc = C.nc
        C.begin_phase("opj_", 8, io or {})
    S = C.S
    hT, _ = C.dram_in("hT", [32, 128, NT], F32)
    oT_d, _ = C.dram_in("oT", [32, 128, NT], BF16)
    Wo, _ = C.dram_in("Wo", [4096, 4096], F32)
    o_h, r_oh = C.dram_out("o_h", [32, 128, NT], F32)
    oT, oT_r = C.tile("oT_t", [128, 32, NT], BF16)
    S.op("sync", lambda e: [e.dma_start(out=oT[:, i * 8:(i + 1) * 8, :], in_=oT_d[i * 8:(i + 1) * 8].rearrange("c p t -> p c t")) for i in range(4)],
         writes=[oT_r], dma=True, dsem=oT_r, ndma=4)
    wtiles = [C.tile(f"w{i}", [128, WT_ELEMS], BF16) for i in range(2)]
    xs = [C.tile(f"xs{i}", [128, 512], F32) for i in range(4)]
    xsi = [0]

    def epio(ci, hf, ps, psr):
        t, r = xs[xsi[0] % 4]
        xsi[0] += 1
        hs = slice(hf * 512, (hf + 1) * 512)
        S.op("sync", lambda e: e.dma_start(out=t[:], in_=hT[ci, :, hs]), writes=[r], dma=True, dsem=r)
        S.op("dve", lambda e: e.tensor_tensor(out=t[:], in0=ps[:], in1=t[:], op=ALU.add), reads=[psr, r], writes=[r])
        S.op("sync", lambda e: e.dma_start(out=o_h[ci, :, hs], in_=t[:]), reads=[r], writes=[r_oh], dma=True, dsem=r)
    linear_T(C, Wo, 32, lambda kc, hf: (oT[:, kc, hf * 512:(hf + 1) * 512], oT_r), [(i * 128, 128) for i in range(32)],
             [0, 1], epio, wtiles)
    if standalone:
        S.op("sync", None, reads=[r_oh])
        S.emit()
        return nc
    C.end_phase()


PAIRS = [[0, 1], [2, 3], [4, 5], [6, 7]]


def build_fused(NT=1024, NS=2048, upto=None, ncores=8):
    nc = bass.Bass("TRN2", target_bir_lowering=False, num_devices=ncores)
    pairs = PAIRS[:ncores // 2]
    C = Ctx(nc)
    C.fused = True
    S = C.S
    xT = C.dram_in("xT", [32, 128, NT], F32)
    pos = C.dram_in("pos", [1, NT], I32)
    qn_s = C.dram_tmp("s_qn", [32, 128, NT], BF16)
    qr_s = C.dram_tmp("s_qr", [16, 128, NT], BF16)
    ex1s, r_ex1s = C.dram_tmp("ex1_src", [640, NT], BF16)
    ex1d = C.dram_tmp("ex1_dst", [1280, NT], BF16)
    def T(name, key, shape=None, dt=F32):
        shape = shape or [32, 128, NT]
        if upto == key:
            return C.dram_out("o_fin", shape, dt)
        return C.dram_tmp(name, shape, dt)
    hA = T("s_hA", "p2a")
    h1 = T("s_h1", "f0")
    hB = T("s_hB", "opj")
    h2 = C.dram_tmp("s_h2", [32, 128, NT], F32)
    dbg = C.dram_tmp("s_dbg", [32, 128, NT], BF16)
    EX2 = [960, 960, 896, 1024]
    ex2s = [C.dram_tmp(f"ex2_src{i}", [n, NT], BF16)[0] for i, n in enumerate(EX2)]
    ex2d = [C.dram_tmp(f"ex2_dst{i}", [2 * n, NT], BF16)[0] for i, n in enumerate(EX2)]
    r_ex2s = S.res("ex2_src", acc=True)
    r_ex2d = S.res("ex2_dst", acc=True)

    def kloc(kp):
        return (0, kp * 96) if kp < 10 else ((1, (kp - 10) * 96) if kp < 20 else (2, (kp - 20) * 96))

    def vloc(bg):
        return (2, 384 + bg * 128) if bg < 4 else (3, (bg - 4) * 128)

    def okf(bg, pc):
        t, o = kloc(bg * 2 + pc)
        return ex2s[t][o:o + 96, :]

    def ovf(bg):
        t, o = vloc(bg)
        return ex2s[t][o:o + 128, :]

    def exk(r, kp):
        t, o = kloc(kp)
        return ex2d[t][r * EX2[t] + o:r * EX2[t] + o + 96, :]

    def exv(r, bg):
        t, o = vloc(bg)
        return ex2d[t][r * EX2[t] + o:r * EX2[t] + o + 128, :]
    q_s = T("s_q", "p3a", [32, 2, 96, NT], BF16)
    g_s = C.dram_tmp("s_g", [96, NT], F32)
    kc_s = T("s_kc", "p3b1", [4, 2, 96, 127], BF16)
    vc_s = C.dram_tmp("s_vc", [127, 4, 128], BF16)
    at_s = T("s_attn", "p3b2", [32, 128, NT], BF16)
    o_fin = C.dram_out("o_fin", [32, 128, NT], F32) if upto is None else None

    def fin(key, r):
        if upto == key:
            S.op("sync", None, reads=[r[1]])
            S.emit()
            return True
        return False

    build_p1(NT, C=C, io={"xT": xT, "pos": pos, "o_qn": qn_s, "o_qr": qr_s,
                          "o_ckv": (ex1s[0:512, :].rearrange("(c p) t -> c p t", p=128), r_ex1s),
                          "o_kr": (ex1s[512:640, :], r_ex1s)})
    S.op("pool", lambda e: e.collective_compute("AllGather", ALU.bypass, replica_groups=pairs, ins=[ex1s], outs=[ex1d[0]]),
         reads=[r_ex1s], writes=[ex1d[1]], dma=True, dsem=ex1d[1], dma_inc=1)
    build_p2a(NT, NS, C=C, io={"xT": xT, "ex1": ex1d, "qn": qn_s, "qr": qr_s, "o_hA": hA, "o_attn": dbg})
    if fin("p2a", hA):
        return nc
    build_ffn(False, NT, C=C, io={"hT": hA, "o_h": h1}, tag="f0")
    if fin("f0", h1):
        return nc
    build_p3a(NT, C=C, io={"hT": h1, "pos": pos,
                           "o_kv_fn": (okf, ovf, r_ex2s),
                           "o_q": q_s, "o_g": g_s})
    if fin("p3a", q_s):
        return nc
    S.op("pool", lambda e: [e.collective_compute("AllGather", ALU.bypass, replica_groups=pairs, ins=[ex2s[i]], outs=[ex2d[i]])
                            for i in range(4)], reads=[r_ex2s], writes=[r_ex2d], dma=True, dsem=r_ex2d, dma_inc=1, ndma=4)
    build_p3b1(NS, C=C, io={"ex2": (exk, exv, r_ex2d), "o_kcT": kc_s, "o_vc": vc_s})
    if fin("p3b1", kc_s):
        return nc
    build_p3b2(NT, NS, C=C, io={"ex2": (exk, exv, r_ex2d), "kcT": kc_s, "vc": vc_s, "qT": q_s, "gates": g_s, "o_attn": at_s})
    if fin("p3b2", at_s):
        return nc
    build_oproj(NT, C=C, io={"hT": h1, "oT": at_s, "o_h": hB})
    if fin("opj", hB):
        return nc
    build_ffn(True, NT, C=C, io={"hT": hB, "o_h": h2, "o_fin": o_fin}, tag="f1")
    S.op("sync", None, reads=[o_fin[1]])
    S.emit()
    return nc


THETA=10000.0
def tok_idx(hf):
    return np.concatenate([np.arange((2*j+hf)*128,(2*j+hf)*128+128) for j in range(8)])
def colT(g):
    return np.ascontiguousarray(g.reshape(-1,128).T)
def featmajor(x):
    T,D=x.shape
    return np.ascontiguousarray(x.T.reshape(D//128,128,T))
def p1_weights(I):
    w=I["a_w_in"][0]
    x1=w[:,2048:2080]; x2=w[:,2080:2112]
    W1=np.concatenate([w[:,:2048], x1,x2,x1,x2, x2,x1,x2,x1],axis=1)
    uq=I["a_w_uq"][0]
    cols=[]
    for h in range(32): cols.append(uq[:,h*192:h*192+128])
    for Pp in range(16):
        h0=2*Pp; h1=2*Pp+1
        a1=uq[:,h0*192+128:h0*192+160]; a2=uq[:,h0*192+160:h0*192+192]
        b1=uq[:,h1*192+128:h1*192+160]; b2=uq[:,h1*192+160:h1*192+192]
        cols += [a1,a2,b1,b2, a2,a1,b2,b1]
    Wq=np.concatenate(cols,axis=1)
    inv=(THETA ** (-np.arange(0,64,2,dtype=np.float32)/np.float32(64))).astype(np.float32)
    inv32=np.tile(inv,4).reshape(128,1).astype(np.float32)
    sgn=np.tile(np.concatenate([-np.ones(32),np.ones(32)]),2).reshape(128,1).astype(np.float32)
    return dict(W1=np.ascontiguousarray(W1),Wq=np.ascontiguousarray(Wq),gA=colT(I["a_norm"][0]),gQ=colT(I["a_q_norm"][0]),gKV=colT(I["a_kv_norm"][0]),inv32=inv32,sgn64=sgn)
def p1_core_inputs(I,b,hf,shared):
    idx=tok_idx(hf)
    d=dict(shared)
    d["xT"]=featmajor(I["x"][b][idx])
    d["pos"]=np.ascontiguousarray(I["positions"][b][idx].reshape(1,-1))
    return d
import ml_dtypes
BF=ml_dtypes.bfloat16
def seq_order(a0,a1,axis=-1):
    a0=np.moveaxis(a0,axis,-1); a1=np.moveaxis(a1,axis,-1)
    sh=a0.shape[:-1]
    b0=a0.reshape(sh+(8,128)); b1=a1.reshape(sh+(8,128))
    full=np.stack([b0,b1],axis=-2).reshape(sh+(2048,))
    return np.ascontiguousarray(np.moveaxis(full,-1,axis))
def p2a_weights(I):
    kv=I["a_w_ukv"][0].reshape(512,32,256)
    Wk=np.ascontiguousarray(kv[:,:,:128].reshape(512,4096)); Wv=np.ascontiguousarray(kv[:,:,128:].reshape(512,4096))
    return dict(Wk=Wk,Wv=Wv,Wo=np.ascontiguousarray(I["a_w_o"][0]))
def maskAB(hf):
    tri=(np.arange(128)[:,None]<=np.arange(128)[None,:]).astype(np.float32)
    if hf==0: m=np.concatenate([tri,np.zeros((128,128),np.float32)],axis=1)
    else: m=np.concatenate([np.ones((128,128),np.float32),tri],axis=1)
    return m.astype(BF)

def ffn_weights(I,layer,final=False):
    w=I["f_w_in"][layer]; DFF=11008
    a=w[:,:DFF].reshape(4096,86,128); b=w[:,DFF:].reshape(4096,86,128)
    Win=np.ascontiguousarray(np.stack([a,b],axis=2).reshape(4096,2*DFF))
    d=dict(Win=Win,Wout=np.ascontiguousarray(I["f_w_out"][layer]),gF=colT(I["f_norm"][layer]))
    if final: d["gN"]=colT(I["final_norm"])
    return d
def p3a_weights(I):
    inv=(THETA ** (-np.arange(0,192,2,dtype=np.float32)/np.float32(192))).astype(np.float32)
    inv96=np.zeros((128,1),np.float32); inv96[:96,0]=inv
    return dict(Ws=np.ascontiguousarray(I["s_w_kv"]),Wb=np.ascontiguousarray(I["b_w_in"][0]),gS=colT(I["s_norm"]),gB=colT(I["b_norm"][0]),inv96=inv96)
def p3b1_weights(I):
    pek=I["s_cmp_pe_k"]
    pekT=np.ascontiguousarray(pek.reshape(32,2,96).transpose(2,0,1).reshape(96,64))
    pevT=np.ascontiguousarray(I["s_cmp_pe_v"].T)
    return dict(w1k=np.ascontiguousarray(I["s_cmp_w1_k"]),w2k=np.ascontiguousarray(I["s_cmp_w2_k"]),pekT=pekT,
                w1v=np.ascontiguousarray(I["s_cmp_w1_v"]),w2v=np.ascontiguousarray(I["s_cmp_w2_v"]),pevT=pevT)
def p3b2_consts(hf):
    t=tok_idx(hf)
    c=np.arange(127)
    mask_c=((c[:,None]*16+31)<=t[None,:]).astype(np.float32)
    bonus=np.zeros((128,8,32),np.float32)
    jj=np.arange(32)
    for j in range(8):
        tt=(2*j+hf)*128+np.arange(128)
        cur=(tt//64)[:,None]
        valid=(jj[None,:]*64)<=tt[:,None]
        forced=valid&((jj[None,:]==0)|(jj[None,:]==cur)|(jj[None,:]==cur-1))
        bonus[:,j,:]=np.where(valid,np.where(forced,1e4,0.0),-1e30)
    kl=np.arange(128)[:,None]; ql=np.arange(128)[None,:]
    su=(kl>ql).astype(np.float32); tri=(kl<=ql).astype(np.float32); one=np.ones((128,128),np.float32); zero=np.zeros((128,128),np.float32)
    mw=[su,one,one,one,tri,zero] if hf==0 else [zero,su,one,one,one,tri]
    maskW=np.stack(mw,axis=1).astype(BF)
    E=np.zeros((32,16,128),np.float32)
    for kt in range(16):
        for key in range(128):
            E[2*kt+key//64,kt,key]=1
    c_start=np.arange(127)*16; j_start=np.arange(32)*64
    ovl=((c_start[:,None]<j_start[None,:]+64)&(c_start[:,None]+32>j_start[None,:])).astype(np.float32)
    return dict(mask_c=mask_c,bonus=bonus,maskAB=maskAB(hf),maskW=maskW,Eexp=E.astype(BF),ovl=ovl,
                ident_f=np.eye(128,dtype=np.float32),ident_b=np.eye(128,dtype=np.float32).astype(BF))


def kernel(**I):
    I = {k: np.asarray(v) for k, v in I.items()}
    NCORE = 8
    cores = [(c // 2, c % 2) for c in range(NCORE)]
    shared = {}
    for pref, d in (("p1_", p1_weights(I)), ("p2a_", p2a_weights(I)), ("f0_", ffn_weights(I, 0, False)),
                    ("p3a_", p3a_weights(I)), ("p3b1_", p3b1_weights(I)), ("opj_", {"Wo": np.ascontiguousarray(I["b_w_o"][0])}),
                    ("f1_", ffn_weights(I, 1, True))):
        for k, v in d.items():
            shared[pref + k] = v
    maps = []
    for (b, hf) in cores:
        d = dict(shared)
        idx = tok_idx(hf)
        d["xT"] = featmajor(I["x"][b][idx])
        d["pos"] = np.ascontiguousarray(I["positions"][b][idx].reshape(1, -1))
        d["p2a_maskAB"] = maskAB(hf)
        for k, v in p3b2_consts(hf).items():
            d["p3b2_" + k] = v
        maps.append(d)
    nc = build_fused()
    res = run_bass_kernel_spmd(nc, maps, core_ids=list(range(NCORE))).results
    out = np.zeros((4, 2048, 4096), np.float32)
    for c, (b, hf) in enumerate(cores):
        out[b, tok_idx(hf)] = np.asarray(res[c]["o_fin"]).transpose(2, 0, 1).reshape(1024, 4096)
    return out
```
